# Optimizing a Trainium2 kernel written in Bass

```python
import jax, jax.numpy as jnp
from jax import lax
import numpy as np

D_MODEL = 2048
BATCH = 8
SEQ = 2048
DEPTH = 1

HEAD_DIM = 64
MIX_WIDTH = D_MODEL
N_HEADS_A = MIX_WIDTH // HEAD_DIM // 2
GQA_RATIO = 8
N_KV_A = N_HEADS_A // GQA_RATIO
GROUP_A = N_HEADS_A // N_KV_A
N_HEADS_B = MIX_WIDTH // HEAD_DIM - N_HEADS_A
WIDTH_A = N_HEADS_A * HEAD_DIM
WIDTH_KV_A = N_KV_A * HEAD_DIM
WIDTH_B = N_HEADS_B * HEAD_DIM
IN_SPLITS = (WIDTH_A, WIDTH_A + WIDTH_KV_A, WIDTH_A + 2 * WIDTH_KV_A, WIDTH_A + 2 * WIDTH_KV_A + WIDTH_B, WIDTH_A + 2 * WIDTH_KV_A + 2 * WIDTH_B)
IN_WIDTH = WIDTH_A + 2 * WIDTH_KV_A + 3 * WIDTH_B
WINDOW_A = 128
DILATED = ((128, 1), (512, 4), (2048, 16))
BLOCK = 128
ROPE_THETA = 500000.0
ROT_DIM = HEAD_DIM // 4
D_FF = 256 * ((8 * D_MODEL // 3 + 255) // 256)
FFN_RES = 0.5
N_SUB = 3
N_MOD = 3
EPS = 1e-5
MAX_START = 4096

kernel_name = 'hybrid_swa_sink_dilated_macaron'


def rmsnorm(x, g):
    xf = x.astype(jnp.float32)
    y = xf * lax.rsqrt(jnp.mean(xf * xf, axis=-1, keepdims=True) + EPS)
    return (y * g.astype(jnp.float32)).astype(x.dtype)


def modulate(h, shift, scale):
    return h * (1 + scale) + shift


def swiglu(h, w_gu, w_down):
    gate, up = jnp.split(h @ w_gu, 2, axis=-1)
    return (jax.nn.silu(gate) * up) @ w_down


def rope_tables(positions):
    inv_freq = ROPE_THETA ** (-(jnp.arange(0, ROT_DIM, 2, dtype=jnp.float32) / ROT_DIM))
    ang = positions.astype(jnp.float32)[..., None] * inv_freq
    return jnp.cos(ang)[:, :, None, :], jnp.sin(ang)[:, :, None, :]


def partial_rope(x, cos, sin):
    xr, xp = x[..., :ROT_DIM], x[..., ROT_DIM:]
    x1, x2 = xr[..., :ROT_DIM // 2], xr[..., ROT_DIM // 2:]
    rot = jnp.concatenate([x1 * cos - x2 * sin, x2 * cos + x1 * sin], axis=-1)
    return jnp.concatenate([rot.astype(x.dtype), xp], axis=-1)


def banded_attention(q, k, v, max_dist, sink=None):
    b, L, hk, g, hd = q.shape
    n_prev = -(-max_dist // BLOCK)
    nb = -(-L // BLOCK)
    pad = nb * BLOCK - L
    front = n_prev * BLOCK
    kw_len = (n_prev + 1) * BLOCK
    qb = jnp.pad(q, ((0, 0), (0, pad), (0, 0), (0, 0), (0, 0))).reshape(b, nb, BLOCK, hk, g, hd)
    kp = jnp.pad(k, ((0, 0), (front, pad), (0, 0), (0, 0))).reshape(b, nb + n_prev, BLOCK, hk, hd)
    vp = jnp.pad(v, ((0, 0), (front, pad), (0, 0), (0, 0))).reshape(b, nb + n_prev, BLOCK, hk, hd)
    kw = jnp.concatenate([kp[:, j:j + nb] for j in range(n_prev + 1)], axis=2)
    vw = jnp.concatenate([vp[:, j:j + nb] for j in range(n_prev + 1)], axis=2)
    qi = jnp.arange(nb)[:, None, None] * BLOCK + jnp.arange(BLOCK)[None, :, None]
    ki = jnp.arange(nb)[:, None, None] * BLOCK + jnp.arange(kw_len)[None, None, :] - front
    dist = qi - ki
    mask = ((dist >= 0) & (dist <= max_dist) & (ki >= 0))[None, :, None, None]
    s = jnp.einsum('bnqhgd,bnkhd->bnhgqk', qb, kw, preferred_element_type=jnp.float32) * (hd ** -0.5)
    s = jnp.where(mask, s, -jnp.inf)
    m = jnp.max(s, axis=-1, keepdims=True)
    if sink is not None:
        sink_l = sink.astype(jnp.float32).reshape(1, 1, hk, g, 1, 1)
        m = jnp.maximum(m, sink_l)
    p = jnp.exp(s - m)
    l = jnp.sum(p, axis=-1, keepdims=True)
    if sink is not None:
        l = l + jnp.exp(sink_l - m)
    o = jnp.einsum('bnhgqk,bnkhd->bnqhgd', p, vw.astype(jnp.float32))
    o = o / jnp.transpose(l, (0, 1, 4, 2, 3, 5))
    lse = jnp.transpose((m + jnp.log(l))[..., 0], (0, 1, 4, 2, 3))
    o = o.reshape(b, nb * BLOCK, hk, g, hd)[:, :L].astype(q.dtype)
    lse = lse.reshape(b, nb * BLOCK, hk, g)[:, :L]
    return o, lse


def to_residues(t, dil):
    b, s = t.shape[:2]
    rest = t.shape[2:]
    return t.reshape((b, s // dil, dil) + rest).swapaxes(1, 2).reshape((b * dil, s // dil) + rest)


def from_residues(t, dil, b):
    m = t.shape[1]
    rest = t.shape[2:]
    return t.reshape((b, dil, m) + rest).swapaxes(1, 2).reshape((b, dil * m) + rest)


def dilated_mixture(q, k, v):
    b, s, h, hd = q.shape
    outs, lses = [], []
    for window, dil in DILATED:
        o, lse = banded_attention(to_residues(q, dil)[:, :, :, None, :], to_residues(k, dil), to_residues(v, dil), window // dil)
        outs.append(from_residues(o[:, :, :, 0, :], dil, b))
        lses.append(from_residues(lse[..., 0], dil, b))
    w = jax.nn.softmax(jnp.stack(lses, axis=0), axis=0)
    out = jnp.sum(w[..., None] * jnp.stack(outs, axis=0).astype(jnp.float32), axis=0)
    return out.reshape(b, s, h * hd).astype(q.dtype)


def hybrid_mixer(h, cos, sin, w_in, b_in, sinks, g_out_a, g_out_b, w_out, b_out):
    b, s, _ = h.shape
    proj = h @ w_in + b_in
    qa, ka, va, qb, kb, vb = jnp.split(proj, IN_SPLITS, axis=-1)
    qa = partial_rope(qa.reshape(b, s, N_HEADS_A, HEAD_DIM), cos, sin).reshape(b, s, N_KV_A, GROUP_A, HEAD_DIM)
    ka = partial_rope(ka.reshape(b, s, N_KV_A, HEAD_DIM), cos, sin)
    va = va.reshape(b, s, N_KV_A, HEAD_DIM)
    out_a, _ = banded_attention(qa, ka, va, WINDOW_A - 1, sinks.reshape(N_KV_A, GROUP_A))
    out_a = out_a.reshape(b, s, WIDTH_A)
    qb = partial_rope(qb.reshape(b, s, N_HEADS_B, HEAD_DIM), cos, sin)
    kb = partial_rope(kb.reshape(b, s, N_HEADS_B, HEAD_DIM), cos, sin)
    vb = vb.reshape(b, s, N_HEADS_B, HEAD_DIM)
    out_b = dilated_mixture(qb, kb, vb)
    y = jnp.concatenate([rmsnorm(out_a, g_out_a), rmsnorm(out_b, g_out_b)], axis=-1)
    return y @ w_out + b_out


def setup_inputs(seed: int = 0) -> dict:
    key = jax.random.key(seed)
    ks = jax.random.split(key, 20)
    nrm = jax.random.normal
    f32 = jnp.float32
    x = nrm(ks[0], (BATCH, SEQ, D_MODEL), f32)
    c = nrm(ks[1], (BATCH, D_MODEL), f32)
    start = jax.random.randint(ks[2], (BATCH, 1), 0, MAX_START, dtype=jnp.int32)
    positions = start + jnp.arange(SEQ, dtype=jnp.int32)[None, :]
    w_ada = nrm(ks[3], (DEPTH, D_MODEL, N_SUB * N_MOD * D_MODEL), f32) * D_MODEL ** -0.5
    b_ada = 0.02 * nrm(ks[4], (DEPTH, N_SUB * N_MOD * D_MODEL), f32)
    g_ffn1 = 1.0 + 0.02 * nrm(ks[5], (DEPTH, D_MODEL), f32)
    w_ffn1_in = nrm(ks[6], (DEPTH, D_MODEL, 2 * D_FF), f32) * D_MODEL ** -0.5
    w_ffn1_out = nrm(ks[7], (DEPTH, D_FF, D_MODEL), f32) * D_FF ** -0.5
    g_mix = 1.0 + 0.02 * nrm(ks[8], (DEPTH, D_MODEL), f32)
    w_in = nrm(ks[9], (DEPTH, D_MODEL, IN_WIDTH), f32) * D_MODEL ** -0.5
    b_in = 0.02 * nrm(ks[10], (DEPTH, IN_WIDTH), f32)
    sinks = nrm(ks[11], (DEPTH, N_HEADS_A), f32)
    g_out_a = 1.0 + 0.02 * nrm(ks[12], (DEPTH, WIDTH_A), f32)
    g_out_b = 1.0 + 0.02 * nrm(ks[13], (DEPTH, WIDTH_B), f32)
    w_out = nrm(ks[14], (DEPTH, MIX_WIDTH, D_MODEL), f32) * MIX_WIDTH ** -0.5
    b_out = 0.02 * nrm(ks[15], (DEPTH, D_MODEL), f32)
    g_ffn2 = 1.0 + 0.02 * nrm(ks[16], (DEPTH, D_MODEL), f32)
    w_ffn2_in = nrm(ks[17], (DEPTH, D_MODEL, 2 * D_FF), f32) * D_MODEL ** -0.5
    w_ffn2_out = nrm(ks[18], (DEPTH, D_FF, D_MODEL), f32) * D_FF ** -0.5
    g_final = 1.0 + 0.02 * nrm(ks[19], (D_MODEL,), f32)
    return {'x': x, 'c': c, 'positions': positions, 'w_ada': w_ada, 'b_ada': b_ada,
            'g_ffn1': g_ffn1, 'w_ffn1_in': w_ffn1_in, 'w_ffn1_out': w_ffn1_out,
            'g_mix': g_mix, 'w_in': w_in, 'b_in': b_in, 'sinks': sinks,
            'g_out_a': g_out_a, 'g_out_b': g_out_b, 'w_out': w_out, 'b_out': b_out,
            'g_ffn2': g_ffn2, 'w_ffn2_in': w_ffn2_in, 'w_ffn2_out': w_ffn2_out, 'g_final': g_final}


def reference(x, c, positions, w_ada, b_ada, g_ffn1, w_ffn1_in, w_ffn1_out, g_mix, w_in, b_in, sinks, g_out_a, g_out_b, w_out, b_out, g_ffn2, w_ffn2_in, w_ffn2_out, g_final):
    b = x.shape[0]
    cos, sin = rope_tables(positions)
    cond = jax.nn.silu(c)
    for layer in range(DEPTH):
        mod = (cond @ w_ada[layer] + b_ada[layer]).reshape(b, N_SUB, N_MOD, D_MODEL)
        shift = mod[:, :, 0, None, :]
        scale = mod[:, :, 1, None, :]
        gate = mod[:, :, 2, None, :]
        h = modulate(rmsnorm(x, g_ffn1[layer]), shift[:, 0], scale[:, 0])
        x = x + FFN_RES * gate[:, 0] * swiglu(h, w_ffn1_in[layer], w_ffn1_out[layer])
        h = modulate(rmsnorm(x, g_mix[layer]), shift[:, 1], scale[:, 1])
        x = x + gate[:, 1] * hybrid_mixer(h, cos, sin, w_in[layer], b_in[layer], sinks[layer], g_out_a[layer], g_out_b[layer], w_out[layer], b_out[layer])
        h = modulate(rmsnorm(x, g_ffn2[layer]), shift[:, 2], scale[:, 2])
        x = x + FFN_RES * gate[:, 2] * swiglu(h, w_ffn2_in[layer], w_ffn2_out[layer])
    return rmsnorm(x, g_final)
```

```python
import contextlib
import math
import numpy as np
import concourse.bass as bass
import concourse.mybir as mybir
from concourse.bass_utils import run_bass_kernel_spmd

F32 = mybir.dt.float32
BF16 = mybir.dt.bfloat16
I32 = mybir.dt.int32
ALU = mybir.AluOpType
AF = mybir.ActivationFunctionType

D = 2048
T = 2048
DFF = 5632
INW = 4352
EPS = 1e-5
NT = 16
PI = math.pi
DT_SIZE = {F32: 4, BF16: 2, I32: 4}

STOP_AFTER = None


class Sched:
    ENG = ('pe', 'act', 'dve', 'pool', 'sp')
    ATTR = {'pe': 'tensor', 'act': 'scalar', 'dve': 'vector', 'pool': 'gpsimd', 'sp': 'sync'}

    def __init__(self, nc, sems):
        self.nc = nc
        self.free = list(sems)
        self.prog = {e: [] for e in self.ENG}
        self.esem = {e: self.free.pop() for e in self.ENG}
        self.ecnt = {e: 0 for e in self.ENG}
        self.dsem = {}
        self.waited = {e: {} for e in self.ENG}
        self.lastw = {}
        self.readers = {}

    def _sem(self, sk):
        return self.esem[sk[1]] if sk[0] == 'e' else self.dsem[sk[1]][0]

    def _need(self, eng, reads, writes):
        need = {}

        def add(ev):
            sk, v = ev
            if sk[0] == 'd':
                v = self.dsem[sk[1]][1]
            if v > need.get(sk, 0):
                need[sk] = v
        for k in reads:
            if k in self.lastw:
                add(self.lastw[k])
        for k in writes:
            if k in self.lastw:
                add(self.lastw[k])
            for sk, v in self.readers.get(k, {}).items():
                add((sk, v))
        out = []
        for sk, v in need.items():
            if sk == ('e', 'pe') and eng == 'pe':
                continue
            if self.waited[eng].get(sk, 0) >= v:
                continue
            self.waited[eng][sk] = v
            out.append((self._sem(sk), v))
        return out

    def _commit(self, ev, reads, writes):
        for k in writes:
            self.lastw[k] = ev
            self.readers[k] = {}
        for k in reads:
            r = self.readers.setdefault(k, {})
            if ev[1] > r.get(ev[0], 0):
                r[ev[0]] = ev[1]

    def op(self, eng, meth, reads=(), writes=(), inc=True, **kw):
        waits = self._need(eng, reads, writes)
        if inc:
            self.ecnt[eng] += 1
            ev = (('e', eng), self.ecnt[eng])
        else:
            ev = (('e', eng), self.ecnt[eng] + 1)
        self._commit(ev, reads, writes)
        sem = self.esem[eng]

        def run(e):
            for s, v in waits:
                e.wait_ge(s, v)
            ins = getattr(e, meth)(**kw)
            if inc:
                ins.then_inc(sem, 1)
        self.prog[eng].append(run)

    def dma(self, q, out, in_, reads=(), writes=(), sname='misc'):
        if sname not in self.dsem:
            self.dsem[sname] = [self.free.pop(), 0]
        waits = self._need(q, reads, writes)
        ds = self.dsem[sname]
        ds[1] += 16
        ev = (('d', sname), ds[1])
        self._commit(ev, reads, writes)
        sem = ds[0]

        def run(e):
            for s, v in waits:
                e.wait_ge(s, v)
            e.dma_start(out=out, in_=in_).then_inc(sem, 16)
        self.prog[q].append(run)

    def barrier(self, engines=None):
        evs = [(('e', e), self.ecnt[e]) for e in self.ENG if self.ecnt[e] > 0]
        evs += [(('d', n), c[1]) for n, c in self.dsem.items()]
        for eng in (engines or self.ENG):
            waits = []
            for sk, v in evs:
                if sk == ('e', eng):
                    continue
                if self.waited[eng].get(sk, 0) >= v:
                    continue
                self.waited[eng][sk] = v
                waits.append((self._sem(sk), v))

            def run(e, waits=waits):
                for s, v in waits:
                    e.wait_ge(s, v)
            self.prog[eng].append(run)
        if engines is None:
            self.lastw = {}
            self.readers = {}

    def emit(self):
        with self.nc.Block() as block:
            for eng in self.ENG:
                prog = self.prog[eng]

                def body(e, prog=prog):
                    for run in prog:
                        run(e)
                getattr(block, self.ATTR[eng])(body)


class Arena:
    def __init__(self, ap_f32, nwords):
        self.t = ap_f32
        self.n = nwords
        self.off = 0

    def mark(self):
        return self.off

    def release(self, m):
        self.off = m

    def seek_bytes(self, b):
        assert b % 32 == 0
        self.off = b // 4

    def alloc(self, free_shape, dt):
        nel = 1
        for s in free_shape:
            nel *= s
        nbytes = nel * DT_SIZE[dt]
        nw = (nbytes + 31) // 32 * 8
        assert self.off + nw <= self.n, f"arena overflow: {self.off + nw} > {self.n}"
        ap = self.t[:, self.off:self.off + nw]
        self.off += nw
        if dt != F32:
            ap = ap.bitcast(dt)
        ap = ap[:, 0:nel]
        if len(free_shape) == 2:
            ap = ap.rearrange("p (a b) -> p a b", b=free_shape[1])
        elif len(free_shape) == 3:
            ap = ap.rearrange("p (a b c) -> p a b c", b=free_shape[1], c=free_shape[2])
        return ap


def build_program():
    nc = bass.Bass("TRN2", target_bir_lowering=False)
    dr = {}

    def din(name, shape, dt=F32):
        dr[name] = nc.dram_tensor(name, shape, dt, kind="ExternalInput").ap()

    din("x", [T, D]); din("c", [1, D]); din("positions", [1, T], I32)
    din("w_ada", [D, 9 * D]); din("b_ada", [1, 9 * D])
    din("g_ffn1", [1, D]); din("w_ffn1_in", [D, 2 * DFF]); din("w_ffn1_out", [DFF, D])
    din("g_mix", [1, D]); din("w_in", [D, INW]); din("b_in", [1, INW]); din("sinks", [1, 16])
    din("g_out_a", [1, 1024]); din("g_out_b", [1, 1024]); din("w_out", [D, D]); din("b_out", [1, D])
    din("g_ffn2", [1, D]); din("w_ffn2_in", [D, 2 * DFF]); din("w_ffn2_out", [DFF, D]); din("g_final", [1, D])
    dr["y"] = nc.dram_tensor("y", [T, D], F32, kind="ExternalOutput").ap()
    dbg = STOP_AFTER is not None
    skind = "ExternalOutput" if dbg else "Internal"
    dr["x1"] = nc.dram_tensor("x1", [T, D], F32, kind=skind).ap()
    dr["x2"] = nc.dram_tensor("x2", [T, D], F32, kind=skind).ap()
    dr["x3"] = nc.dram_tensor("x3", [T, D], F32, kind=skind).ap()
    dr["gbc"] = nc.dram_tensor("gbc", [3, 128, D], F32, kind=skind).ap()
    dr["qkT"] = nc.dram_tensor("qkT", [26, 128, T], BF16, kind=skind).ap()
    if dbg:
        dr["dbg"] = nc.dram_tensor("dbg", [128, 2048], F32, kind="ExternalOutput").ap()

    ARENA_WORDS = 51 * 1024
    with contextlib.ExitStack() as st:
        arena_t = st.enter_context(nc.sbuf_tensor("arena", [128, ARENA_WORDS], F32))
        iar = st.enter_context(nc.sbuf_tensor("iarena", [128, 768], I32))
        ps = [st.enter_context(nc.psum_tensor(f"ps{i}", [128, 512], F32)) for i in range(8)]
        sems = [st.enter_context(nc.semaphore(f"s{i}")) for i in range(96)]
        S = Sched(nc, sems)
        A = Arena(arena_t[:, :], ARENA_WORDS)

        def PS(i):
            return ps[i][:, :]

        ident = A.alloc((128,), F32)
        ones = A.alloc((128,), F32)
        condT = A.alloc((16,), F32)
        modT = A.alloc((9, 16), F32)
        gscT = A.alloc((3, 16), F32)
        ssq = A.alloc((16,), F32)
        rstd = A.alloc((16,), F32)
        ssq2 = A.alloc((32,), F32)
        rstd2 = A.alloc((32,), F32)
        rl = A.alloc((8,), F32)
        esink = A.alloc((16,), F32)
        cosT = A.alloc((16, 8), F32)
        sinT = A.alloc((16, 8), F32)
        st16 = A.alloc((128,), F32)

        S.op('pool', 'memset', writes=['ones'], ap=ones, constant=1.0)
        S.op('pool', 'memset', writes=['ident'], ap=ident, constant=1.0)
        S.op('pool', 'affine_select', reads=['ident'], writes=['ident'], out=ident, in_=ident,
             pattern=[[-1, 128]], compare_op=ALU.is_equal, fill=0.0, base=0, channel_multiplier=1)

        def vec_to_T(src_row, dst, extra=None):
            S.dma('sp', out=st16[0:16, :], in_=src_row.rearrange("o (a p) -> (o a) p", p=128), writes=['st16'], sname='st16')
            S.op('pe', 'transpose', reads=['st16', 'ident'], writes=[('ps', 7)], out=ps[7][:, 0:16], in_=st16[0:16, :],
                 identity=ident[0:16, 0:16])
            extra(ps[7][:, 0:16])

        m0 = A.mark()
        vec_to_T(dr["c"], condT, lambda p: S.op('act', 'activation', reads=[('ps', 7)], writes=['condT'],
                                                out=condT, in_=p, func=AF.Silu))
        wv = dr["w_ada"].rearrange("(kc p) n -> p kc n", p=128)
        wcnt = [0]

        def m_block_gen(blk, wt, nk, a, ak, gbp, pbanks):
            S.op('pool', 'memset', writes=[ak], ap=a, constant=0.0)
            S.dma('sp', out=a[0:1, :], in_=dr["b_ada"][0:1, blk * 2048:(blk + 1) * 2048], writes=[ak], sname='bada')
            nst = 16 // nk

            def load(st):
                sl = wcnt[0] % len(wt)
                wcnt[0] += 1
                S.dma('sp', out=wt[sl][:, 0:nk, :], in_=wv[:, st * nk:(st + 1) * nk, blk * 2048:(blk + 1) * 2048], writes=[('wt', sl)],
                      sname=f'wt{sl}')
                return sl
            slots = {0: load(0)}
            for st in range(nst):
                if st + 1 < nst:
                    slots[st + 1] = load(st + 1)
                sl = slots[st]
                for j in range(nk):
                    kc = st * nk + j
                    S.op('dve', 'scalar_tensor_tensor', reads=[('wt', sl), 'condT', ak], writes=[ak], out=a, in0=wt[sl][:, j, :],
                         scalar=condT[:, kc:kc + 1], in1=a, op0=ALU.mult, op1=ALU.add)
                yield
            if blk % 3 != 2:
                pb = pbanks[0]
                for cc in range(16):
                    S.op('pe', 'matmul', reads=[ak, 'ones'], writes=[('ps', pb)], inc=(cc == 15), out=ps[pb][:, cc:cc + 1],
                         lhsT=a[:, cc * 128:(cc + 1) * 128], rhs=ones[:, 0:1], start=True, stop=True)
                S.op('dve', 'tensor_copy', reads=[('ps', pb)], writes=[('modT', blk)], out=modT[:, blk, :], in_=ps[pb][:, 0:16])
            else:
                sub = blk // 3
                for s4 in range(4):
                    pb = pbanks[s4 % 2]
                    S.op('pe', 'matmul', reads=[ak, 'ones'], writes=[('ps', pb)], out=PS(pb), lhsT=ones,
                         rhs=a[:, s4 * 512:(s4 + 1) * 512], start=True, stop=True)
                    S.op('act', 'activation', reads=[('ps', pb)], writes=[('gbp', s4 % 2)], out=gbp[s4 % 2], in_=PS(pb),
                         func=AF.Copy, scale=(1.0 if sub == 1 else 0.5))
                    S.dma('sp', out=dr["gbc"][sub][:, s4 * 512:(s4 + 1) * 512], in_=gbp[s4 % 2], reads=[('gbp', s4 % 2)],
                          writes=[('gbc', sub)], sname=f'gbst{s4 % 2}')
            yield

        def gsc_for(sub):
            gname = ("g_ffn1", "g_mix", "g_ffn2")[sub]
            vec_to_T(dr[gname], None, lambda p: S.op(
                'dve', 'scalar_tensor_tensor', reads=[('ps', 7), ('modT', 3 * sub + 1)], writes=[('gscT', sub)], out=gscT[:, sub, :],
                in0=modT[:, 3 * sub + 1, :], scalar=1.0, in1=p, op0=ALU.add, op1=ALU.mult))

        wtB = [A.alloc((4, 2048), F32) for _ in range(2)]
        accB = [A.alloc((2048,), F32) for _ in range(2)]
        gbpB = [A.alloc((512,), F32) for _ in range(2)]
        for blk in range(2):
            for _ in m_block_gen(blk, wtB, 4, accB[blk % 2], ('acc', blk % 2), gbpB, (blk % 2, 2 + blk % 2)):
                pass
        gsc_for(0)

        def m_chain(wt, a, gbp):
            for blk in range(2, 9):
                yield from m_block_gen(blk, wt, 1, a, ('accS', 0), gbp, (6, 7))
                if blk == 4:
                    gsc_for(1)
                if blk == 7:
                    gsc_for(2)
        S.barrier()
        A.release(m0)

        hT = A.alloc((16, 2048), BF16)
        H_END = A.mark()
        TOPB = ARENA_WORDS * 4
        V_BYTES = 16 * 18 * 65 * 2

        def rstd_ops(src, dst, kin, kout, inv_n):
            S.op('dve', 'tensor_scalar', reads=[kin], writes=[kout], out=dst, in0=src, scalar1=inv_n, scalar2=EPS,
                 op0=ALU.mult, op1=ALU.add)
            S.op('act', 'activation', reads=[kout], writes=[kout], out=dst, in_=dst, func=AF.Sqrt)
            S.op('dve', 'reciprocal', reads=[kout], writes=[kout], out=dst, in_=dst)

        NXT = 4
        NORM_ALIAS = [('xt', i) for i in range(NXT)] + ['junk']

        def norm_transpose(xin, sub, dst, dstkey):
            A.off = H_END
            xt = [A.alloc((2048,), F32) for _ in range(NXT)]
            junk = A.alloc((2048,), BF16)
            def stA1(t):
                b = t % NXT
                xk = ('xt', b)
                S.dma('sp', out=xt[b], in_=xin[t * 128:(t + 1) * 128, :], writes=[xk], sname=f'xt{b}')
                S.op('act', 'activation', reads=[xk], writes=['junk', ('ssq', t)], out=junk, in_=xt[b], func=AF.Square,
                     accum_out=ssq[:, t:t + 1])
                S.op('dve', 'tensor_scalar', reads=[('ssq', t)], writes=[('rstd', t)], out=rstd[:, t:t + 1], in0=ssq[:, t:t + 1],
                     scalar1=1.0 / D, scalar2=EPS, op0=ALU.mult, op1=ALU.add)

            def stA2(t):
                b = t % NXT
                xk = ('xt', b)
                S.op('act', 'activation', reads=[('rstd', t)], writes=[('rstd', t)], out=rstd[:, t:t + 1], in_=rstd[:, t:t + 1],
                     func=AF.Sqrt)
                S.op('dve', 'reciprocal', reads=[('rstd', t)], writes=[('rstd', t)], out=rstd[:, t:t + 1], in_=rstd[:, t:t + 1])
                S.op('dve', 'tensor_scalar', reads=[xk, ('rstd', t)], writes=[xk], out=xt[b], in0=xt[b],
                     scalar1=rstd[:, t:t + 1], scalar2=None, op0=ALU.mult)

            def stB(t):
                b = t % NXT
                xk = ('xt', b)
                for g4 in range(4):
                    pb = 4 + g4
                    for i in range(4):
                        kc = g4 * 4 + i
                        S.op('pe', 'transpose', reads=[xk, 'ident'], writes=[('ps', pb)], inc=(i == 3),
                             out=ps[pb][:, i * 128:(i + 1) * 128], in_=xt[b][:, kc * 128:(kc + 1) * 128], identity=ident)
                    for i in range(4):
                        kc = g4 * 4 + i
                        o = dst[:, kc, t * 128:(t + 1) * 128]
                        src = ps[pb][:, i * 128:(i + 1) * 128]
                        if g4 % 2 == 0:
                            S.op('act', 'activation', reads=[('ps', pb), ('gscT', sub), ('modT', 3 * sub)], writes=[(dstkey, t, kc)], out=o, in_=src,
                                 func=AF.Identity, scale=gscT[:, sub, kc:kc + 1], bias=modT[:, 3 * sub, kc:kc + 1])
                        else:
                            S.op('dve', 'tensor_scalar', reads=[('ps', pb), ('gscT', sub), ('modT', 3 * sub)], writes=[(dstkey, t, kc)], out=o, in0=src,
                                 scalar1=gscT[:, sub, kc:kc + 1], scalar2=modT[:, 3 * sub, kc:kc + 1], op0=ALU.mult, op1=ALU.add)
            for i in range(NT + 2):
                if i < NT:
                    stA1(i)
                if 0 <= i - 1 < NT:
                    stA2(i - 1)
                if 0 <= i - 2 < NT:
                    stB(i - 2)
            A.off = H_END

        def ffn(xin, xout, xname, w_in_d, w_out_d, sub, passes=((0, 12), (12, 24), (24, 34), (34, 44)), with_m=False):
            norm_transpose(xin, sub, hT, 'hT')
            passes = list(passes)
            maxn = max(b_ - a_ for a_, b_ in passes)
            wd = [A.alloc((maxn, 512), BF16) for _ in range(2)]
            wg = [A.alloc((16, 256), BF16) for _ in range(2)]
            wu = [A.alloc((16, 256), BF16) for _ in range(2)]
            actT = A.alloc((maxn, 2048), BF16)
            sg = [A.alloc((512,), F32) for _ in range(2)]
            gbt = A.alloc((2048,), F32)
            xp = [A.alloc((512,), F32) for _ in range(6)]
            tmp = [A.alloc((512,), F32) for _ in range(2)]
            mch = None
            if with_m:
                wtS = [A.alloc((1, 2048), F32) for _ in range(2)]
                accS = A.alloc((2048,), F32)
                gbpS = [A.alloc((512,), F32) for _ in range(2)]
                mch = m_chain(wtS, accS, gbpS)
            wiv = w_in_d.rearrange("(kc p) n -> p kc n", p=128)
            wov = w_out_d.rearrange("(j p) d -> p j d", p=128)
            pairs = []
            for pi, (a, b) in enumerate(passes):
                for pr in range(a // 2, b // 2):
                    pairs.append((pi, pr))

            def load_pair(idx):
                _, pr = pairs[idx]
                sl = idx % 2
                S.dma('pool', out=wg[sl], in_=wiv[:, :, pr * 256:(pr + 1) * 256], writes=[('wg', sl)] + NORM_ALIAS, sname=f'wg{sl}')
                S.dma('pool', out=wu[sl], in_=wiv[:, :, DFF + pr * 256:DFF + (pr + 1) * 256], writes=[('wu', sl)] + NORM_ALIAS,
                      sname=f'wu{sl}')

            wd_count = [0]

            def load_wd(pi, s):
                a, b = passes[pi]
                sl = (pi * 4 + s) % 2
                S.dma('pool', out=wd[sl][:, 0:b - a, :], in_=wov[:, a:b, s * 512:(s + 1) * 512], writes=[('wd', sl)] + NORM_ALIAS,
                      sname=f'wd{sl}')

            unit = 0
            ep = 0
            load_pair(0)
            for idx, (pi, pr) in enumerate(pairs):
                a, b = passes[pi]
                if idx + 1 < len(pairs):
                    load_pair(idx + 1)
                sl = idx % 2
                for ci in range(2):
                    jj = 2 * pr + ci - a
                    for ts in range(4):
                        slot = unit % 2
                        unit += 1
                        gbk, ubk = 2 * slot, 2 * slot + 1
                        for kc in range(16):
                            S.op('pe', 'matmul', reads=[('wg', sl)] + [('hT', 4 * ts + q, kc) for q in range(4)], writes=[('ps', gbk)], inc=(kc == 15), out=PS(gbk),
                                 lhsT=wg[sl][:, kc, ci * 128:(ci + 1) * 128], rhs=hT[:, kc, ts * 512:(ts + 1) * 512],
                                 start=(kc == 0), stop=(kc == 15))
                        for kc in range(16):
                            S.op('pe', 'matmul', reads=[('wu', sl)] + [('hT', 4 * ts + q, kc) for q in range(4)], writes=[('ps', ubk)], inc=(kc == 15), out=PS(ubk),
                                 lhsT=wu[sl][:, kc, ci * 128:(ci + 1) * 128], rhs=hT[:, kc, ts * 512:(ts + 1) * 512],
                                 start=(kc == 0), stop=(kc == 15))
                        S.op('act', 'activation', reads=[('ps', gbk)], writes=[('sg', slot)], out=sg[slot], in_=PS(gbk), func=AF.Silu)
                        S.op('dve', 'tensor_tensor', reads=[('sg', slot), ('ps', ubk)], writes=[('actT', ts, jj)],
                             out=actT[:, jj, ts * 512:(ts + 1) * 512], in0=sg[slot], in1=PS(ubk), op=ALU.mult)
                        if mch is not None:
                            next(mch, None)
                first_in_pass = (pr == a // 2)
                second_in_pass = (pr == a // 2 + 1)
                if first_in_pass:
                    load_wd(pi, 0)
                if second_in_pass:
                    load_wd(pi, 1)
                if pr == b // 2 - 1:
                    n = b - a
                    if pi == 0:
                        S.dma('sp', out=gbt, in_=dr["gbc"][sub], reads=[('gbc', sub)], writes=['gbt'], sname='gbt')
                    xsrc = xin if pi == 0 else xout
                    NXP = len(xp)
                    PRE = 3
                    dunits = [(s_, t_) for s_ in range(4) for t_ in range(NT)]

                    def xload(k):
                        s_, t_ = dunits[k]
                        xs_ = k % NXP
                        S.dma('act', out=xp[xs_], in_=xsrc[t_ * 128:(t_ + 1) * 128, s_ * 512:(s_ + 1) * 512],
                              reads=[(xname, t_, s_)] if pi > 0 else [], writes=[('xp', xs_)], sname=f'ald{xs_}')
                    for k in range(PRE):
                        xload(k)
                    for k, (s, t) in enumerate(dunits):
                        if k + PRE < len(dunits):
                            xload(k + PRE)
                        wsl = (pi * 4 + s) % 2
                        ob = 4 + ep % 2
                        xs = k % NXP
                        ts_ = ep % 2
                        ep += 1
                        xkey = (xname, t, s)
                        for jj in range(n):
                            S.op('pe', 'matmul', reads=[('actT', t // 4, jj), ('wd', wsl)], writes=[('ps', ob)], inc=(jj == n - 1),
                                 out=PS(ob), lhsT=actT[:, jj, t * 128:(t + 1) * 128], rhs=wd[wsl][:, jj, :],
                                 start=(jj == 0), stop=(jj == n - 1))
                        S.op('dve', 'tensor_tensor', reads=[('ps', ob), 'gbt'], writes=[('tmp', ts_)], out=tmp[ts_], in0=PS(ob),
                             in1=gbt[:, s * 512:(s + 1) * 512], op=ALU.mult)
                        S.op('dve', 'tensor_tensor', reads=[('tmp', ts_), ('xp', xs)], writes=[('xp', xs)], out=xp[xs],
                             in0=tmp[ts_], in1=xp[xs], op=ALU.add)
                        S.dma('sp', out=xout[t * 128:(t + 1) * 128, s * 512:(s + 1) * 512], in_=xp[xs], reads=[('xp', xs)],
                              writes=[xkey], sname=f'xst{xs}')
                        if t == NT - 1 and s + 2 < 4:
                            load_wd(pi, s + 2)
            if mch is not None:
                for _ in mch:
                    pass
            S.barrier()
            A.off = H_END

        ffn(dr["x"], dr["x1"], 'x1', dr["w_ffn1_in"], dr["w_ffn1_out"], 0,
            passes=((0, 8), (8, 16), (16, 24), (24, 32), (32, 40), (40, 44)), with_m=True)

        def finish():
            S.barrier(engines=['sp'])
            S.emit()

        if STOP_AFTER in ('mod', 'ffn1'):
            finish()
            return nc

        A.off = H_END
        wis = [A.alloc((16, 512), BF16) for _ in range(2)]
        norm_transpose(dr["x1"], 1, hT, 'hT')
        A.off = H_END
        wis = [A.alloc((16, 512), BF16) for _ in range(2)]
        A.seek_bytes(TOPB - V_BYTES)
        V_all = A.alloc((16, 18, 65), BF16)
        A.off = H_END + 2 * 16 * 512 * 2 // 4
        S.op('dve', 'memset', writes=['V_all'], ap=V_all, constant=1.0)
        b_in_bc = A.alloc((INW,), F32)
        S.dma('sp', out=b_in_bc, in_=dr["b_in"].broadcast_to([128, INW]), writes=['b_in_bc'] + NORM_ALIAS, sname='bc')
        posi = iar[:, 0:128]
        posf = A.alloc((128,), F32)
        posT = A.alloc((16,), F32)
        invf = A.alloc((8,), F32)
        ang = A.alloc((16, 8), F32)
        S.dma('sp', out=posi[0:16, :], in_=dr["positions"].rearrange("o (a p) -> (o a) p", p=128), writes=['posi'], sname='bc')
        S.op('dve', 'tensor_copy', reads=['posi'], writes=['posf'], out=posf[0:16, :], in_=posi[0:16, :])
        S.op('pe', 'transpose', reads=['posf', 'ident'], writes=[('ps', 7)], out=ps[7][:, 0:16], in_=posf[0:16, :],
             identity=ident[0:16, 0:16])
        S.op('dve', 'tensor_copy', reads=[('ps', 7)], writes=['posT'], out=posT, in_=ps[7][:, 0:16])
        inv_freq = (np.float32(500000.0) ** (-(np.arange(0, 16, 2, dtype=np.float32) / np.float32(16)))).astype(np.float32)
        for f in range(8):
            S.op('dve', 'memset', writes=['invf'], ap=invf[:, f:f + 1], constant=float(inv_freq[f]))
        S.op('dve', 'tensor_tensor', reads=['posT', 'invf'], writes=['ang'], out=ang,
             in0=posT.unsqueeze(2).broadcast_to([128, 16, 8]), in1=invf.unsqueeze(1).broadcast_to([128, 16, 8]), op=ALU.mult)
        kq = A.alloc((16, 8), F32)
        kqi = iar[:, 128:256].rearrange("p (a b) -> p a b", b=8)
        C1 = 6.28125
        C2 = 2.0 * PI - C1
        for dst, shift in ((sinT, 0.0), (cosT, 0.5 * PI)):
            S.op('dve', 'tensor_scalar', reads=['ang'], writes=['rtab'], out=dst, in0=ang, scalar1=shift, scalar2=None, op0=ALU.add)
            S.op('dve', 'tensor_scalar', reads=['rtab'], writes=['kq'], out=kq, in0=dst, scalar1=1.0 / (2.0 * PI), scalar2=None,
                 op0=ALU.mult)
            S.op('dve', 'tensor_copy', reads=['kq'], writes=['kqi'], out=kqi, in_=kq)
            S.op('dve', 'tensor_copy', reads=['kqi'], writes=['kq'], out=kq, in_=kqi)
            S.op('dve', 'scalar_tensor_tensor', reads=['kq', 'rtab'], writes=['rtab'], out=dst, in0=kq, scalar=-C1, in1=dst,
                 op0=ALU.mult, op1=ALU.add)
            S.op('dve', 'scalar_tensor_tensor', reads=['kq', 'rtab'], writes=['rtab'], out=dst, in0=kq, scalar=-C2, in1=dst,
                 op0=ALU.mult, op1=ALU.add)
            S.op('dve', 'tensor_scalar', reads=['rtab'], writes=['rtab'], out=dst, in0=dst, scalar1=-PI, scalar2=PI,
                 op0=ALU.max, op1=ALU.min)
            S.op('act', 'activation', reads=['rtab'], writes=['rtab'], out=dst, in_=dst, func=AF.Sin)

        slabs = [(0, 512, 'qk', 0), (512, 512, 'qk', 4), (1280, 512, 'qk', 8), (1792, 512, 'qk', 12),
                 (2304, 512, 'qk', 16), (2816, 512, 'qk', 20), (1024, 256, 'kava', 24),
                 (3328, 512, 'v', 2), (3840, 512, 'v', 10)]
        qst = [A.alloc((4, 2048), BF16) for _ in range(2)]
        pj = [A.alloc((512,), F32) for _ in range(4)]
        rt = [A.alloc((8, 8), F32) for _ in range(4)]
        kd = [A.alloc((2, 128), F32) for _ in range(2)]
        assert A.off * 4 <= TOPB - V_BYTES
        wiv = dr["w_in"].rearrange("(kc p) n -> p kc n", p=128)

        def load_slab(si):
            c0, w, _, _ = slabs[si]
            S.dma('pool', out=wis[si % 2][:, :, 0:w], in_=wiv[:, :, c0:c0 + w], writes=[('wis', si % 2)] + NORM_ALIAS,
                  sname=f'wis{si % 2}')

        def rope(pjt, nh, t, pk):
            v = pjt[:, 0:nh * 64].rearrange("p (h d) -> p h d", d=64)
            x1, x2 = v[:, :, 0:8], v[:, :, 8:16]
            cb = cosT[:, t, :].unsqueeze(1).broadcast_to([128, nh, 8])
            sb = sinT[:, t, :].unsqueeze(1).broadcast_to([128, nh, 8])
            r = [q[:, 0:nh, :] for q in rt]
            S.op('dve', 'tensor_tensor', reads=[pk, 'rtab'], writes=['rt0'], out=r[0], in0=x1, in1=cb, op=ALU.mult)
            S.op('dve', 'tensor_tensor', reads=[pk, 'rtab'], writes=['rt1'], out=r[1], in0=x2, in1=sb, op=ALU.mult)
            S.op('dve', 'tensor_tensor', reads=[pk, 'rtab'], writes=['rt2'], out=r[2], in0=x2, in1=cb, op=ALU.mult)
            S.op('dve', 'tensor_tensor', reads=[pk, 'rtab'], writes=['rt3'], out=r[3], in0=x1, in1=sb, op=ALU.mult)
            S.op('dve', 'tensor_tensor', reads=['rt0', 'rt1'], writes=[pk], out=x1, in0=r[0], in1=r[1], op=ALU.subtract)
            S.op('dve', 'tensor_tensor', reads=['rt2', 'rt3'], writes=[pk], out=x2, in0=r[2], in1=r[3], op=ALU.add)

        load_slab(0)
        u = 0
        trc = [0]
        pend = [None]
        for si, (c0, w, kind, aux) in enumerate(slabs):
            if si + 1 < len(slabs):
                load_slab(si + 1)
            wsl = si % 2
            qsl = si % 2
            for t in range(NT):
                pb = (0, 1, 4, 5)[u % 4]
                pjt = pj[u % 4]
                pk = ('pj', u % 4)
                u += 1
                for kc in range(16):
                    S.op('pe', 'matmul', reads=[('wis', wsl), ('hT', t, kc)], writes=[('ps', pb)], inc=(kc == 15), out=ps[pb][:, 0:w],
                         lhsT=hT[:, kc, t * 128:(t + 1) * 128], rhs=wis[wsl][:, kc, 0:w], start=(kc == 0), stop=(kc == 15))
                S.op('dve', 'tensor_tensor', reads=[('ps', pb), 'b_in_bc'], writes=[pk], out=pjt[:, 0:w], in0=ps[pb][:, 0:w],
                     in1=b_in_bc[:, c0:c0 + w], op=ALU.add)
                if pend[0] is not None:
                    pend[0]()
                    pend[0] = None
                if kind == 'qk':
                    rope(pjt, 8, t, pk)

                    def post(pjt=pjt, pk=pk, t=t, qsl=qsl):
                        tb = (2, 3, 6, 7)[trc[0] % 4]
                        trc[0] += 1
                        for i in range(4):
                            S.op('pe', 'transpose', reads=[pk, 'ident'], writes=[('ps', tb)], inc=(i == 3),
                                 out=ps[tb][:, i * 128:(i + 1) * 128], in_=pjt[:, i * 128:(i + 1) * 128], identity=ident)
                        S.op('act', 'activation', reads=[('ps', tb)], writes=[('qst', qsl)], out=qst[qsl][:, :, t * 128:(t + 1) * 128],
                             in_=ps[tb][:, :].rearrange("p (a b) -> p a b", b=128), func=AF.Copy)
                    pend[0] = post
                elif kind == 'kava':
                    rope(pjt, 2, t, pk)
                    for hh in range(2):
                        for dd in range(2):
                            S.op('dve', 'tensor_copy', reads=[pk], writes=[('kd', t % 2)], out=kd[t % 2][:, hh, dd * 64:(dd + 1) * 64],
                                 in_=pjt[:, hh * 64:(hh + 1) * 64])
                    S.op('act', 'activation', reads=[pk], writes=['V_all'], out=V_all[:, t, 0:2, 0:64],
                         in_=pjt[:, 128:256].rearrange("p (h d) -> p h d", d=64), func=AF.Copy)

                    def post(t=t, qsl=qsl):
                        tb = (2, 3, 6, 7)[trc[0] % 4]
                        trc[0] += 1
                        for i in range(2):
                            S.op('pe', 'transpose', reads=[('kd', t % 2), 'ident'], writes=[('ps', tb)], inc=(i == 1),
                                 out=ps[tb][:, i * 128:(i + 1) * 128], in_=kd[t % 2][:, i, :], identity=ident)
                        S.op('act', 'activation', reads=[('ps', tb)], writes=[('qst', qsl)], out=qst[qsl][:, 0:2, t * 128:(t + 1) * 128],
                             in_=ps[tb][:, 0:256].rearrange("p (a b) -> p a b", b=128), func=AF.Copy)
                    pend[0] = post
                else:
                    S.op('act', 'activation', reads=[pk], writes=['V_all'], out=V_all[:, t, aux:aux + 8, 0:64],
                         in_=pjt[:, 0:512].rearrange("p (h d) -> p h d", d=64), func=AF.Copy)
            if pend[0] is not None and kind in ('qk', 'kava'):
                pend[0]()
                pend[0] = None
            if kind == 'qk':
                S.dma('sp', out=dr["qkT"][aux:aux + 4].rearrange("c p n -> p c n"), in_=qst[qsl], reads=[('qst', qsl)],
                      writes=[('qkT', aux // 4)], sname=f'qst{qsl}')
            elif kind == 'kava':
                S.dma('sp', out=dr["qkT"][24:26].rearrange("c p n -> p c n"), in_=qst[qsl][:, 0:2, :], reads=[('qst', qsl)],
                      writes=[('qkT', 6)], sname=f'qst{qsl}')
        if STOP_AFTER == 'proj':
            for t in range(NT):
                S.op('dve', 'tensor_copy', reads=['V_all'], writes=[('pj', 0)], out=pj[0],
                     in_=V_all[:, t, 0:8, 0:64].rearrange("p h d -> p (h d)"))
                S.dma('sp', out=dr["x2"][t * 128:(t + 1) * 128, 0:512], in_=pj[0], reads=[('pj', 0)], writes=[('x2o', t)], sname='out')
            finish()
            return nc
        S.barrier()

        Y_END = m0 + 16 * 2048
        A.off = Y_END
        maskB = A.alloc((2048,), BF16)
        maskA4 = A.alloc((512,), BF16)
        maskA2 = A.alloc((256,), BF16)
        qT = [A.alloc((2048,), BF16) for _ in range(2)]
        kT = [A.alloc((2048,), BF16) for _ in range(2)]
        PT = [A.alloc((512,), BF16) for _ in range(8)]
        sk16 = A.alloc((16,), F32)
        assert A.off * 4 <= TOPB - V_BYTES
        A.off = m0
        vf = A.alloc((2048,), F32)
        c1 = A.alloc((2048,), F32)
        c2 = A.alloc((2048,), F32)
        ji = iar[:, 256:768]
        t4 = A.alloc((2048,), F32)
        t16 = A.alloc((2048,), F32)
        A.off = m0
        y_all = A.alloc((16, 2048), F32)
        S.op('pool', 'iota', writes=['vf'], out=vf, pattern=[[1, 2048]], base=2048, channel_multiplier=-1,
             allow_small_or_imprecise_dtypes=True)
        S.op('dve', 'tensor_scalar', reads=['vf'], writes=['c1'], out=c1, in0=vf, scalar1=2048.0, scalar2=None, op0=ALU.is_ge)
        for blk, diag in ((0, False), (1, True), (2, False), (3, True)):
            if diag:
                S.op('dve', 'tensor_copy', reads=['c1'], writes=['maskA'], out=maskA4[:, blk * 128:(blk + 1) * 128], in_=c1[:, 0:128])
            else:
                S.op('dve', 'tensor_scalar', reads=['c1'], writes=['maskA'], out=maskA4[:, blk * 128:(blk + 1) * 128], in0=c1[:, 0:128],
                     scalar1=-1.0, scalar2=1.0, op0=ALU.mult, op1=ALU.add)
        for blk in range(2):
            S.op('dve', 'tensor_copy', reads=['c1'], writes=['maskA'], out=maskA2[:, blk * 128:(blk + 1) * 128], in_=c1[:, 0:128])
        S.op('dve', 'scalar_tensor_tensor', reads=['vf', 'c1'], writes=['c2'], out=c2, in0=vf, scalar=2048.0 + 128.0, in1=c1,
             op0=ALU.is_le, op1=ALU.mult)
        for ch in range(4):
            cs = slice(ch * 512, (ch + 1) * 512)
            for msk, dstt, dk in ((3, t4, 't4'), (15, t16, 't16')):
                S.op('dve', 'tensor_copy', reads=['vf'], writes=['ji'], out=ji, in_=vf[:, cs])
                S.op('dve', 'tensor_scalar', reads=['ji'], writes=['ji'], out=ji, in0=ji, scalar1=msk, scalar2=None, op0=ALU.bitwise_and)
                S.op('dve', 'tensor_scalar', reads=['ji'], writes=[dk], out=dstt[:, cs], in0=ji, scalar1=0.0, scalar2=None, op0=ALU.is_equal)
        S.op('dve', 'scalar_tensor_tensor', reads=['vf', 't4'], writes=['t4'], out=t4, in0=vf, scalar=2048.0 + 512.0, in1=t4,
             op0=ALU.is_le, op1=ALU.mult)
        S.op('dve', 'tensor_tensor', reads=['t4', 'c1'], writes=['t4'], out=t4, in0=t4, in1=c1, op=ALU.mult)
        S.op('dve', 'tensor_tensor', reads=['t4', 'c2'], writes=['c2'], out=c2, in0=t4, in1=c2, op=ALU.add)
        S.op('dve', 'tensor_tensor', reads=['t16', 'c1'], writes=['t16'], out=t16, in0=t16, in1=c1, op=ALU.mult)
        S.op('dve', 'tensor_tensor', reads=['t16', 'c2'], writes=['maskB'], out=maskB, in0=t16, in1=c2, op=ALU.add)
        S.dma('sp', out=sk16, in_=dr["sinks"].broadcast_to([128, 16]), writes=['sk16'], sname='bc')
        S.op('act', 'activation', reads=['sk16'], writes=['esink'], out=esink, in_=sk16, func=AF.Exp)
        S.barrier()

        heads = []
        for h in range(16):
            heads.append((h // 2, 64 * (h % 2), 24 + h // 8, h // 8, h * 64, True, h))
        for h in range(16):
            heads.append((8 + h // 2, 64 * (h % 2), 16 + h // 2, 2 + h, 1024 + h * 64, False, None))
        cur = {'q': [None, 0, 0], 'k': [None, 0, 0]}

        def ensure(kind, chunk, bufs):
            c = cur[kind]
            if c[0] == chunk:
                return c[1]
            sl = c[2] % 2
            c[0], c[1], c[2] = chunk, sl, c[2] + 1
            S.dma('sp', out=bufs[sl], in_=dr["qkT"][chunk], writes=[(kind + 'T', sl)], sname=f'{kind}T{sl}')
            return sl

        SB = (0, 1, 6, 7)
        NPT = len(PT)
        units = []
        ruse = [0]

        def new_rl(n):
            i = ruse[0] % 4
            ruse[0] += 1
            return rl[:, 2 * i:2 * i + n], ('rl', i)

        for kv in range(2):
            for ip in range(4):
                qc = 4 * kv + ip
                h0 = 8 * kv + 2 * ip
                for qt in range(NT):
                    kts = [qt - 1, qt] if qt > 0 else [qt]
                    nb = len(kts)
                    sgr, pvl = [], []
                    for hd in range(2):
                        sgr.append(dict(mms=[(64 * hd, kt, qt * 128, 128, bi * 128) for bi, kt in enumerate(kts)], ncols=nb * 128,
                                        pt_off=hd * nb * 128))
                    n = 2 * nb * 128
                    ab = 2 + len(units) % 4

                    def evacA(ab=ab, qt=qt, h0=h0):
                        for hd in range(2):
                            r, rk = new_rl(1)
                            S.op('dve', 'tensor_scalar', reads=[('ps', ab), 'esink'], writes=[rk], out=r,
                                 in0=ps[ab][:, hd * 128 + 64:hd * 128 + 65], scalar1=esink[:, h0 + hd:h0 + hd + 1], scalar2=None, op0=ALU.add)
                            S.op('dve', 'reciprocal', reads=[rk], writes=[rk], out=r, in_=r)
                            S.op('act', 'activation', reads=[('ps', ab), rk], writes=[('y', qt, h0 + hd)],
                                 out=y_all[:, qt, (h0 + hd) * 64:(h0 + hd + 1) * 64], in_=ps[ab][:, hd * 128:hd * 128 + 64],
                                 func=AF.Identity, scale=r)
                    for hd in range(2):
                        for bi, kt in enumerate(kts):
                            last = (hd == 1 and bi == nb - 1)
                            pvl.append(dict(off=(hd * nb + bi) * 128, kt=kt, vidx=kv, bank=ab, col0=hd * 128, start=(bi == 0),
                                            stop=(bi == nb - 1), evac=evacA if last else None))
                    units.append(dict(qc=qc, kc=24 + kv, sgr=sgr, n=n, mask=(maskA4 if nb == 2 else maskA2), mk='maskA', pv=pvl))
        for h in range(16):
            qc, base, kcb, vidx, ycol = 8 + h // 2, 64 * (h % 2), 16 + h // 2, 2 + h, 1024 + h * 64
            for qs in range(4):
                for kt in range(4 * qs + 4):
                    q0 = max(kt * 128, qs * 512)
                    n = (qs + 1) * 512 - q0
                    d0 = q0 - kt * 128
                    pvl = []
                    for qt in range(q0 // 128, (qs + 1) * 4):
                        ab = 2 + qt % 4

                        def evacB(ab=ab, qt=qt, ycol=ycol):
                            r, rk = new_rl(1)
                            S.op('dve', 'reciprocal', reads=[('ps', ab)], writes=[rk], out=r, in_=ps[ab][:, 64:65])
                            S.op('dve', 'tensor_scalar', reads=[('ps', ab), rk], writes=[('y', qt, ycol // 64)], out=y_all[:, qt, ycol:ycol + 64],
                                 in0=ps[ab][:, 0:64], scalar1=r, scalar2=None, op0=ALU.mult)
                        pvl.append(dict(off=qt * 128 - q0, kt=kt, vidx=vidx, bank=ab, col0=0, start=(kt == 0), stop=(kt == qt),
                                        evac=evacB if kt == qt else None))
                    units.append(dict(qc=qc, kc=kcb, sgr=[dict(mms=[(base, kt, q0, n, 0)], ncols=n, pt_off=0)], n=n,
                                      mask=maskB[:, d0:d0 + n], mk='maskB', pv=pvl))

        sbc = [0]

        def emit_S(i):
            U = units[i]
            qs_ = ensure('q', U['qc'], qT)
            ks_ = ensure('k', U['kc'], kT)
            Q, K = qT[qs_], kT[ks_]
            n = U['n']
            pt = i % NPT
            for G in U['sgr']:
                sb_ = SB[sbc[0] % 4]
                sbc[0] += 1
                for j, (base, kt, q0, w, oc) in enumerate(G['mms']):
                    S.op('pe', 'matmul', reads=[('qT', qs_), ('kT', ks_)], writes=[('ps', sb_)], inc=(j == len(G['mms']) - 1),
                         out=ps[sb_][:, oc:oc + w], lhsT=K[base:base + 64, kt * 128:(kt + 1) * 128], rhs=Q[base:base + 64, q0:q0 + w],
                         start=True, stop=True)
                S.op('act', 'activation', reads=[('ps', sb_)], writes=[('PT', pt)], out=PT[pt][:, G['pt_off']:G['pt_off'] + G['ncols']],
                     in_=ps[sb_][:, 0:G['ncols']], func=AF.Exp, scale=0.125)
            S.op('dve', 'tensor_tensor', reads=[('PT', pt), U['mk']], writes=[('PT', pt)], out=PT[pt][:, 0:n],
                 in0=PT[pt][:, 0:n], in1=U['mask'], op=ALU.mult)

        def emit_PV(i):
            U = units[i]
            pt = i % NPT
            for e in U['pv']:
                ab = e['bank']
                S.op('pe', 'matmul', reads=[('PT', pt), 'V_all'], writes=[('ps', ab)], out=ps[ab][:, e['col0']:e['col0'] + 65],
                     lhsT=PT[pt][:, e['off']:e['off'] + 128], rhs=V_all[:, e['kt'], e['vidx'], :], start=e['start'], stop=e['stop'])
                if e['evac'] is not None:
                    e['evac']()

        LOOK = 5
        GRP = 3
        assert NPT >= LOOK + GRP
        for i0 in range(0, len(units) + LOOK + GRP, GRP):
            for i in range(i0, i0 + GRP):
                if i < len(units):
                    emit_S(i)
            for i in range(i0, i0 + GRP):
                if 0 <= i - LOOK < len(units):
                    emit_PV(i - LOOK)
        if STOP_AFTER == 'attn':
            for t in range(NT):
                S.dma('sp', out=dr["x2"][t * 128:(t + 1) * 128, :], in_=y_all[:, t, :], reads=[('y', t, h_) for h_ in range(32)], writes=[('x2o', t)], sname='out')
            finish()
            return nc
        S.barrier()

        goT = sk16
        A.off = Y_END
        junk2 = A.alloc((1024,), BF16)
        goT = A.alloc((16,), F32)
        A.seek_bytes(TOPB - 16 * 2048 * 2)
        assert A.off >= Y_END + 1024
        yT = A.alloc((16, 2048), BF16)
        S.dma('sp', out=st16[0:8, :], in_=dr["g_out_a"].rearrange("o (a p) -> (o a) p", p=128), writes=['st16'], sname='st16')
        S.dma('sp', out=st16[8:16, :], in_=dr["g_out_b"].rearrange("o (a p) -> (o a) p", p=128), writes=['st16'], sname='st16')
        S.op('pe', 'transpose', reads=['st16', 'ident'], writes=[('ps', 7)], out=ps[7][:, 0:16], in_=st16[0:16, :],
             identity=ident[0:16, 0:16])
        S.op('dve', 'tensor_copy', reads=[('ps', 7)], writes=['goT'], out=goT, in_=ps[7][:, 0:16])
        for t in range(NT):
            for g in range(2):
                col = t * 2 + g
                yv = y_all[:, t, g * 1024:(g + 1) * 1024]
                S.op('act', 'activation', reads=[('y', t, h_) for h_ in range(16 * g, 16 * g + 16)], writes=['junk2', 'ssq2'], out=junk2, in_=yv, func=AF.Square,
                     accum_out=ssq2[:, col:col + 1])
        rstd_ops(ssq2, rstd2, 'ssq2', 'rstd2', 1.0 / 1024)
        for t in range(NT):
            for g in range(2):
                col = t * 2 + g
                yv = y_all[:, t, g * 1024:(g + 1) * 1024]
                S.op('dve', 'tensor_scalar', reads=[('y', t, h_) for h_ in range(16 * g, 16 * g + 16)] + ['rstd2'],
                     writes=[('y', t, h_) for h_ in range(16 * g, 16 * g + 16)], out=yv, in0=yv,
                     scalar1=rstd2[:, col:col + 1], scalar2=None, op0=ALU.mult)
            for g4 in range(4):
                pb = 6 + g4 % 2
                for i in range(4):
                    kc = g4 * 4 + i
                    S.op('pe', 'transpose', reads=[('y', t, 2 * kc), ('y', t, 2 * kc + 1), 'ident'], writes=[('ps', pb)], inc=(i == 3),
                         out=ps[pb][:, i * 128:(i + 1) * 128], in_=y_all[:, t, kc * 128:(kc + 1) * 128], identity=ident)
                for i in range(4):
                    kc = g4 * 4 + i
                    o = yT[:, kc, t * 128:(t + 1) * 128]
                    src = ps[pb][:, i * 128:(i + 1) * 128]
                    if g4 % 2 == 0:
                        S.op('act', 'activation', reads=[('ps', pb), 'goT'], writes=[('yT', t, kc)], out=o, in_=src, func=AF.Identity,
                             scale=goT[:, kc:kc + 1])
                    else:
                        S.op('dve', 'tensor_scalar', reads=[('ps', pb), 'goT'], writes=[('yT', t, kc)], out=o, in0=src,
                             scalar1=goT[:, kc:kc + 1], scalar2=None, op0=ALU.mult)
        S.barrier()

        A.off = m0
        wos = [A.alloc((16, 512), BF16) for _ in range(2)]
        g1b = A.alloc((2048,), F32)
        bg = A.alloc((2048,), F32)
        xq = [A.alloc((512,), F32) for _ in range(6)]
        tq = [A.alloc((512,), F32) for _ in range(2)]
        assert A.off * 4 <= TOPB - 16 * 2048 * 2
        wov = dr["w_out"].rearrange("(kc p) n -> p kc n", p=128)
        S.dma('sp', out=bg, in_=dr["b_out"].broadcast_to([128, D]), writes=['bg'], sname='bc')
        S.dma('sp', out=g1b, in_=dr["gbc"][1], writes=['g1b'], sname='bc')
        S.op('dve', 'tensor_tensor', reads=['bg', 'g1b'], writes=['bg'], out=bg, in0=bg, in1=g1b, op=ALU.mult)
        S.dma('pool', out=wos[0], in_=wov[:, :, 0:512], writes=[('wos', 0)], sname='wos0')
        NXQ = len(xq)
        PRE = 3
        ounits = [(s_, t_) for s_ in range(4) for t_ in range(NT)]

        def qload(k):
            s_, t_ = ounits[k]
            S.dma('sp', out=xq[k % NXQ], in_=dr["x1"][t_ * 128:(t_ + 1) * 128, s_ * 512:(s_ + 1) * 512], writes=[('xq', k % NXQ)],
                  sname=f'xld{k % NXQ}')
        for k in range(PRE):
            qload(k)
        for k, (s, t) in enumerate(ounits):
            if k + PRE < len(ounits):
                qload(k + PRE)
            if t == 0 and s + 1 < 4:
                S.dma('pool', out=wos[(s + 1) % 2], in_=wov[:, :, (s + 1) * 512:(s + 2) * 512], writes=[('wos', (s + 1) % 2)],
                      sname=f'wos{(s + 1) % 2}')
            ob = 4 + k % 2
            xs = k % NXQ
            ts_ = k % 2
            for kc in range(16):
                S.op('pe', 'matmul', reads=[('yT', t, kc), ('wos', s % 2)], writes=[('ps', ob)], inc=(kc == 15), out=PS(ob),
                     lhsT=yT[:, kc, t * 128:(t + 1) * 128], rhs=wos[s % 2][:, kc, :], start=(kc == 0), stop=(kc == 15))
            S.op('dve', 'tensor_tensor', reads=[('ps', ob), 'g1b'], writes=[('tq', ts_)], out=tq[ts_], in0=PS(ob),
                 in1=g1b[:, s * 512:(s + 1) * 512], op=ALU.mult)
            S.op('dve', 'tensor_tensor', reads=[('tq', ts_), 'bg'], writes=[('tq', ts_)], out=tq[ts_], in0=tq[ts_],
                 in1=bg[:, s * 512:(s + 1) * 512], op=ALU.add)
            S.op('dve', 'tensor_tensor', reads=[('tq', ts_), ('xq', xs)], writes=[('xq', xs)], out=xq[xs], in0=tq[ts_], in1=xq[xs],
                 op=ALU.add)
            S.dma('sp', out=dr["x2"][t * 128:(t + 1) * 128, s * 512:(s + 1) * 512], in_=xq[xs], reads=[('xq', xs)],
                  writes=[('x2', t, s)], sname=f'xst{xs}')
        if STOP_AFTER == 'mix':
            finish()
            return nc
        S.barrier()

        ffn(dr["x2"], dr["x3"], 'x3', dr["w_ffn2_in"], dr["w_ffn2_out"], 2)

        A.off = m0
        gfb = A.alloc((2048,), F32)
        xf = [A.alloc((2048,), F32) for _ in range(6)]
        junk3 = A.alloc((2048,), BF16)
        S.dma('sp', out=gfb, in_=dr["g_final"].broadcast_to([128, D]), writes=['gfb'], sname='bc')
        def fA1(t):
            b = t % 6
            xk = ('xf', b)
            S.dma('sp', out=xf[b], in_=dr["x3"][t * 128:(t + 1) * 128, :], writes=[xk], sname=f'xld{b}')
            S.op('act', 'activation', reads=[xk], writes=['junk3', ('ssq', t)], out=junk3, in_=xf[b], func=AF.Square,
                 accum_out=ssq[:, t:t + 1])
            S.op('dve', 'tensor_scalar', reads=[('ssq', t)], writes=[('rstd', t)], out=rstd[:, t:t + 1], in0=ssq[:, t:t + 1],
                 scalar1=1.0 / D, scalar2=EPS, op0=ALU.mult, op1=ALU.add)

        def fA2(t):
            S.op('act', 'activation', reads=[('rstd', t)], writes=[('rstd', t)], out=rstd[:, t:t + 1], in_=rstd[:, t:t + 1], func=AF.Sqrt)
            S.op('dve', 'reciprocal', reads=[('rstd', t)], writes=[('rstd', t)], out=rstd[:, t:t + 1], in_=rstd[:, t:t + 1])

        def fB(t):
            b = t % 6
            xk = ('xf', b)
            S.op('dve', 'scalar_tensor_tensor', reads=[xk, ('rstd', t), 'gfb'], writes=[xk], out=xf[b], in0=xf[b],
                 scalar=rstd[:, t:t + 1], in1=gfb, op0=ALU.mult, op1=ALU.mult)
            S.dma('pool', out=dr["y"][t * 128:(t + 1) * 128, :], in_=xf[b], reads=[xk], writes=[('yout', t)], sname=f'pst{b}')
        for i in range(NT + 2):
            if i < NT:
                fA1(i)
            if 0 <= i - 1 < NT:
                fA2(i - 1)
            if 0 <= i - 2 < NT:
                fB(i - 2)
        finish()
    return nc


_W_NAMES = ["w_ada", "b_ada", "g_ffn1", "w_ffn1_in", "w_ffn1_out", "g_mix", "w_in", "b_in", "sinks", "g_out_a", "g_out_b",
            "w_out", "b_out", "g_ffn2", "w_ffn2_in", "w_ffn2_out"]


def make_in_maps(inputs):
    shared = {}
    for n in _W_NAMES:
        a = np.ascontiguousarray(np.asarray(inputs[n], dtype=np.float32))
        shared[n] = a.reshape(a.shape[-2], a.shape[-1]) if a.ndim == 3 else a.reshape(1, -1)
    shared["g_final"] = np.ascontiguousarray(np.asarray(inputs["g_final"], dtype=np.float32)).reshape(1, -1)
    x = np.asarray(inputs["x"], dtype=np.float32)
    c = np.asarray(inputs["c"], dtype=np.float32)
    pos = np.asarray(inputs["positions"], dtype=np.int32)
    maps = []
    for b in range(8):
        m = dict(shared)
        m["x"] = np.ascontiguousarray(x[b])
        m["c"] = np.ascontiguousarray(c[b:b + 1])
        m["positions"] = np.ascontiguousarray(pos[b:b + 1])
        maps.append(m)
    return maps


def kernel(**inputs):
    nc = build_program()
    in_maps = make_in_maps(inputs)
    res = run_bass_kernel_spmd(nc, in_maps, core_ids=list(range(8)))
    return np.stack([np.asarray(r["y"], dtype=np.float32) for r in res.results], axis=0)
```

```python
import contextlib
import math
import numpy as np
import concourse.bass as bass
import concourse.mybir as mybir
from concourse.bass_utils import run_bass_kernel_spmd

F32 = mybir.dt.float32
BF16 = mybir.dt.bfloat16
I32 = mybir.dt.int32
ALU = mybir.AluOpType
AF = mybir.ActivationFunctionType

D = 2048
T = 2048
DFF = 5632
INW = 4352
EPS = 1e-5
NT = 16
PI = math.pi
DT_SIZE = {F32: 4, BF16: 2, I32: 4}

STOP_AFTER = None


class Sched:
    ENG = ('pe', 'act', 'dve', 'pool', 'sp')
    ATTR = {'pe': 'tensor', 'act': 'scalar', 'dve': 'vector', 'pool': 'gpsimd', 'sp': 'sync'}

    def __init__(self, nc, sems):
        self.nc = nc
        self.free = list(sems)
        self.prog = {e: [] for e in self.ENG}
        self.esem = {e: self.free.pop() for e in self.ENG}
        self.ecnt = {e: 0 for e in self.ENG}
        self.dsem = {}
        self.waited = {e: {} for e in self.ENG}
        self.lastw = {}
        self.readers = {}

    def _sem(self, sk):
        return self.esem[sk[1]] if sk[0] == 'e' else self.dsem[sk[1]][0]

    def _need(self, eng, reads, writes):
        need = {}

        def add(ev):
            sk, v = ev
            if sk[0] == 'd':
                v = self.dsem[sk[1]][1]
            if v > need.get(sk, 0):
                need[sk] = v
        for k in reads:
            if k in self.lastw:
                add(self.lastw[k])
        for k in writes:
            if k in self.lastw:
                add(self.lastw[k])
            for sk, v in self.readers.get(k, {}).items():
                add((sk, v))
        out = []
        for sk, v in need.items():
            if sk == ('e', 'pe') and eng == 'pe':
                continue
            if self.waited[eng].get(sk, 0) >= v:
                continue
            self.waited[eng][sk] = v
            out.append((self._sem(sk), v))
        return out

    def _commit(self, ev, reads, writes):
        for k in writes:
            self.lastw[k] = ev
            self.readers[k] = {}
        for k in reads:
            r = self.readers.setdefault(k, {})
            if ev[1] > r.get(ev[0], 0):
                r[ev[0]] = ev[1]

    def op(self, eng, meth, reads=(), writes=(), inc=True, **kw):
        waits = self._need(eng, reads, writes)
        if inc:
            self.ecnt[eng] += 1
            ev = (('e', eng), self.ecnt[eng])
        else:
            ev = (('e', eng), self.ecnt[eng] + 1)
        self._commit(ev, reads, writes)
        sem = self.esem[eng]

        def run(e):
            for s, v in waits:
                e.wait_ge(s, v)
            ins = getattr(e, meth)(**kw)
            if inc:
                ins.then_inc(sem, 1)
        self.prog[eng].append(run)

    def dma(self, q, out, in_, reads=(), writes=(), sname='misc'):
        if sname not in self.dsem:
            self.dsem[sname] = [self.free.pop(), 0]
        waits = self._need(q, reads, writes)
        ds = self.dsem[sname]
        ds[1] += 16
        ev = (('d', sname), ds[1])
        self._commit(ev, reads, writes)
        sem = ds[0]

        def run(e):
            for s, v in waits:
                e.wait_ge(s, v)
            e.dma_start(out=out, in_=in_).then_inc(sem, 16)
        self.prog[q].append(run)

    def barrier(self, engines=None):
        evs = [(('e', e), self.ecnt[e]) for e in self.ENG if self.ecnt[e] > 0]
        evs += [(('d', n), c[1]) for n, c in self.dsem.items()]
        for eng in (engines or self.ENG):
            waits = []
            for sk, v in evs:
                if sk == ('e', eng):
                    continue
                if self.waited[eng].get(sk, 0) >= v:
                    continue
                self.waited[eng][sk] = v
                waits.append((self._sem(sk), v))

            def run(e, waits=waits):
                for s, v in waits:
                    e.wait_ge(s, v)
            self.prog[eng].append(run)
        if engines is None:
            self.lastw = {}
            self.readers = {}

    def emit(self):
        with self.nc.Block() as block:
            for eng in self.ENG:
                prog = self.prog[eng]

                def body(e, prog=prog):
                    for run in prog:
                        run(e)
                getattr(block, self.ATTR[eng])(body)


class Arena:
    def __init__(self, ap_f32, nwords):
        self.t = ap_f32
        self.n = nwords
        self.off = 0

    def mark(self):
        return self.off

    def release(self, m):
        self.off = m

    def seek_bytes(self, b):
        assert b % 32 == 0
        self.off = b // 4

    def alloc(self, free_shape, dt):
        nel = 1
        for s in free_shape:
            nel *= s
        nbytes = nel * DT_SIZE[dt]
        nw = (nbytes + 31) // 32 * 8
        assert self.off + nw <= self.n, f"arena overflow: {self.off + nw} > {self.n}"
        ap = self.t[:, self.off:self.off + nw]
        self.off += nw
        if dt != F32:
            ap = ap.bitcast(dt)
        ap = ap[:, 0:nel]
        if len(free_shape) == 2:
            ap = ap.rearrange("p (a b) -> p a b", b=free_shape[1])
        elif len(free_shape) == 3:
            ap = ap.rearrange("p (a b c) -> p a b c", b=free_shape[1], c=free_shape[2])
        return ap


def build_program():
    nc = bass.Bass("TRN2", target_bir_lowering=False)
    dr = {}

    def din(name, shape, dt=F32):
        dr[name] = nc.dram_tensor(name, shape, dt, kind="ExternalInput").ap()

    din("x", [T, D]); din("c", [1, D]); din("positions", [1, T], I32)
    din("w_ada", [D, 9 * D]); din("b_ada", [1, 9 * D])
    din("g_ffn1", [1, D]); din("w_ffn1_in", [D, 2 * DFF]); din("w_ffn1_out", [DFF, D])
    din("g_mix", [1, D]); din("w_in", [D, INW]); din("b_in", [1, INW]); din("sinks", [1, 16])
    din("g_out_a", [1, 1024]); din("g_out_b", [1, 1024]); din("w_out", [D, D]); din("b_out", [1, D])
    din("g_ffn2", [1, D]); din("w_ffn2_in", [D, 2 * DFF]); din("w_ffn2_out", [DFF, D]); din("g_final", [1, D])
    dr["y"] = nc.dram_tensor("y", [T, D], F32, kind="ExternalOutput").ap()
    dbg = STOP_AFTER is not None
    skind = "ExternalOutput" if dbg else "Internal"
    dr["x1"] = nc.dram_tensor("x1", [T, D], F32, kind=skind).ap()
    dr["x2"] = nc.dram_tensor("x2", [T, D], F32, kind=skind).ap()
    dr["x3"] = nc.dram_tensor("x3", [T, D], F32, kind=skind).ap()
    dr["gbc"] = nc.dram_tensor("gbc", [3, 128, D], F32, kind=skind).ap()
    dr["qkT"] = nc.dram_tensor("qkT", [26, 128, T], BF16, kind=skind).ap()
    if dbg:
        dr["dbg"] = nc.dram_tensor("dbg", [128, 2048], F32, kind="ExternalOutput").ap()

    ARENA_WORDS = 51 * 1024
    with contextlib.ExitStack() as st:
        arena_t = st.enter_context(nc.sbuf_tensor("arena", [128, ARENA_WORDS], F32))
        iar = st.enter_context(nc.sbuf_tensor("iarena", [128, 768], I32))
        ps = [st.enter_context(nc.psum_tensor(f"ps{i}", [128, 512], F32)) for i in range(8)]
        sems = [st.enter_context(nc.semaphore(f"s{i}")) for i in range(96)]
        S = Sched(nc, sems)
        A = Arena(arena_t[:, :], ARENA_WORDS)

        def PS(i):
            return ps[i][:, :]

        ident = A.alloc((128,), F32)
        ones = A.alloc((128,), F32)
        condT = A.alloc((16,), F32)
        modT = A.alloc((9, 16), F32)
        gscT = A.alloc((3, 16), F32)
        ssq = A.alloc((16,), F32)
        rstd = A.alloc((16,), F32)
        ssq2 = A.alloc((32,), F32)
        rstd2 = A.alloc((32,), F32)
        rl = A.alloc((8,), F32)
        esink = A.alloc((16,), F32)
        cosT = A.alloc((16, 8), F32)
        sinT = A.alloc((16, 8), F32)
        st16 = A.alloc((128,), F32)

        S.op('pool', 'memset', writes=['ones'], ap=ones, constant=1.0)
        S.op('pool', 'memset', writes=['ident'], ap=ident, constant=1.0)
        S.op('pool', 'affine_select', reads=['ident'], writes=['ident'], out=ident, in_=ident,
             pattern=[[-1, 128]], compare_op=ALU.is_equal, fill=0.0, base=0, channel_multiplier=1)

        def vec_to_T(src_row, dst, extra=None):
            S.dma('sp', out=st16[0:16, :], in_=src_row.rearrange("o (a p) -> (o a) p", p=128), writes=['st16'], sname='st16')
            S.op('pe', 'transpose', reads=['st16', 'ident'], writes=[('ps', 7)], out=ps[7][:, 0:16], in_=st16[0:16, :],
                 identity=ident[0:16, 0:16])
            extra(ps[7][:, 0:16])

        m0 = A.mark()
        vec_to_T(dr["c"], condT, lambda p: S.op('act', 'activation', reads=[('ps', 7)], writes=['condT'],
                                                out=condT, in_=p, func=AF.Silu))
        wv = dr["w_ada"].rearrange("(kc p) n -> p kc n", p=128)
        wcnt = [0]

        def m_block_gen(blk, wt, nk, a, ak, gbp, pbanks):
            S.op('pool', 'memset', writes=[ak], ap=a, constant=0.0)
            S.dma('sp', out=a[0:1, :], in_=dr["b_ada"][0:1, blk * 2048:(blk + 1) * 2048], writes=[ak], sname='bada')
            nst = 16 // nk

            def load(st):
                sl = wcnt[0] % len(wt)
                wcnt[0] += 1
                S.dma('sp', out=wt[sl][:, 0:nk, :], in_=wv[:, st * nk:(st + 1) * nk, blk * 2048:(blk + 1) * 2048], writes=[('wt', sl)],
                      sname=f'wt{sl}')
                return sl
            slots = {0: load(0)}
            for st in range(nst):
                if st + 1 < nst:
                    slots[st + 1] = load(st + 1)
                sl = slots[st]
                for j in range(nk):
                    kc = st * nk + j
                    S.op('dve', 'scalar_tensor_tensor', reads=[('wt', sl), 'condT', ak], writes=[ak], out=a, in0=wt[sl][:, j, :],
                         scalar=condT[:, kc:kc + 1], in1=a, op0=ALU.mult, op1=ALU.add)
                yield
            if blk % 3 != 2:
                pb = pbanks[0]
                for cc in range(16):
                    S.op('pe', 'matmul', reads=[ak, 'ones'], writes=[('ps', pb)], inc=(cc == 15), out=ps[pb][:, cc:cc + 1],
                         lhsT=a[:, cc * 128:(cc + 1) * 128], rhs=ones[:, 0:1], start=True, stop=True)
                S.op('dve', 'tensor_copy', reads=[('ps', pb)], writes=[('modT', blk)], out=modT[:, blk, :], in_=ps[pb][:, 0:16])
            else:
                sub = blk // 3
                for s4 in range(4):
                    pb = pbanks[s4 % 2]
                    S.op('pe', 'matmul', reads=[ak, 'ones'], writes=[('ps', pb)], out=PS(pb), lhsT=ones,
                         rhs=a[:, s4 * 512:(s4 + 1) * 512], start=True, stop=True)
                    S.op('act', 'activation', reads=[('ps', pb)], writes=[('gbp', s4 % 2)], out=gbp[s4 % 2], in_=PS(pb),
                         func=AF.Copy, scale=(1.0 if sub == 1 else 0.5))
                    S.dma('sp', out=dr["gbc"][sub][:, s4 * 512:(s4 + 1) * 512], in_=gbp[s4 % 2], reads=[('gbp', s4 % 2)],
                          writes=[('gbc', sub)], sname=f'gbst{s4 % 2}')
            yield

        def gsc_for(sub):
            gname = ("g_ffn1", "g_mix", "g_ffn2")[sub]
            vec_to_T(dr[gname], None, lambda p: S.op(
                'dve', 'scalar_tensor_tensor', reads=[('ps', 7), ('modT', 3 * sub + 1)], writes=[('gscT', sub)], out=gscT[:, sub, :],
                in0=modT[:, 3 * sub + 1, :], scalar=1.0, in1=p, op0=ALU.add, op1=ALU.mult))

        wtB = [A.alloc((4, 2048), F32) for _ in range(2)]
        accB = [A.alloc((2048,), F32) for _ in range(2)]
        gbpB = [A.alloc((512,), F32) for _ in range(2)]
        for blk in range(2):
            for _ in m_block_gen(blk, wtB, 4, accB[blk % 2], ('acc', blk % 2), gbpB, (blk % 2, 2 + blk % 2)):
                pass
        gsc_for(0)

        def m_chain(wt, a, gbp):
            for blk in range(2, 9):
                yield from m_block_gen(blk, wt, 1, a, ('accS', 0), gbp, (6, 7))
                if blk == 4:
                    gsc_for(1)
                if blk == 7:
                    gsc_for(2)
        S.barrier()
        A.release(m0)

        hT = A.alloc((16, 2048), BF16)
        H_END = A.mark()
        TOPB = ARENA_WORDS * 4
        V_BYTES = 16 * 18 * 65 * 2

        def rstd_ops(src, dst, kin, kout, inv_n):
            S.op('dve', 'tensor_scalar', reads=[kin], writes=[kout], out=dst, in0=src, scalar1=inv_n, scalar2=EPS,
                 op0=ALU.mult, op1=ALU.add)
            S.op('act', 'activation', reads=[kout], writes=[kout], out=dst, in_=dst, func=AF.Sqrt)
            S.op('dve', 'reciprocal', reads=[kout], writes=[kout], out=dst, in_=dst)

        NXT = 4
        NORM_ALIAS = [('xt', i) for i in range(NXT)] + ['junk']

        def norm_transpose(xin, sub, dst, dstkey):
            A.off = H_END
            xt = [A.alloc((2048,), F32) for _ in range(NXT)]
            junk = A.alloc((2048,), BF16)
            def stA1(t):
                b = t % NXT
                xk = ('xt', b)
                S.dma('sp', out=xt[b], in_=xin[t * 128:(t + 1) * 128, :], writes=[xk], sname=f'xt{b}')
                S.op('act', 'activation', reads=[xk], writes=['junk', ('ssq', t)], out=junk, in_=xt[b], func=AF.Square,
                     accum_out=ssq[:, t:t + 1])
                S.op('dve', 'tensor_scalar', reads=[('ssq', t)], writes=[('rstd', t)], out=rstd[:, t:t + 1], in0=ssq[:, t:t + 1],
                     scalar1=1.0 / D, scalar2=EPS, op0=ALU.mult, op1=ALU.add)

            def stA2(t):
                b = t % NXT
                xk = ('xt', b)
                S.op('act', 'activation', reads=[('rstd', t)], writes=[('rstd', t)], out=rstd[:, t:t + 1], in_=rstd[:, t:t + 1],
                     func=AF.Sqrt)
                S.op('dve', 'reciprocal', reads=[('rstd', t)], writes=[('rstd', t)], out=rstd[:, t:t + 1], in_=rstd[:, t:t + 1])
                S.op('dve', 'tensor_scalar', reads=[xk, ('rstd', t)], writes=[xk], out=xt[b], in0=xt[b],
                     scalar1=rstd[:, t:t + 1], scalar2=None, op0=ALU.mult)

            def stB(t):
                b = t % NXT
                xk = ('xt', b)
                for g4 in range(4):
                    pb = 4 + g4
                    for i in range(4):
                        kc = g4 * 4 + i
                        S.op('pe', 'transpose', reads=[xk, 'ident'], writes=[('ps', pb)], inc=(i == 3),
                             out=ps[pb][:, i * 128:(i + 1) * 128], in_=xt[b][:, kc * 128:(kc + 1) * 128], identity=ident)
                    for i in range(4):
                        kc = g4 * 4 + i
                        o = dst[:, kc, t * 128:(t + 1) * 128]
                        src = ps[pb][:, i * 128:(i + 1) * 128]
                        if g4 % 2 == 0:
                            S.op('act', 'activation', reads=[('ps', pb), ('gscT', sub), ('modT', 3 * sub)], writes=[(dstkey, t, kc)], out=o, in_=src,
                                 func=AF.Identity, scale=gscT[:, sub, kc:kc + 1], bias=modT[:, 3 * sub, kc:kc + 1])
                        else:
                            S.op('dve', 'tensor_scalar', reads=[('ps', pb), ('gscT', sub), ('modT', 3 * sub)], writes=[(dstkey, t, kc)], out=o, in0=src,
                                 scalar1=gscT[:, sub, kc:kc + 1], scalar2=modT[:, 3 * sub, kc:kc + 1], op0=ALU.mult, op1=ALU.add)
            for i in range(NT + 2):
                if i < NT:
                    stA1(i)
                if 0 <= i - 1 < NT:
                    stA2(i - 1)
                if 0 <= i - 2 < NT:
                    stB(i - 2)
            A.off = H_END

        def ffn(xin, xout, xname, w_in_d, w_out_d, sub, passes=((0, 12), (12, 24), (24, 34), (34, 44)), with_m=False):
            norm_transpose(xin, sub, hT, 'hT')
            passes = list(passes)
            maxn = max(b_ - a_ for a_, b_ in passes)
            wd = [A.alloc((maxn, 512), BF16) for _ in range(2)]
            wg = [A.alloc((16, 256), BF16) for _ in range(2)]
            wu = [A.alloc((16, 256), BF16) for _ in range(2)]
            actT = A.alloc((maxn, 2048), BF16)
            sg = [A.alloc((512,), F32) for _ in range(2)]
            gbt = A.alloc((2048,), F32)
            xp = [A.alloc((512,), F32) for _ in range(6)]
            tmp = [A.alloc((512,), F32) for _ in range(2)]
            mch = None
            if with_m:
                wtS = [A.alloc((1, 2048), F32) for _ in range(2)]
                accS = A.alloc((2048,), F32)
                gbpS = [A.alloc((512,), F32) for _ in range(2)]
                mch = m_chain(wtS, accS, gbpS)
            wiv = w_in_d.rearrange("(kc p) n -> p kc n", p=128)
            wov = w_out_d.rearrange("(j p) d -> p j d", p=128)
            pairs = []
            for pi, (a, b) in enumerate(passes):
                for pr in range(a // 2, b // 2):
                    pairs.append((pi, pr))

            def load_pair(idx):
                _, pr = pairs[idx]
                sl = idx % 2
                S.dma('pool', out=wg[sl], in_=wiv[:, :, pr * 256:(pr + 1) * 256], writes=[('wg', sl)] + NORM_ALIAS, sname=f'wg{sl}')
                S.dma('pool', out=wu[sl], in_=wiv[:, :, DFF + pr * 256:DFF + (pr + 1) * 256], writes=[('wu', sl)] + NORM_ALIAS,
                      sname=f'wu{sl}')

            wd_count = [0]

            def load_wd(pi, s):
                a, b = passes[pi]
                sl = (pi * 4 + s) % 2
                S.dma('pool', out=wd[sl][:, 0:b - a, :], in_=wov[:, a:b, s * 512:(s + 1) * 512], writes=[('wd', sl)] + NORM_ALIAS,
                      sname=f'wd{sl}')

            unit = 0
            ep = 0
            load_pair(0)
            for idx, (pi, pr) in enumerate(pairs):
                a, b = passes[pi]
                if idx + 1 < len(pairs):
                    load_pair(idx + 1)
                sl = idx % 2
                for ci in range(2):
                    jj = 2 * pr + ci - a
                    for ts in range(4):
                        slot = unit % 2
                        unit += 1
                        gbk, ubk = 2 * slot, 2 * slot + 1
                        for kc in range(16):
                            S.op('pe', 'matmul', reads=[('wg', sl)] + [('hT', 4 * ts + q, kc) for q in range(4)], writes=[('ps', gbk)], inc=(kc == 15), out=PS(gbk),
                                 lhsT=wg[sl][:, kc, ci * 128:(ci + 1) * 128], rhs=hT[:, kc, ts * 512:(ts + 1) * 512],
                                 start=(kc == 0), stop=(kc == 15))
                        for kc in range(16):
                            S.op('pe', 'matmul', reads=[('wu', sl)] + [('hT', 4 * ts + q, kc) for q in range(4)], writes=[('ps', ubk)], inc=(kc == 15), out=PS(ubk),
                                 lhsT=wu[sl][:, kc, ci * 128:(ci + 1) * 128], rhs=hT[:, kc, ts * 512:(ts + 1) * 512],
                                 start=(kc == 0), stop=(kc == 15))
                        S.op('act', 'activation', reads=[('ps', gbk)], writes=[('sg', slot)], out=sg[slot], in_=PS(gbk), func=AF.Silu)
                        S.op('dve', 'tensor_tensor', reads=[('sg', slot), ('ps', ubk)], writes=[('actT', ts, jj)],
                             out=actT[:, jj, ts * 512:(ts + 1) * 512], in0=sg[slot], in1=PS(ubk), op=ALU.mult)
                        if mch is not None:
                            next(mch, None)
                first_in_pass = (pr == a // 2)
                second_in_pass = (pr == a // 2 + 1)
                if first_in_pass:
                    load_wd(pi, 0)
                if second_in_pass:
                    load_wd(pi, 1)
                if pr == b // 2 - 1:
                    n = b - a
                    if pi == 0:
                        S.dma('sp', out=gbt, in_=dr["gbc"][sub], reads=[('gbc', sub)], writes=['gbt'], sname='gbt')
                    xsrc = xin if pi == 0 else xout
                    NXP = len(xp)
                    PRE = 3
                    dunits = [(s_, t_) for s_ in range(4) for t_ in range(NT)]

                    def xload(k):
                        s_, t_ = dunits[k]
                        xs_ = k % NXP
                        S.dma('sp', out=xp[xs_], in_=xsrc[t_ * 128:(t_ + 1) * 128, s_ * 512:(s_ + 1) * 512],
                              reads=[(xname, t_, s_)] if pi > 0 else [], writes=[('xp', xs_)], sname=f'xld{xs_}')
                    for k in range(PRE):
                        xload(k)
                    for k, (s, t) in enumerate(dunits):
                        if k + PRE < len(dunits):
                            xload(k + PRE)
                        wsl = (pi * 4 + s) % 2
                        ob = 4 + ep % 2
                        xs = k % NXP
                        ts_ = ep % 2
                        ep += 1
                        xkey = (xname, t, s)
                        for jj in range(n):
                            S.op('pe', 'matmul', reads=[('actT', t // 4, jj), ('wd', wsl)], writes=[('ps', ob)], inc=(jj == n - 1),
                                 out=PS(ob), lhsT=actT[:, jj, t * 128:(t + 1) * 128], rhs=wd[wsl][:, jj, :],
                                 start=(jj == 0), stop=(jj == n - 1))
                        S.op('dve', 'tensor_tensor', reads=[('ps', ob), 'gbt'], writes=[('tmp', ts_)], out=tmp[ts_], in0=PS(ob),
                             in1=gbt[:, s * 512:(s + 1) * 512], op=ALU.mult)
                        S.op('dve', 'tensor_tensor', reads=[('tmp', ts_), ('xp', xs)], writes=[('xp', xs)], out=xp[xs],
                             in0=tmp[ts_], in1=xp[xs], op=ALU.add)
                        S.dma('sp', out=xout[t * 128:(t + 1) * 128, s * 512:(s + 1) * 512], in_=xp[xs], reads=[('xp', xs)],
                              writes=[xkey], sname=f'xst{xs}')
                        if t == NT - 1 and s + 2 < 4:
                            load_wd(pi, s + 2)
            if mch is not None:
                for _ in mch:
                    pass
            S.barrier()
            A.off = H_END

        ffn(dr["x"], dr["x1"], 'x1', dr["w_ffn1_in"], dr["w_ffn1_out"], 0,
            passes=((0, 8), (8, 16), (16, 24), (24, 32), (32, 40), (40, 44)), with_m=True)

        def finish():
            S.barrier(engines=['sp'])
            S.emit()

        if STOP_AFTER in ('mod', 'ffn1'):
            finish()
            return nc

        A.off = H_END
        wis = [A.alloc((16, 512), BF16) for _ in range(2)]
        norm_transpose(dr["x1"], 1, hT, 'hT')
        A.off = H_END
        wis = [A.alloc((16, 512), BF16) for _ in range(2)]
        A.seek_bytes(TOPB - V_BYTES)
        V_all = A.alloc((16, 18, 65), BF16)
        A.off = H_END + 2 * 16 * 512 * 2 // 4
        S.op('dve', 'memset', writes=['V_all'], ap=V_all, constant=1.0)
        b_in_bc = A.alloc((INW,), F32)
        S.dma('sp', out=b_in_bc, in_=dr["b_in"].broadcast_to([128, INW]), writes=['b_in_bc'] + NORM_ALIAS, sname='bc')
        posi = iar[:, 0:128]
        posf = A.alloc((128,), F32)
        posT = A.alloc((16,), F32)
        invf = A.alloc((8,), F32)
        ang = A.alloc((16, 8), F32)
        S.dma('sp', out=posi[0:16, :], in_=dr["positions"].rearrange("o (a p) -> (o a) p", p=128), writes=['posi'], sname='bc')
        S.op('dve', 'tensor_copy', reads=['posi'], writes=['posf'], out=posf[0:16, :], in_=posi[0:16, :])
        S.op('pe', 'transpose', reads=['posf', 'ident'], writes=[('ps', 7)], out=ps[7][:, 0:16], in_=posf[0:16, :],
             identity=ident[0:16, 0:16])
        S.op('dve', 'tensor_copy', reads=[('ps', 7)], writes=['posT'], out=posT, in_=ps[7][:, 0:16])
        inv_freq = (np.float32(500000.0) ** (-(np.arange(0, 16, 2, dtype=np.float32) / np.float32(16)))).astype(np.float32)
        for f in range(8):
            S.op('dve', 'memset', writes=['invf'], ap=invf[:, f:f + 1], constant=float(inv_freq[f]))
        S.op('dve', 'tensor_tensor', reads=['posT', 'invf'], writes=['ang'], out=ang,
             in0=posT.unsqueeze(2).broadcast_to([128, 16, 8]), in1=invf.unsqueeze(1).broadcast_to([128, 16, 8]), op=ALU.mult)
        kq = A.alloc((16, 8), F32)
        kqi = iar[:, 128:256].rearrange("p (a b) -> p a b", b=8)
        C1 = 6.28125
        C2 = 2.0 * PI - C1
        for dst, shift in ((sinT, 0.0), (cosT, 0.5 * PI)):
            S.op('dve', 'tensor_scalar', reads=['ang'], writes=['rtab'], out=dst, in0=ang, scalar1=shift, scalar2=None, op0=ALU.add)
            S.op('dve', 'tensor_scalar', reads=['rtab'], writes=['kq'], out=kq, in0=dst, scalar1=1.0 / (2.0 * PI), scalar2=None,
                 op0=ALU.mult)
            S.op('dve', 'tensor_copy', reads=['kq'], writes=['kqi'], out=kqi, in_=kq)
            S.op('dve', 'tensor_copy', reads=['kqi'], writes=['kq'], out=kq, in_=kqi)
            S.op('dve', 'scalar_tensor_tensor', reads=['kq', 'rtab'], writes=['rtab'], out=dst, in0=kq, scalar=-C1, in1=dst,
                 op0=ALU.mult, op1=ALU.add)
            S.op('dve', 'scalar_tensor_tensor', reads=['kq', 'rtab'], writes=['rtab'], out=dst, in0=kq, scalar=-C2, in1=dst,
                 op0=ALU.mult, op1=ALU.add)
            S.op('dve', 'tensor_scalar', reads=['rtab'], writes=['rtab'], out=dst, in0=dst, scalar1=-PI, scalar2=PI,
                 op0=ALU.max, op1=ALU.min)
            S.op('act', 'activation', reads=['rtab'], writes=['rtab'], out=dst, in_=dst, func=AF.Sin)

        slabs = [(0, 512, 'qk', 0), (512, 512, 'qk', 4), (1280, 512, 'qk', 8), (1792, 512, 'qk', 12),
                 (2304, 512, 'qk', 16), (2816, 512, 'qk', 20), (1024, 256, 'kava', 24),
                 (3328, 512, 'v', 2), (3840, 512, 'v', 10)]
        qst = [A.alloc((4, 2048), BF16) for _ in range(2)]
        pj = [A.alloc((512,), F32) for _ in range(4)]
        rt = [A.alloc((8, 8), F32) for _ in range(4)]
        kd = [A.alloc((2, 128), F32) for _ in range(2)]
        assert A.off * 4 <= TOPB - V_BYTES
        wiv = dr["w_in"].rearrange("(kc p) n -> p kc n", p=128)

        def load_slab(si):
            c0, w, _, _ = slabs[si]
            S.dma('pool', out=wis[si % 2][:, :, 0:w], in_=wiv[:, :, c0:c0 + w], writes=[('wis', si % 2)] + NORM_ALIAS,
                  sname=f'wis{si % 2}')

        def rope(pjt, nh, t, pk):
            v = pjt[:, 0:nh * 64].rearrange("p (h d) -> p h d", d=64)
            x1, x2 = v[:, :, 0:8], v[:, :, 8:16]
            cb = cosT[:, t, :].unsqueeze(1).broadcast_to([128, nh, 8])
            sb = sinT[:, t, :].unsqueeze(1).broadcast_to([128, nh, 8])
            r = [q[:, 0:nh, :] for q in rt]
            S.op('dve', 'tensor_tensor', reads=[pk, 'rtab'], writes=['rt0'], out=r[0], in0=x1, in1=cb, op=ALU.mult)
            S.op('dve', 'tensor_tensor', reads=[pk, 'rtab'], writes=['rt1'], out=r[1], in0=x2, in1=sb, op=ALU.mult)
            S.op('dve', 'tensor_tensor', reads=[pk, 'rtab'], writes=['rt2'], out=r[2], in0=x2, in1=cb, op=ALU.mult)
            S.op('dve', 'tensor_tensor', reads=[pk, 'rtab'], writes=['rt3'], out=r[3], in0=x1, in1=sb, op=ALU.mult)
            S.op('dve', 'tensor_tensor', reads=['rt0', 'rt1'], writes=[pk], out=x1, in0=r[0], in1=r[1], op=ALU.subtract)
            S.op('dve', 'tensor_tensor', reads=['rt2', 'rt3'], writes=[pk], out=x2, in0=r[2], in1=r[3], op=ALU.add)

        load_slab(0)
        u = 0
        trc = [0]
        pend = [None]
        for si, (c0, w, kind, aux) in enumerate(slabs):
            if si + 1 < len(slabs):
                load_slab(si + 1)
            wsl = si % 2
            qsl = si % 2
            for t in range(NT):
                pb = (0, 1, 4, 5)[u % 4]
                pjt = pj[u % 4]
                pk = ('pj', u % 4)
                u += 1
                for kc in range(16):
                    S.op('pe', 'matmul', reads=[('wis', wsl), ('hT', t, kc)], writes=[('ps', pb)], inc=(kc == 15), out=ps[pb][:, 0:w],
                         lhsT=hT[:, kc, t * 128:(t + 1) * 128], rhs=wis[wsl][:, kc, 0:w], start=(kc == 0), stop=(kc == 15))
                S.op('dve', 'tensor_tensor', reads=[('ps', pb), 'b_in_bc'], writes=[pk], out=pjt[:, 0:w], in0=ps[pb][:, 0:w],
                     in1=b_in_bc[:, c0:c0 + w], op=ALU.add)
                if pend[0] is not None:
                    pend[0]()
                    pend[0] = None
                if kind == 'qk':
                    rope(pjt, 8, t, pk)

                    def post(pjt=pjt, pk=pk, t=t, qsl=qsl):
                        tb = (2, 3, 6, 7)[trc[0] % 4]
                        trc[0] += 1
                        for i in range(4):
                            S.op('pe', 'transpose', reads=[pk, 'ident'], writes=[('ps', tb)], inc=(i == 3),
                                 out=ps[tb][:, i * 128:(i + 1) * 128], in_=pjt[:, i * 128:(i + 1) * 128], identity=ident)
                        S.op('act', 'activation', reads=[('ps', tb)], writes=[('qst', qsl)], out=qst[qsl][:, :, t * 128:(t + 1) * 128],
                             in_=ps[tb][:, :].rearrange("p (a b) -> p a b", b=128), func=AF.Copy)
                    pend[0] = post
                elif kind == 'kava':
                    rope(pjt, 2, t, pk)
                    for hh in range(2):
                        for dd in range(2):
                            S.op('dve', 'tensor_copy', reads=[pk], writes=[('kd', t % 2)], out=kd[t % 2][:, hh, dd * 64:(dd + 1) * 64],
                                 in_=pjt[:, hh * 64:(hh + 1) * 64])
                    S.op('act', 'activation', reads=[pk], writes=['V_all'], out=V_all[:, t, 0:2, 0:64],
                         in_=pjt[:, 128:256].rearrange("p (h d) -> p h d", d=64), func=AF.Copy)

                    def post(t=t, qsl=qsl):
                        tb = (2, 3, 6, 7)[trc[0] % 4]
                        trc[0] += 1
                        for i in range(2):
                            S.op('pe', 'transpose', reads=[('kd', t % 2), 'ident'], writes=[('ps', tb)], inc=(i == 1),
                                 out=ps[tb][:, i * 128:(i + 1) * 128], in_=kd[t % 2][:, i, :], identity=ident)
                        S.op('act', 'activation', reads=[('ps', tb)], writes=[('qst', qsl)], out=qst[qsl][:, 0:2, t * 128:(t + 1) * 128],
                             in_=ps[tb][:, 0:256].rearrange("p (a b) -> p a b", b=128), func=AF.Copy)
                    pend[0] = post
                else:
                    S.op('act', 'activation', reads=[pk], writes=['V_all'], out=V_all[:, t, aux:aux + 8, 0:64],
                         in_=pjt[:, 0:512].rearrange("p (h d) -> p h d", d=64), func=AF.Copy)
            if pend[0] is not None and kind in ('qk', 'kava'):
                pend[0]()
                pend[0] = None
            if kind == 'qk':
                S.dma('sp', out=dr["qkT"][aux:aux + 4].rearrange("c p n -> p c n"), in_=qst[qsl], reads=[('qst', qsl)],
                      writes=[('qkT', aux // 4)], sname=f'qst{qsl}')
            elif kind == 'kava':
                S.dma('sp', out=dr["qkT"][24:26].rearrange("c p n -> p c n"), in_=qst[qsl][:, 0:2, :], reads=[('qst', qsl)],
                      writes=[('qkT', 6)], sname=f'qst{qsl}')
        if STOP_AFTER == 'proj':
            for t in range(NT):
                S.op('dve', 'tensor_copy', reads=['V_all'], writes=[('pj', 0)], out=pj[0],
                     in_=V_all[:, t, 0:8, 0:64].rearrange("p h d -> p (h d)"))
                S.dma('sp', out=dr["x2"][t * 128:(t + 1) * 128, 0:512], in_=pj[0], reads=[('pj', 0)], writes=[('x2o', t)], sname='out')
            finish()
            return nc
        S.barrier()

        Y_END = m0 + 16 * 2048
        A.off = Y_END
        maskB = A.alloc((2048,), BF16)
        maskA4 = A.alloc((512,), BF16)
        maskA2 = A.alloc((256,), BF16)
        qT = [A.alloc((2048,), BF16) for _ in range(2)]
        kT = [A.alloc((2048,), BF16) for _ in range(2)]
        PT = [A.alloc((512,), BF16) for _ in range(8)]
        sk16 = A.alloc((16,), F32)
        assert A.off * 4 <= TOPB - V_BYTES
        A.off = m0
        vf = A.alloc((2048,), F32)
        c1 = A.alloc((2048,), F32)
        c2 = A.alloc((2048,), F32)
        ji = iar[:, 256:768]
        t4 = A.alloc((2048,), F32)
        t16 = A.alloc((2048,), F32)
        A.off = m0
        y_all = A.alloc((16, 2048), F32)
        S.op('pool', 'iota', writes=['vf'], out=vf, pattern=[[1, 2048]], base=2048, channel_multiplier=-1,
             allow_small_or_imprecise_dtypes=True)
        S.op('dve', 'tensor_scalar', reads=['vf'], writes=['c1'], out=c1, in0=vf, scalar1=2048.0, scalar2=None, op0=ALU.is_ge)
        for blk, diag in ((0, False), (1, True), (2, False), (3, True)):
            if diag:
                S.op('dve', 'tensor_copy', reads=['c1'], writes=['maskA'], out=maskA4[:, blk * 128:(blk + 1) * 128], in_=c1[:, 0:128])
            else:
                S.op('dve', 'tensor_scalar', reads=['c1'], writes=['maskA'], out=maskA4[:, blk * 128:(blk + 1) * 128], in0=c1[:, 0:128],
                     scalar1=-1.0, scalar2=1.0, op0=ALU.mult, op1=ALU.add)
        for blk in range(2):
            S.op('dve', 'tensor_copy', reads=['c1'], writes=['maskA'], out=maskA2[:, blk * 128:(blk + 1) * 128], in_=c1[:, 0:128])
        S.op('dve', 'scalar_tensor_tensor', reads=['vf', 'c1'], writes=['c2'], out=c2, in0=vf, scalar=2048.0 + 128.0, in1=c1,
             op0=ALU.is_le, op1=ALU.mult)
        for ch in range(4):
            cs = slice(ch * 512, (ch + 1) * 512)
            for msk, dstt, dk in ((3, t4, 't4'), (15, t16, 't16')):
                S.op('dve', 'tensor_copy', reads=['vf'], writes=['ji'], out=ji, in_=vf[:, cs])
                S.op('dve', 'tensor_scalar', reads=['ji'], writes=['ji'], out=ji, in0=ji, scalar1=msk, scalar2=None, op0=ALU.bitwise_and)
                S.op('dve', 'tensor_scalar', reads=['ji'], writes=[dk], out=dstt[:, cs], in0=ji, scalar1=0.0, scalar2=None, op0=ALU.is_equal)
        S.op('dve', 'scalar_tensor_tensor', reads=['vf', 't4'], writes=['t4'], out=t4, in0=vf, scalar=2048.0 + 512.0, in1=t4,
             op0=ALU.is_le, op1=ALU.mult)
        S.op('dve', 'tensor_tensor', reads=['t4', 'c1'], writes=['t4'], out=t4, in0=t4, in1=c1, op=ALU.mult)
        S.op('dve', 'tensor_tensor', reads=['t4', 'c2'], writes=['c2'], out=c2, in0=t4, in1=c2, op=ALU.add)
        S.op('dve', 'tensor_tensor', reads=['t16', 'c1'], writes=['t16'], out=t16, in0=t16, in1=c1, op=ALU.mult)
        S.op('dve', 'tensor_tensor', reads=['t16', 'c2'], writes=['maskB'], out=maskB, in0=t16, in1=c2, op=ALU.add)
        S.dma('sp', out=sk16, in_=dr["sinks"].broadcast_to([128, 16]), writes=['sk16'], sname='bc')
        S.op('act', 'activation', reads=['sk16'], writes=['esink'], out=esink, in_=sk16, func=AF.Exp)
        S.barrier()

        heads = []
        for h in range(16):
            heads.append((h // 2, 64 * (h % 2), 24 + h // 8, h // 8, h * 64, True, h))
        for h in range(16):
            heads.append((8 + h // 2, 64 * (h % 2), 16 + h // 2, 2 + h, 1024 + h * 64, False, None))
        cur = {'q': [None, 0, 0], 'k': [None, 0, 0]}

        def ensure(kind, chunk, bufs):
            c = cur[kind]
            if c[0] == chunk:
                return c[1]
            sl = c[2] % 2
            c[0], c[1], c[2] = chunk, sl, c[2] + 1
            S.dma('sp', out=bufs[sl], in_=dr["qkT"][chunk], writes=[(kind + 'T', sl)], sname=f'{kind}T{sl}')
            return sl

        SB = (0, 1, 6, 7)
        NPT = len(PT)
        units = []
        ruse = [0]

        def new_rl(n):
            i = ruse[0] % 4
            ruse[0] += 1
            return rl[:, 2 * i:2 * i + n], ('rl', i)

        for kv in range(2):
            for ip in range(4):
                qc = 4 * kv + ip
                h0 = 8 * kv + 2 * ip
                for qt in range(NT):
                    kts = [qt - 1, qt] if qt > 0 else [qt]
                    nb = len(kts)
                    sgr, pvl = [], []
                    for hd in range(2):
                        sgr.append(dict(mms=[(64 * hd, kt, qt * 128, 128, bi * 128) for bi, kt in enumerate(kts)], ncols=nb * 128,
                                        pt_off=hd * nb * 128))
                    n = 2 * nb * 128
                    ab = 2 + len(units) % 4

                    def evacA(ab=ab, qt=qt, h0=h0):
                        for hd in range(2):
                            r, rk = new_rl(1)
                            S.op('dve', 'tensor_scalar', reads=[('ps', ab), 'esink'], writes=[rk], out=r,
                                 in0=ps[ab][:, hd * 128 + 64:hd * 128 + 65], scalar1=esink[:, h0 + hd:h0 + hd + 1], scalar2=None, op0=ALU.add)
                            S.op('dve', 'reciprocal', reads=[rk], writes=[rk], out=r, in_=r)
                            S.op('dve', 'tensor_scalar', reads=[('ps', ab), rk], writes=[('y', qt, h0 + hd)],
                                 out=y_all[:, qt, (h0 + hd) * 64:(h0 + hd + 1) * 64], in0=ps[ab][:, hd * 128:hd * 128 + 64],
                                 scalar1=r, scalar2=None, op0=ALU.mult)
                    for hd in range(2):
                        for bi, kt in enumerate(kts):
                            last = (hd == 1 and bi == nb - 1)
                            pvl.append(dict(off=(hd * nb + bi) * 128, kt=kt, vidx=kv, bank=ab, col0=hd * 128, start=(bi == 0),
                                            stop=(bi == nb - 1), evac=evacA if last else None))
                    units.append(dict(qc=qc, kc=24 + kv, sgr=sgr, n=n, mask=(maskA4 if nb == 2 else maskA2), mk='maskA', pv=pvl))
        for h in range(16):
            qc, base, kcb, vidx, ycol = 8 + h // 2, 64 * (h % 2), 16 + h // 2, 2 + h, 1024 + h * 64
            for qs in range(4):
                for kt in range(4 * qs + 4):
                    q0 = max(kt * 128, qs * 512)
                    n = (qs + 1) * 512 - q0
                    d0 = q0 - kt * 128
                    pvl = []
                    for qt in range(q0 // 128, (qs + 1) * 4):
                        ab = 2 + qt % 4

                        def evacB(ab=ab, qt=qt, ycol=ycol):
                            r, rk = new_rl(1)
                            S.op('dve', 'reciprocal', reads=[('ps', ab)], writes=[rk], out=r, in_=ps[ab][:, 64:65])
                            S.op('dve', 'tensor_scalar', reads=[('ps', ab), rk], writes=[('y', qt, ycol // 64)], out=y_all[:, qt, ycol:ycol + 64],
                                 in0=ps[ab][:, 0:64], scalar1=r, scalar2=None, op0=ALU.mult)
                        pvl.append(dict(off=qt * 128 - q0, kt=kt, vidx=vidx, bank=ab, col0=0, start=(kt == 0), stop=(kt == qt),
                                        evac=evacB if kt == qt else None))
                    units.append(dict(qc=qc, kc=kcb, sgr=[dict(mms=[(base, kt, q0, n, 0)], ncols=n, pt_off=0)], n=n,
                                      mask=maskB[:, d0:d0 + n], mk='maskB', pv=pvl))

        sbc = [0]

        def emit_S(i):
            U = units[i]
            qs_ = ensure('q', U['qc'], qT)
            ks_ = ensure('k', U['kc'], kT)
            Q, K = qT[qs_], kT[ks_]
            n = U['n']
            pt = i % NPT
            for G in U['sgr']:
                sb_ = SB[sbc[0] % 4]
                sbc[0] += 1
                for j, (base, kt, q0, w, oc) in enumerate(G['mms']):
                    S.op('pe', 'matmul', reads=[('qT', qs_), ('kT', ks_)], writes=[('ps', sb_)], inc=(j == len(G['mms']) - 1),
                         out=ps[sb_][:, oc:oc + w], lhsT=K[base:base + 64, kt * 128:(kt + 1) * 128], rhs=Q[base:base + 64, q0:q0 + w],
                         start=True, stop=True)
                S.op('act', 'activation', reads=[('ps', sb_)], writes=[('PT', pt)], out=PT[pt][:, G['pt_off']:G['pt_off'] + G['ncols']],
                     in_=ps[sb_][:, 0:G['ncols']], func=AF.Exp, scale=0.125)
            S.op('dve', 'tensor_tensor', reads=[('PT', pt), U['mk']], writes=[('PT', pt)], out=PT[pt][:, 0:n],
                 in0=PT[pt][:, 0:n], in1=U['mask'], op=ALU.mult)

        def emit_PV(i):
            U = units[i]
            pt = i % NPT
            for e in U['pv']:
                ab = e['bank']
                S.op('pe', 'matmul', reads=[('PT', pt), 'V_all'], writes=[('ps', ab)], out=ps[ab][:, e['col0']:e['col0'] + 65],
                     lhsT=PT[pt][:, e['off']:e['off'] + 128], rhs=V_all[:, e['kt'], e['vidx'], :], start=e['start'], stop=e['stop'])
                if e['evac'] is not None:
                    e['evac']()

        LOOK = 5
        GRP = 3
        assert NPT >= LOOK + GRP
        for i0 in range(0, len(units) + LOOK + GRP, GRP):
            for i in range(i0, i0 + GRP):
                if i < len(units):
                    emit_S(i)
            for i in range(i0, i0 + GRP):
                if 0 <= i - LOOK < len(units):
                    emit_PV(i - LOOK)
        if STOP_AFTER == 'attn':
            for t in range(NT):
                S.dma('sp', out=dr["x2"][t * 128:(t + 1) * 128, :], in_=y_all[:, t, :], reads=[('y', t, h_) for h_ in range(32)], writes=[('x2o', t)], sname='out')
            finish()
            return nc
        S.barrier()

        goT = sk16
        A.off = Y_END
        junk2 = A.alloc((1024,), BF16)
        goT = A.alloc((16,), F32)
        A.seek_bytes(TOPB - 16 * 2048 * 2)
        assert A.off >= Y_END + 1024
        yT = A.alloc((16, 2048), BF16)
        S.dma('sp', out=st16[0:8, :], in_=dr["g_out_a"].rearrange("o (a p) -> (o a) p", p=128), writes=['st16'], sname='st16')
        S.dma('sp', out=st16[8:16, :], in_=dr["g_out_b"].rearrange("o (a p) -> (o a) p", p=128), writes=['st16'], sname='st16')
        S.op('pe', 'transpose', reads=['st16', 'ident'], writes=[('ps', 7)], out=ps[7][:, 0:16], in_=st16[0:16, :],
             identity=ident[0:16, 0:16])
        S.op('dve', 'tensor_copy', reads=[('ps', 7)], writes=['goT'], out=goT, in_=ps[7][:, 0:16])
        for t in range(NT):
            for g in range(2):
                col = t * 2 + g
                yv = y_all[:, t, g * 1024:(g + 1) * 1024]
                S.op('act', 'activation', reads=[('y', t, h_) for h_ in range(16 * g, 16 * g + 16)], writes=['junk2', 'ssq2'], out=junk2, in_=yv, func=AF.Square,
                     accum_out=ssq2[:, col:col + 1])
        rstd_ops(ssq2, rstd2, 'ssq2', 'rstd2', 1.0 / 1024)
        for t in range(NT):
            for g in range(2):
                col = t * 2 + g
                yv = y_all[:, t, g * 1024:(g + 1) * 1024]
                S.op('dve', 'tensor_scalar', reads=[('y', t, h_) for h_ in range(16 * g, 16 * g + 16)] + ['rstd2'],
                     writes=[('y', t, h_) for h_ in range(16 * g, 16 * g + 16)], out=yv, in0=yv,
                     scalar1=rstd2[:, col:col + 1], scalar2=None, op0=ALU.mult)
            for g4 in range(4):
                pb = 6 + g4 % 2
                for i in range(4):
                    kc = g4 * 4 + i
                    S.op('pe', 'transpose', reads=[('y', t, 2 * kc), ('y', t, 2 * kc + 1), 'ident'], writes=[('ps', pb)], inc=(i == 3),
                         out=ps[pb][:, i * 128:(i + 1) * 128], in_=y_all[:, t, kc * 128:(kc + 1) * 128], identity=ident)
                for i in range(4):
                    kc = g4 * 4 + i
                    o = yT[:, kc, t * 128:(t + 1) * 128]
                    src = ps[pb][:, i * 128:(i + 1) * 128]
                    if g4 % 2 == 0:
                        S.op('act', 'activation', reads=[('ps', pb), 'goT'], writes=[('yT', t, kc)], out=o, in_=src, func=AF.Identity,
                             scale=goT[:, kc:kc + 1])
                    else:
                        S.op('dve', 'tensor_scalar', reads=[('ps', pb), 'goT'], writes=[('yT', t, kc)], out=o, in0=src,
                             scalar1=goT[:, kc:kc + 1], scalar2=None, op0=ALU.mult)
        S.barrier()

        A.off = m0
        wos = [A.alloc((16, 512), BF16) for _ in range(2)]
        g1b = A.alloc((2048,), F32)
        bg = A.alloc((2048,), F32)
        xq = [A.alloc((512,), F32) for _ in range(6)]
        tq = [A.alloc((512,), F32) for _ in range(2)]
        assert A.off * 4 <= TOPB - 16 * 2048 * 2
        wov = dr["w_out"].rearrange("(kc p) n -> p kc n", p=128)
        S.dma('sp', out=bg, in_=dr["b_out"].broadcast_to([128, D]), writes=['bg'], sname='bc')
        S.dma('sp', out=g1b, in_=dr["gbc"][1], writes=['g1b'], sname='bc')
        S.op('dve', 'tensor_tensor', reads=['bg', 'g1b'], writes=['bg'], out=bg, in0=bg, in1=g1b, op=ALU.mult)
        S.dma('pool', out=wos[0], in_=wov[:, :, 0:512], writes=[('wos', 0)], sname='wos0')
        NXQ = len(xq)
        PRE = 3
        ounits = [(s_, t_) for s_ in range(4) for t_ in range(NT)]

        def qload(k):
            s_, t_ = ounits[k]
            S.dma('sp', out=xq[k % NXQ], in_=dr["x1"][t_ * 128:(t_ + 1) * 128, s_ * 512:(s_ + 1) * 512], writes=[('xq', k % NXQ)],
                  sname=f'xld{k % NXQ}')
        for k in range(PRE):
            qload(k)
        for k, (s, t) in enumerate(ounits):
            if k + PRE < len(ounits):
                qload(k + PRE)
            if t == 0 and s + 1 < 4:
                S.dma('pool', out=wos[(s + 1) % 2], in_=wov[:, :, (s + 1) * 512:(s + 2) * 512], writes=[('wos', (s + 1) % 2)],
                      sname=f'wos{(s + 1) % 2}')
            ob = 4 + k % 2
            xs = k % NXQ
            ts_ = k % 2
            for kc in range(16):
                S.op('pe', 'matmul', reads=[('yT', t, kc), ('wos', s % 2)], writes=[('ps', ob)], inc=(kc == 15), out=PS(ob),
                     lhsT=yT[:, kc, t * 128:(t + 1) * 128], rhs=wos[s % 2][:, kc, :], start=(kc == 0), stop=(kc == 15))
            S.op('dve', 'tensor_tensor', reads=[('ps', ob), 'g1b'], writes=[('tq', ts_)], out=tq[ts_], in0=PS(ob),
                 in1=g1b[:, s * 512:(s + 1) * 512], op=ALU.mult)
            S.op('dve', 'tensor_tensor', reads=[('tq', ts_), 'bg'], writes=[('tq', ts_)], out=tq[ts_], in0=tq[ts_],
                 in1=bg[:, s * 512:(s + 1) * 512], op=ALU.add)
            S.op('dve', 'tensor_tensor', reads=[('tq', ts_), ('xq', xs)], writes=[('xq', xs)], out=xq[xs], in0=tq[ts_], in1=xq[xs],
                 op=ALU.add)
            S.dma('sp', out=dr["x2"][t * 128:(t + 1) * 128, s * 512:(s + 1) * 512], in_=xq[xs], reads=[('xq', xs)],
                  writes=[('x2', t, s)], sname=f'xst{xs}')
        if STOP_AFTER == 'mix':
            finish()
            return nc
        S.barrier()

        ffn(dr["x2"], dr["x3"], 'x3', dr["w_ffn2_in"], dr["w_ffn2_out"], 2)

        A.off = m0
        gfb = A.alloc((2048,), F32)
        xf = [A.alloc((2048,), F32) for _ in range(6)]
        junk3 = A.alloc((2048,), BF16)
        S.dma('sp', out=gfb, in_=dr["g_final"].broadcast_to([128, D]), writes=['gfb'], sname='bc')
        def fA1(t):
            b = t % 6
            xk = ('xf', b)
            S.dma('sp', out=xf[b], in_=dr["x3"][t * 128:(t + 1) * 128, :], writes=[xk], sname=f'xld{b}')
            S.op('act', 'activation', reads=[xk], writes=['junk3', ('ssq', t)], out=junk3, in_=xf[b], func=AF.Square,
                 accum_out=ssq[:, t:t + 1])
            S.op('dve', 'tensor_scalar', reads=[('ssq', t)], writes=[('rstd', t)], out=rstd[:, t:t + 1], in0=ssq[:, t:t + 1],
                 scalar1=1.0 / D, scalar2=EPS, op0=ALU.mult, op1=ALU.add)

        def fA2(t):
            S.op('act', 'activation', reads=[('rstd', t)], writes=[('rstd', t)], out=rstd[:, t:t + 1], in_=rstd[:, t:t + 1], func=AF.Sqrt)
            S.op('dve', 'reciprocal', reads=[('rstd', t)], writes=[('rstd', t)], out=rstd[:, t:t + 1], in_=rstd[:, t:t + 1])

        def fB(t):
            b = t % 6
            xk = ('xf', b)
            S.op('dve', 'scalar_tensor_tensor', reads=[xk, ('rstd', t), 'gfb'], writes=[xk], out=xf[b], in0=xf[b],
                 scalar=rstd[:, t:t + 1], in1=gfb, op0=ALU.mult, op1=ALU.mult)
            S.dma('pool', out=dr["y"][t * 128:(t + 1) * 128, :], in_=xf[b], reads=[xk], writes=[('yout', t)], sname=f'pst{b}')
        for i in range(NT + 2):
            if i < NT:
                fA1(i)
            if 0 <= i - 1 < NT:
                fA2(i - 1)
            if 0 <= i - 2 < NT:
                fB(i - 2)
        finish()
    return nc


_W_NAMES = ["w_ada", "b_ada", "g_ffn1", "w_ffn1_in", "w_ffn1_out", "g_mix", "w_in", "b_in", "sinks", "g_out_a", "g_out_b",
            "w_out", "b_out", "g_ffn2", "w_ffn2_in", "w_ffn2_out"]


def make_in_maps(inputs):
    shared = {}
    for n in _W_NAMES:
        a = np.ascontiguousarray(np.asarray(inputs[n], dtype=np.float32))
        shared[n] = a.reshape(a.shape[-2], a.shape[-1]) if a.ndim == 3 else a.reshape(1, -1)
    shared["g_final"] = np.ascontiguousarray(np.asarray(inputs["g_final"], dtype=np.float32)).reshape(1, -1)
    x = np.asarray(inputs["x"], dtype=np.float32)
    c = np.asarray(inputs["c"], dtype=np.float32)
    pos = np.asarray(inputs["positions"], dtype=np.int32)
    maps = []
    for b in range(8):
        m = dict(shared)
        m["x"] = np.ascontiguousarray(x[b])
        m["c"] = np.ascontiguousarray(c[b:b + 1])
        m["positions"] = np.ascontiguousarray(pos[b:b + 1])
        maps.append(m)
    return maps


def kernel(**inputs):
    nc = build_program()
    in_maps = make_in_maps(inputs)
    res = run_bass_kernel_spmd(nc, in_maps, core_ids=list(range(8)))
    return np.stack([np.asarray(r["y"], dtype=np.float32) for r in res.results], axis=0)
```

```python
import contextlib
import math
import numpy as np
import concourse.bass as bass
import concourse.mybir as mybir
from concourse.bass_utils import run_bass_kernel_spmd

F32 = mybir.dt.float32
BF16 = mybir.dt.bfloat16
I32 = mybir.dt.int32
ALU = mybir.AluOpType
AF = mybir.ActivationFunctionType

D = 2048
T = 2048
DFF = 5632
INW = 4352
EPS = 1e-5
NT = 16
PI = math.pi
DT_SIZE = {F32: 4, BF16: 2, I32: 4}

STOP_AFTER = None


class Sched:
    ENG = ('pe', 'act', 'dve', 'pool', 'sp')
    ATTR = {'pe': 'tensor', 'act': 'scalar', 'dve': 'vector', 'pool': 'gpsimd', 'sp': 'sync'}

    def __init__(self, nc, sems):
        self.nc = nc
        self.free = list(sems)
        self.prog = {e: [] for e in self.ENG}
        self.esem = {e: self.free.pop() for e in self.ENG}
        self.ecnt = {e: 0 for e in self.ENG}
        self.dsem = {}
        self.waited = {e: {} for e in self.ENG}
        self.lastw = {}
        self.readers = {}

    def _sem(self, sk):
        return self.esem[sk[1]] if sk[0] == 'e' else self.dsem[sk[1]][0]

    def _need(self, eng, reads, writes):
        need = {}

        def add(ev):
            sk, v = ev
            if sk[0] == 'd':
                v = self.dsem[sk[1]][1]
            if v > need.get(sk, 0):
                need[sk] = v
        for k in reads:
            if k in self.lastw:
                add(self.lastw[k])
        for k in writes:
            if k in self.lastw:
                add(self.lastw[k])
            for sk, v in self.readers.get(k, {}).items():
                add((sk, v))
        out = []
        for sk, v in need.items():
            if sk == ('e', 'pe') and eng == 'pe':
                continue
            if self.waited[eng].get(sk, 0) >= v:
                continue
            self.waited[eng][sk] = v
            out.append((self._sem(sk), v))
        return out

    def _commit(self, ev, reads, writes):
        for k in writes:
            self.lastw[k] = ev
            self.readers[k] = {}
        for k in reads:
            r = self.readers.setdefault(k, {})
            if ev[1] > r.get(ev[0], 0):
                r[ev[0]] = ev[1]

    def op(self, eng, meth, reads=(), writes=(), inc=True, **kw):
        waits = self._need(eng, reads, writes)
        if inc:
            self.ecnt[eng] += 1
            ev = (('e', eng), self.ecnt[eng])
        else:
            ev = (('e', eng), self.ecnt[eng] + 1)
        self._commit(ev, reads, writes)
        sem = self.esem[eng]

        def run(e):
            for s, v in waits:
                e.wait_ge(s, v)
            ins = getattr(e, meth)(**kw)
            if inc:
                ins.then_inc(sem, 1)
        self.prog[eng].append(run)

    def dma(self, q, out, in_, reads=(), writes=(), sname='misc'):
        if sname not in self.dsem:
            self.dsem[sname] = [self.free.pop(), 0]
        waits = self._need(q, reads, writes)
        ds = self.dsem[sname]
        ds[1] += 16
        ev = (('d', sname), ds[1])
        self._commit(ev, reads, writes)
        sem = ds[0]

        def run(e):
            for s, v in waits:
                e.wait_ge(s, v)
            e.dma_start(out=out, in_=in_).then_inc(sem, 16)
        self.prog[q].append(run)

    def barrier(self, engines=None):
        evs = [(('e', e), self.ecnt[e]) for e in self.ENG if self.ecnt[e] > 0]
        evs += [(('d', n), c[1]) for n, c in self.dsem.items()]
        for eng in (engines or self.ENG):
            waits = []
            for sk, v in evs:
                if sk == ('e', eng):
                    continue
                if self.waited[eng].get(sk, 0) >= v:
                    continue
                self.waited[eng][sk] = v
                waits.append((self._sem(sk), v))

            def run(e, waits=waits):
                for s, v in waits:
                    e.wait_ge(s, v)
            self.prog[eng].append(run)
        if engines is None:
            self.lastw = {}
            self.readers = {}

    def emit(self):
        with self.nc.Block() as block:
            for eng in self.ENG:
                prog = self.prog[eng]

                def body(e, prog=prog):
                    for run in prog:
                        run(e)
                getattr(block, self.ATTR[eng])(body)


class Arena:
    def __init__(self, ap_f32, nwords):
        self.t = ap_f32
        self.n = nwords
        self.off = 0

    def mark(self):
        return self.off

    def release(self, m):
        self.off = m

    def seek_bytes(self, b):
        assert b % 32 == 0
        self.off = b // 4

    def alloc(self, free_shape, dt):
        nel = 1
        for s in free_shape:
            nel *= s
        nbytes = nel * DT_SIZE[dt]
        nw = (nbytes + 31) // 32 * 8
        assert self.off + nw <= self.n, f"arena overflow: {self.off + nw} > {self.n}"
        ap = self.t[:, self.off:self.off + nw]
        self.off += nw
        if dt != F32:
            ap = ap.bitcast(dt)
        ap = ap[:, 0:nel]
        if len(free_shape) == 2:
            ap = ap.rearrange("p (a b) -> p a b", b=free_shape[1])
        elif len(free_shape) == 3:
            ap = ap.rearrange("p (a b c) -> p a b c", b=free_shape[1], c=free_shape[2])
        return ap


def build_program():
    nc = bass.Bass("TRN2", target_bir_lowering=False)
    dr = {}

    def din(name, shape, dt=F32):
        dr[name] = nc.dram_tensor(name, shape, dt, kind="ExternalInput").ap()

    din("x", [T, D]); din("c", [1, D]); din("positions", [1, T], I32)
    din("w_ada", [D, 9 * D]); din("b_ada", [1, 9 * D])
    din("g_ffn1", [1, D]); din("w_ffn1_in", [D, 2 * DFF]); din("w_ffn1_out", [DFF, D])
    din("g_mix", [1, D]); din("w_in", [D, INW]); din("b_in", [1, INW]); din("sinks", [1, 16])
    din("g_out_a", [1, 1024]); din("g_out_b", [1, 1024]); din("w_out", [D, D]); din("b_out", [1, D])
    din("g_ffn2", [1, D]); din("w_ffn2_in", [D, 2 * DFF]); din("w_ffn2_out", [DFF, D]); din("g_final", [1, D])
    dr["y"] = nc.dram_tensor("y", [T, D], F32, kind="ExternalOutput").ap()
    dbg = STOP_AFTER is not None
    skind = "ExternalOutput" if dbg else "Internal"
    dr["x1"] = nc.dram_tensor("x1", [T, D], F32, kind=skind).ap()
    dr["x2"] = nc.dram_tensor("x2", [T, D], F32, kind=skind).ap()
    dr["x3"] = nc.dram_tensor("x3", [T, D], F32, kind=skind).ap()
    dr["gbc"] = nc.dram_tensor("gbc", [3, 128, D], F32, kind=skind).ap()
    dr["qkT"] = nc.dram_tensor("qkT", [26, 128, T], BF16, kind=skind).ap()
    if dbg:
        dr["dbg"] = nc.dram_tensor("dbg", [128, 2048], F32, kind="ExternalOutput").ap()

    ARENA_WORDS = 51 * 1024
    with contextlib.ExitStack() as st:
        arena_t = st.enter_context(nc.sbuf_tensor("arena", [128, ARENA_WORDS], F32))
        iar = st.enter_context(nc.sbuf_tensor("iarena", [128, 768], I32))
        ps = [st.enter_context(nc.psum_tensor(f"ps{i}", [128, 512], F32)) for i in range(8)]
        sems = [st.enter_context(nc.semaphore(f"s{i}")) for i in range(96)]
        S = Sched(nc, sems)
        A = Arena(arena_t[:, :], ARENA_WORDS)

        def PS(i):
            return ps[i][:, :]

        ident = A.alloc((128,), F32)
        ones = A.alloc((128,), F32)
        condT = A.alloc((16,), F32)
        modT = A.alloc((9, 16), F32)
        gscT = A.alloc((3, 16), F32)
        ssq = A.alloc((16,), F32)
        rstd = A.alloc((16,), F32)
        ssq2 = A.alloc((32,), F32)
        rstd2 = A.alloc((32,), F32)
        rl = A.alloc((8,), F32)
        esink = A.alloc((16,), F32)
        cosT = A.alloc((16, 8), F32)
        sinT = A.alloc((16, 8), F32)
        st16 = A.alloc((128,), F32)

        S.op('pool', 'memset', writes=['ones'], ap=ones, constant=1.0)
        S.op('pool', 'memset', writes=['ident'], ap=ident, constant=1.0)
        S.op('pool', 'affine_select', reads=['ident'], writes=['ident'], out=ident, in_=ident,
             pattern=[[-1, 128]], compare_op=ALU.is_equal, fill=0.0, base=0, channel_multiplier=1)

        def vec_to_T(src_row, dst, extra=None):
            S.dma('sp', out=st16[0:16, :], in_=src_row.rearrange("o (a p) -> (o a) p", p=128), writes=['st16'], sname='st16')
            S.op('pe', 'transpose', reads=['st16', 'ident'], writes=[('ps', 7)], out=ps[7][:, 0:16], in_=st16[0:16, :],
                 identity=ident[0:16, 0:16])
            extra(ps[7][:, 0:16])

        m0 = A.mark()
        vec_to_T(dr["c"], condT, lambda p: S.op('act', 'activation', reads=[('ps', 7)], writes=['condT'],
                                                out=condT, in_=p, func=AF.Silu))
        wv = dr["w_ada"].rearrange("(kc p) n -> p kc n", p=128)
        wcnt = [0]

        def m_block_gen(blk, wt, nk, a, ak, gbp, pbanks):
            S.op('pool', 'memset', writes=[ak], ap=a, constant=0.0)
            S.dma('sp', out=a[0:1, :], in_=dr["b_ada"][0:1, blk * 2048:(blk + 1) * 2048], writes=[ak], sname='bada')
            nst = 16 // nk

            def load(st):
                sl = wcnt[0] % len(wt)
                wcnt[0] += 1
                S.dma('sp', out=wt[sl][:, 0:nk, :], in_=wv[:, st * nk:(st + 1) * nk, blk * 2048:(blk + 1) * 2048], writes=[('wt', sl)],
                      sname=f'wt{sl}')
                return sl
            slots = {0: load(0)}
            for st in range(nst):
                if st + 1 < nst:
                    slots[st + 1] = load(st + 1)
                sl = slots[st]
                for j in range(nk):
                    kc = st * nk + j
                    S.op('dve', 'scalar_tensor_tensor', reads=[('wt', sl), 'condT', ak], writes=[ak], out=a, in0=wt[sl][:, j, :],
                         scalar=condT[:, kc:kc + 1], in1=a, op0=ALU.mult, op1=ALU.add)
                yield
            if blk % 3 != 2:
                pb = pbanks[0]
                for cc in range(16):
                    S.op('pe', 'matmul', reads=[ak, 'ones'], writes=[('ps', pb)], inc=(cc == 15), out=ps[pb][:, cc:cc + 1],
                         lhsT=a[:, cc * 128:(cc + 1) * 128], rhs=ones[:, 0:1], start=True, stop=True)
                S.op('dve', 'tensor_copy', reads=[('ps', pb)], writes=[('modT', blk)], out=modT[:, blk, :], in_=ps[pb][:, 0:16])
            else:
                sub = blk // 3
                for s4 in range(4):
                    pb = pbanks[s4 % 2]
                    S.op('pe', 'matmul', reads=[ak, 'ones'], writes=[('ps', pb)], out=PS(pb), lhsT=ones,
                         rhs=a[:, s4 * 512:(s4 + 1) * 512], start=True, stop=True)
                    S.op('act', 'activation', reads=[('ps', pb)], writes=[('gbp', s4 % 2)], out=gbp[s4 % 2], in_=PS(pb),
                         func=AF.Copy, scale=(1.0 if sub == 1 else 0.5))
                    S.dma('sp', out=dr["gbc"][sub][:, s4 * 512:(s4 + 1) * 512], in_=gbp[s4 % 2], reads=[('gbp', s4 % 2)],
                          writes=[('gbc', sub)], sname=f'gbst{s4 % 2}')
            yield

        def gsc_for(sub):
            gname = ("g_ffn1", "g_mix", "g_ffn2")[sub]
            vec_to_T(dr[gname], None, lambda p: S.op(
                'dve', 'scalar_tensor_tensor', reads=[('ps', 7), ('modT', 3 * sub + 1)], writes=[('gscT', sub)], out=gscT[:, sub, :],
                in0=modT[:, 3 * sub + 1, :], scalar=1.0, in1=p, op0=ALU.add, op1=ALU.mult))

        wtB = [A.alloc((4, 2048), F32) for _ in range(2)]
        accB = [A.alloc((2048,), F32) for _ in range(2)]
        gbpB = [A.alloc((512,), F32) for _ in range(2)]
        for blk in range(2):
            for _ in m_block_gen(blk, wtB, 4, accB[blk % 2], ('acc', blk % 2), gbpB, (blk % 2, 2 + blk % 2)):
                pass
        gsc_for(0)

        def m_chain(wt, a, gbp):
            for blk in range(2, 9):
                yield from m_block_gen(blk, wt, 1, a, ('accS', 0), gbp, (6, 7))
                if blk == 4:
                    gsc_for(1)
                if blk == 7:
                    gsc_for(2)
        S.barrier()
        A.release(m0)

        hT = A.alloc((16, 2048), BF16)
        H_END = A.mark()
        TOPB = ARENA_WORDS * 4
        V_BYTES = 16 * 18 * 65 * 2

        def rstd_ops(src, dst, kin, kout, inv_n):
            S.op('dve', 'tensor_scalar', reads=[kin], writes=[kout], out=dst, in0=src, scalar1=inv_n, scalar2=EPS,
                 op0=ALU.mult, op1=ALU.add)
            S.op('act', 'activation', reads=[kout], writes=[kout], out=dst, in_=dst, func=AF.Sqrt)
            S.op('dve', 'reciprocal', reads=[kout], writes=[kout], out=dst, in_=dst)

        NXT = 4
        NORM_ALIAS = [('xt', i) for i in range(NXT)] + ['junk']

        def norm_transpose(xin, sub, dst, dstkey):
            A.off = H_END
            xt = [A.alloc((2048,), F32) for _ in range(NXT)]
            junk = A.alloc((2048,), BF16)
            def stA1(t):
                b = t % NXT
                xk = ('xt', b)
                S.dma('sp', out=xt[b], in_=xin[t * 128:(t + 1) * 128, :], writes=[xk], sname=f'xt{b}')
                S.op('act', 'activation', reads=[xk], writes=['junk', ('ssq', t)], out=junk, in_=xt[b], func=AF.Square,
                     accum_out=ssq[:, t:t + 1])
                S.op('dve', 'tensor_scalar', reads=[('ssq', t)], writes=[('rstd', t)], out=rstd[:, t:t + 1], in0=ssq[:, t:t + 1],
                     scalar1=1.0 / D, scalar2=EPS, op0=ALU.mult, op1=ALU.add)

            def stA2(t):
                b = t % NXT
                xk = ('xt', b)
                S.op('act', 'activation', reads=[('rstd', t)], writes=[('rstd', t)], out=rstd[:, t:t + 1], in_=rstd[:, t:t + 1],
                     func=AF.Sqrt)
                S.op('dve', 'reciprocal', reads=[('rstd', t)], writes=[('rstd', t)], out=rstd[:, t:t + 1], in_=rstd[:, t:t + 1])
                S.op('dve', 'tensor_scalar', reads=[xk, ('rstd', t)], writes=[xk], out=xt[b], in0=xt[b],
                     scalar1=rstd[:, t:t + 1], scalar2=None, op0=ALU.mult)

            def stB(t):
                b = t % NXT
                xk = ('xt', b)
                for g4 in range(4):
                    pb = 4 + g4
                    for i in range(4):
                        kc = g4 * 4 + i
                        S.op('pe', 'transpose', reads=[xk, 'ident'], writes=[('ps', pb)], inc=(i == 3),
                             out=ps[pb][:, i * 128:(i + 1) * 128], in_=xt[b][:, kc * 128:(kc + 1) * 128], identity=ident)
                    for i in range(4):
                        kc = g4 * 4 + i
                        o = dst[:, kc, t * 128:(t + 1) * 128]
                        src = ps[pb][:, i * 128:(i + 1) * 128]
                        if g4 % 2 == 0:
                            S.op('act', 'activation', reads=[('ps', pb), ('gscT', sub), ('modT', 3 * sub)], writes=[(dstkey, t, kc)], out=o, in_=src,
                                 func=AF.Identity, scale=gscT[:, sub, kc:kc + 1], bias=modT[:, 3 * sub, kc:kc + 1])
                        else:
                            S.op('dve', 'tensor_scalar', reads=[('ps', pb), ('gscT', sub), ('modT', 3 * sub)], writes=[(dstkey, t, kc)], out=o, in0=src,
                                 scalar1=gscT[:, sub, kc:kc + 1], scalar2=modT[:, 3 * sub, kc:kc + 1], op0=ALU.mult, op1=ALU.add)
            for i in range(NT + 2):
                if i < NT:
                    stA1(i)
                if 0 <= i - 1 < NT:
                    stA2(i - 1)
                if 0 <= i - 2 < NT:
                    stB(i - 2)
            A.off = H_END

        def ffn(xin, xout, xname, w_in_d, w_out_d, sub, passes=((0, 12), (12, 24), (24, 34), (34, 44)), with_m=False):
            norm_transpose(xin, sub, hT, 'hT')
            passes = list(passes)
            maxn = max(b_ - a_ for a_, b_ in passes)
            wd = [A.alloc((maxn, 512), BF16) for _ in range(2)]
            wg = [A.alloc((16, 256), BF16) for _ in range(2)]
            wu = [A.alloc((16, 256), BF16) for _ in range(2)]
            actT = A.alloc((maxn, 2048), BF16)
            sg = [A.alloc((512,), F32) for _ in range(2)]
            gbt = A.alloc((2048,), F32)
            xp = [A.alloc((512,), F32) for _ in range(6)]
            tmp = [A.alloc((512,), F32) for _ in range(2)]
            mch = None
            if with_m:
                wtS = [A.alloc((1, 2048), F32) for _ in range(2)]
                accS = A.alloc((2048,), F32)
                gbpS = [A.alloc((512,), F32) for _ in range(2)]
                mch = m_chain(wtS, accS, gbpS)
            wiv = w_in_d.rearrange("(kc p) n -> p kc n", p=128)
            wov = w_out_d.rearrange("(j p) d -> p j d", p=128)
            pairs = []
            for pi, (a, b) in enumerate(passes):
                for pr in range(a // 2, b // 2):
                    pairs.append((pi, pr))

            def load_pair(idx):
                _, pr = pairs[idx]
                sl = idx % 2
                S.dma('pool', out=wg[sl], in_=wiv[:, :, pr * 256:(pr + 1) * 256], writes=[('wg', sl)] + NORM_ALIAS, sname=f'wg{sl}')
                S.dma('pool', out=wu[sl], in_=wiv[:, :, DFF + pr * 256:DFF + (pr + 1) * 256], writes=[('wu', sl)] + NORM_ALIAS,
                      sname=f'wu{sl}')

            wd_count = [0]

            def load_wd(pi, s):
                a, b = passes[pi]
                sl = (pi * 4 + s) % 2
                S.dma('pool', out=wd[sl][:, 0:b - a, :], in_=wov[:, a:b, s * 512:(s + 1) * 512], writes=[('wd', sl)] + NORM_ALIAS,
                      sname=f'wd{sl}')

            unit = 0
            ep = 0
            load_pair(0)
            for idx, (pi, pr) in enumerate(pairs):
                a, b = passes[pi]
                if idx + 1 < len(pairs):
                    load_pair(idx + 1)
                sl = idx % 2
                for ci in range(2):
                    jj = 2 * pr + ci - a
                    for ts in range(4):
                        slot = unit % 2
                        unit += 1
                        gbk, ubk = 2 * slot, 2 * slot + 1
                        for kc in range(16):
                            S.op('pe', 'matmul', reads=[('wg', sl)] + [('hT', 4 * ts + q, kc) for q in range(4)], writes=[('ps', gbk)], inc=(kc == 15), out=PS(gbk),
                                 lhsT=wg[sl][:, kc, ci * 128:(ci + 1) * 128], rhs=hT[:, kc, ts * 512:(ts + 1) * 512],
                                 start=(kc == 0), stop=(kc == 15))
                        for kc in range(16):
                            S.op('pe', 'matmul', reads=[('wu', sl)] + [('hT', 4 * ts + q, kc) for q in range(4)], writes=[('ps', ubk)], inc=(kc == 15), out=PS(ubk),
                                 lhsT=wu[sl][:, kc, ci * 128:(ci + 1) * 128], rhs=hT[:, kc, ts * 512:(ts + 1) * 512],
                                 start=(kc == 0), stop=(kc == 15))
                        S.op('act', 'activation', reads=[('ps', gbk)], writes=[('sg', slot)], out=sg[slot], in_=PS(gbk), func=AF.Silu)
                        S.op('dve', 'tensor_tensor', reads=[('sg', slot), ('ps', ubk)], writes=[('actT', ts, jj)],
                             out=actT[:, jj, ts * 512:(ts + 1) * 512], in0=sg[slot], in1=PS(ubk), op=ALU.mult)
                        if mch is not None:
                            next(mch, None)
                first_in_pass = (pr == a // 2)
                second_in_pass = (pr == a // 2 + 1)
                if first_in_pass:
                    load_wd(pi, 0)
                if second_in_pass:
                    load_wd(pi, 1)
                if pr == b // 2 - 1:
                    n = b - a
                    if pi == 0:
                        S.dma('sp', out=gbt, in_=dr["gbc"][sub], reads=[('gbc', sub)], writes=['gbt'], sname='gbt')
                    xsrc = xin if pi == 0 else xout
                    NXP = len(xp)
                    PRE = 3
                    dunits = [(s_, t_) for s_ in range(4) for t_ in range(NT)]

                    def xload(k):
                        s_, t_ = dunits[k]
                        xs_ = k % NXP
                        S.dma('sp', out=xp[xs_], in_=xsrc[t_ * 128:(t_ + 1) * 128, s_ * 512:(s_ + 1) * 512],
                              reads=[(xname, t_, s_)] if pi > 0 else [], writes=[('xp', xs_)], sname=f'xld{xs_}')
                    for k in range(PRE):
                        xload(k)
                    for k, (s, t) in enumerate(dunits):
                        if k + PRE < len(dunits):
                            xload(k + PRE)
                        wsl = (pi * 4 + s) % 2
                        ob = 4 + ep % 2
                        xs = k % NXP
                        ts_ = ep % 2
                        ep += 1
                        xkey = (xname, t, s)
                        for jj in range(n):
                            S.op('pe', 'matmul', reads=[('actT', t // 4, jj), ('wd', wsl)], writes=[('ps', ob)], inc=(jj == n - 1),
                                 out=PS(ob), lhsT=actT[:, jj, t * 128:(t + 1) * 128], rhs=wd[wsl][:, jj, :],
                                 start=(jj == 0), stop=(jj == n - 1))
                        S.op('dve', 'tensor_tensor', reads=[('ps', ob), 'gbt'], writes=[('tmp', ts_)], out=tmp[ts_], in0=PS(ob),
                             in1=gbt[:, s * 512:(s + 1) * 512], op=ALU.mult)
                        S.op('dve', 'tensor_tensor', reads=[('tmp', ts_), ('xp', xs)], writes=[('xp', xs)], out=xp[xs],
                             in0=tmp[ts_], in1=xp[xs], op=ALU.add)
                        S.dma('sp', out=xout[t * 128:(t + 1) * 128, s * 512:(s + 1) * 512], in_=xp[xs], reads=[('xp', xs)],
                              writes=[xkey], sname=f'xst{xs}')
                        if t == NT - 1 and s + 2 < 4:
                            load_wd(pi, s + 2)
            if mch is not None:
                for _ in mch:
                    pass
            S.barrier()
            A.off = H_END

        ffn(dr["x"], dr["x1"], 'x1', dr["w_ffn1_in"], dr["w_ffn1_out"], 0,
            passes=((0, 8), (8, 16), (16, 24), (24, 32), (32, 40), (40, 44)), with_m=True)

        def finish():
            S.barrier(engines=['sp'])
            S.emit()

        if STOP_AFTER in ('mod', 'ffn1'):
            finish()
            return nc

        A.off = H_END
        wis = [A.alloc((16, 512), BF16) for _ in range(2)]
        norm_transpose(dr["x1"], 1, hT, 'hT')
        A.off = H_END
        wis = [A.alloc((16, 512), BF16) for _ in range(2)]
        A.seek_bytes(TOPB - V_BYTES)
        V_all = A.alloc((16, 18, 65), BF16)
        A.off = H_END + 2 * 16 * 512 * 2 // 4
        S.op('dve', 'memset', writes=['V_all'], ap=V_all, constant=1.0)
        b_in_bc = A.alloc((INW,), F32)
        S.dma('sp', out=b_in_bc, in_=dr["b_in"].broadcast_to([128, INW]), writes=['b_in_bc'] + NORM_ALIAS, sname='bc')
        posi = iar[:, 0:128]
        posf = A.alloc((128,), F32)
        posT = A.alloc((16,), F32)
        invf = A.alloc((8,), F32)
        ang = A.alloc((16, 8), F32)
        S.dma('sp', out=posi[0:16, :], in_=dr["positions"].rearrange("o (a p) -> (o a) p", p=128), writes=['posi'], sname='bc')
        S.op('dve', 'tensor_copy', reads=['posi'], writes=['posf'], out=posf[0:16, :], in_=posi[0:16, :])
        S.op('pe', 'transpose', reads=['posf', 'ident'], writes=[('ps', 7)], out=ps[7][:, 0:16], in_=posf[0:16, :],
             identity=ident[0:16, 0:16])
        S.op('dve', 'tensor_copy', reads=[('ps', 7)], writes=['posT'], out=posT, in_=ps[7][:, 0:16])
        inv_freq = (np.float32(500000.0) ** (-(np.arange(0, 16, 2, dtype=np.float32) / np.float32(16)))).astype(np.float32)
        for f in range(8):
            S.op('dve', 'memset', writes=['invf'], ap=invf[:, f:f + 1], constant=float(inv_freq[f]))
        S.op('dve', 'tensor_tensor', reads=['posT', 'invf'], writes=['ang'], out=ang,
             in0=posT.unsqueeze(2).broadcast_to([128, 16, 8]), in1=invf.unsqueeze(1).broadcast_to([128, 16, 8]), op=ALU.mult)
        kq = A.alloc((16, 8), F32)
        kqi = iar[:, 128:256].rearrange("p (a b) -> p a b", b=8)
        C1 = 6.28125
        C2 = 2.0 * PI - C1
        for dst, shift in ((sinT, 0.0), (cosT, 0.5 * PI)):
            S.op('dve', 'tensor_scalar', reads=['ang'], writes=['rtab'], out=dst, in0=ang, scalar1=shift, scalar2=None, op0=ALU.add)
            S.op('dve', 'tensor_scalar', reads=['rtab'], writes=['kq'], out=kq, in0=dst, scalar1=1.0 / (2.0 * PI), scalar2=None,
                 op0=ALU.mult)
            S.op('dve', 'tensor_copy', reads=['kq'], writes=['kqi'], out=kqi, in_=kq)
            S.op('dve', 'tensor_copy', reads=['kqi'], writes=['kq'], out=kq, in_=kqi)
            S.op('dve', 'scalar_tensor_tensor', reads=['kq', 'rtab'], writes=['rtab'], out=dst, in0=kq, scalar=-C1, in1=dst,
                 op0=ALU.mult, op1=ALU.add)
            S.op('dve', 'scalar_tensor_tensor', reads=['kq', 'rtab'], writes=['rtab'], out=dst, in0=kq, scalar=-C2, in1=dst,
                 op0=ALU.mult, op1=ALU.add)
            S.op('dve', 'tensor_scalar', reads=['rtab'], writes=['rtab'], out=dst, in0=dst, scalar1=-PI, scalar2=PI,
                 op0=ALU.max, op1=ALU.min)
            S.op('act', 'activation', reads=['rtab'], writes=['rtab'], out=dst, in_=dst, func=AF.Sin)

        slabs = [(0, 512, 'qk', 0), (512, 512, 'qk', 4), (1280, 512, 'qk', 8), (1792, 512, 'qk', 12),
                 (2304, 512, 'qk', 16), (2816, 512, 'qk', 20), (1024, 256, 'kava', 24),
                 (3328, 512, 'v', 2), (3840, 512, 'v', 10)]
        qst = [A.alloc((4, 2048), BF16) for _ in range(2)]
        pj = [A.alloc((512,), F32) for _ in range(4)]
        rt = [A.alloc((8, 8), F32) for _ in range(4)]
        kd = [A.alloc((2, 128), F32) for _ in range(2)]
        assert A.off * 4 <= TOPB - V_BYTES
        wiv = dr["w_in"].rearrange("(kc p) n -> p kc n", p=128)

        def load_slab(si):
            c0, w, _, _ = slabs[si]
            S.dma('pool', out=wis[si % 2][:, :, 0:w], in_=wiv[:, :, c0:c0 + w], writes=[('wis', si % 2)] + NORM_ALIAS,
                  sname=f'wis{si % 2}')

        def rope(pjt, nh, t, pk):
            v = pjt[:, 0:nh * 64].rearrange("p (h d) -> p h d", d=64)
            x1, x2 = v[:, :, 0:8], v[:, :, 8:16]
            cb = cosT[:, t, :].unsqueeze(1).broadcast_to([128, nh, 8])
            sb = sinT[:, t, :].unsqueeze(1).broadcast_to([128, nh, 8])
            r = [q[:, 0:nh, :] for q in rt]
            S.op('dve', 'tensor_tensor', reads=[pk, 'rtab'], writes=['rt0'], out=r[0], in0=x1, in1=cb, op=ALU.mult)
            S.op('dve', 'tensor_tensor', reads=[pk, 'rtab'], writes=['rt1'], out=r[1], in0=x2, in1=sb, op=ALU.mult)
            S.op('dve', 'tensor_tensor', reads=[pk, 'rtab'], writes=['rt2'], out=r[2], in0=x2, in1=cb, op=ALU.mult)
            S.op('dve', 'tensor_tensor', reads=[pk, 'rtab'], writes=['rt3'], out=r[3], in0=x1, in1=sb, op=ALU.mult)
            S.op('dve', 'tensor_tensor', reads=['rt0', 'rt1'], writes=[pk], out=x1, in0=r[0], in1=r[1], op=ALU.subtract)
            S.op('dve', 'tensor_tensor', reads=['rt2', 'rt3'], writes=[pk], out=x2, in0=r[2], in1=r[3], op=ALU.add)

        load_slab(0)
        u = 0
        trc = [0]
        pend = [None]
        for si, (c0, w, kind, aux) in enumerate(slabs):
            if si + 1 < len(slabs):
                load_slab(si + 1)
            wsl = si % 2
            qsl = si % 2
            for t in range(NT):
                pb = (0, 1, 4, 5)[u % 4]
                pjt = pj[u % 4]
                pk = ('pj', u % 4)
                u += 1
                for kc in range(16):
                    S.op('pe', 'matmul', reads=[('wis', wsl), ('hT', t, kc)], writes=[('ps', pb)], inc=(kc == 15), out=ps[pb][:, 0:w],
                         lhsT=hT[:, kc, t * 128:(t + 1) * 128], rhs=wis[wsl][:, kc, 0:w], start=(kc == 0), stop=(kc == 15))
                S.op('dve', 'tensor_tensor', reads=[('ps', pb), 'b_in_bc'], writes=[pk], out=pjt[:, 0:w], in0=ps[pb][:, 0:w],
                     in1=b_in_bc[:, c0:c0 + w], op=ALU.add)
                if pend[0] is not None:
                    pend[0]()
                    pend[0] = None
                if kind == 'qk':
                    rope(pjt, 8, t, pk)

                    def post(pjt=pjt, pk=pk, t=t, qsl=qsl):
                        tb = (2, 3, 6, 7)[trc[0] % 4]
                        trc[0] += 1
                        for i in range(4):
                            S.op('pe', 'transpose', reads=[pk, 'ident'], writes=[('ps', tb)], inc=(i == 3),
                                 out=ps[tb][:, i * 128:(i + 1) * 128], in_=pjt[:, i * 128:(i + 1) * 128], identity=ident)
                        S.op('act', 'activation', reads=[('ps', tb)], writes=[('qst', qsl)], out=qst[qsl][:, :, t * 128:(t + 1) * 128],
                             in_=ps[tb][:, :].rearrange("p (a b) -> p a b", b=128), func=AF.Copy)
                    pend[0] = post
                elif kind == 'kava':
                    rope(pjt, 2, t, pk)
                    for hh in range(2):
                        for dd in range(2):
                            S.op('dve', 'tensor_copy', reads=[pk], writes=[('kd', t % 2)], out=kd[t % 2][:, hh, dd * 64:(dd + 1) * 64],
                                 in_=pjt[:, hh * 64:(hh + 1) * 64])
                    S.op('act', 'activation', reads=[pk], writes=['V_all'], out=V_all[:, t, 0:2, 0:64],
                         in_=pjt[:, 128:256].rearrange("p (h d) -> p h d", d=64), func=AF.Copy)

                    def post(t=t, qsl=qsl):
                        tb = (2, 3, 6, 7)[trc[0] % 4]
                        trc[0] += 1
                        for i in range(2):
                            S.op('pe', 'transpose', reads=[('kd', t % 2), 'ident'], writes=[('ps', tb)], inc=(i == 1),
                                 out=ps[tb][:, i * 128:(i + 1) * 128], in_=kd[t % 2][:, i, :], identity=ident)
                        S.op('act', 'activation', reads=[('ps', tb)], writes=[('qst', qsl)], out=qst[qsl][:, 0:2, t * 128:(t + 1) * 128],
                             in_=ps[tb][:, 0:256].rearrange("p (a b) -> p a b", b=128), func=AF.Copy)
                    pend[0] = post
                else:
                    S.op('act', 'activation', reads=[pk], writes=['V_all'], out=V_all[:, t, aux:aux + 8, 0:64],
                         in_=pjt[:, 0:512].rearrange("p (h d) -> p h d", d=64), func=AF.Copy)
            if pend[0] is not None and kind in ('qk', 'kava'):
                pend[0]()
                pend[0] = None
            if kind == 'qk':
                S.dma('sp', out=dr["qkT"][aux:aux + 4].rearrange("c p n -> p c n"), in_=qst[qsl], reads=[('qst', qsl)],
                      writes=[('qkT', aux // 4)], sname=f'qst{qsl}')
            elif kind == 'kava':
                S.dma('sp', out=dr["qkT"][24:26].rearrange("c p n -> p c n"), in_=qst[qsl][:, 0:2, :], reads=[('qst', qsl)],
                      writes=[('qkT', 6)], sname=f'qst{qsl}')
        if STOP_AFTER == 'proj':
            for t in range(NT):
                S.op('dve', 'tensor_copy', reads=['V_all'], writes=[('pj', 0)], out=pj[0],
                     in_=V_all[:, t, 0:8, 0:64].rearrange("p h d -> p (h d)"))
                S.dma('sp', out=dr["x2"][t * 128:(t + 1) * 128, 0:512], in_=pj[0], reads=[('pj', 0)], writes=[('x2o', t)], sname='out')
            finish()
            return nc
        S.barrier()

        Y_END = m0 + 16 * 2048
        A.off = Y_END
        maskB = A.alloc((2048,), BF16)
        maskA4 = A.alloc((512,), BF16)
        maskA2 = A.alloc((256,), BF16)
        qz = [[A.alloc((2048,), BF16) for _ in range(2)] for _ in range(2)]
        kT = [A.alloc((2048,), BF16) for _ in range(2)]
        PT = [A.alloc((512,), BF16) for _ in range(6)]
        sk16 = A.alloc((16,), F32)
        assert A.off * 4 <= TOPB - V_BYTES
        A.off = m0
        vf = A.alloc((2048,), F32)
        c1 = A.alloc((2048,), F32)
        c2 = A.alloc((2048,), F32)
        ji = iar[:, 256:768]
        t4 = A.alloc((2048,), F32)
        t16 = A.alloc((2048,), F32)
        A.off = m0
        y_all = A.alloc((16, 2048), F32)
        S.op('pool', 'iota', writes=['vf'], out=vf, pattern=[[1, 2048]], base=2048, channel_multiplier=-1,
             allow_small_or_imprecise_dtypes=True)
        S.op('dve', 'tensor_scalar', reads=['vf'], writes=['c1'], out=c1, in0=vf, scalar1=2048.0, scalar2=None, op0=ALU.is_ge)
        for blk, diag in ((0, False), (1, True), (2, False), (3, True)):
            if diag:
                S.op('dve', 'tensor_copy', reads=['c1'], writes=['maskA'], out=maskA4[:, blk * 128:(blk + 1) * 128], in_=c1[:, 0:128])
            else:
                S.op('dve', 'tensor_scalar', reads=['c1'], writes=['maskA'], out=maskA4[:, blk * 128:(blk + 1) * 128], in0=c1[:, 0:128],
                     scalar1=-1.0, scalar2=1.0, op0=ALU.mult, op1=ALU.add)
        for blk in range(2):
            S.op('dve', 'tensor_copy', reads=['c1'], writes=['maskA'], out=maskA2[:, blk * 128:(blk + 1) * 128], in_=c1[:, 0:128])
        S.op('dve', 'scalar_tensor_tensor', reads=['vf', 'c1'], writes=['c2'], out=c2, in0=vf, scalar=2048.0 + 128.0, in1=c1,
             op0=ALU.is_le, op1=ALU.mult)
        for ch in range(4):
            cs = slice(ch * 512, (ch + 1) * 512)
            for msk, dstt, dk in ((3, t4, 't4'), (15, t16, 't16')):
                S.op('dve', 'tensor_copy', reads=['vf'], writes=['ji'], out=ji, in_=vf[:, cs])
                S.op('dve', 'tensor_scalar', reads=['ji'], writes=['ji'], out=ji, in0=ji, scalar1=msk, scalar2=None, op0=ALU.bitwise_and)
                S.op('dve', 'tensor_scalar', reads=['ji'], writes=[dk], out=dstt[:, cs], in0=ji, scalar1=0.0, scalar2=None, op0=ALU.is_equal)
        S.op('dve', 'scalar_tensor_tensor', reads=['vf', 't4'], writes=['t4'], out=t4, in0=vf, scalar=2048.0 + 512.0, in1=t4,
             op0=ALU.is_le, op1=ALU.mult)
        S.op('dve', 'tensor_tensor', reads=['t4', 'c1'], writes=['t4'], out=t4, in0=t4, in1=c1, op=ALU.mult)
        S.op('dve', 'tensor_tensor', reads=['t4', 'c2'], writes=['c2'], out=c2, in0=t4, in1=c2, op=ALU.add)
        S.op('dve', 'tensor_tensor', reads=['t16', 'c1'], writes=['t16'], out=t16, in0=t16, in1=c1, op=ALU.mult)
        S.op('dve', 'tensor_tensor', reads=['t16', 'c2'], writes=['maskB'], out=maskB, in0=t16, in1=c2, op=ALU.add)
        S.dma('sp', out=sk16, in_=dr["sinks"].broadcast_to([128, 16]), writes=['sk16'], sname='bc')
        for sl_ in range(2):
            S.op('dve', 'memset', writes=[('qT', sl_)], ap=qz[sl_][0][64:128, :], constant=0.0)
            S.op('dve', 'memset', writes=[('qT', sl_)], ap=qz[sl_][1][0:64, :], constant=0.0)
        S.op('act', 'activation', reads=['sk16'], writes=['esink'], out=esink, in_=sk16, func=AF.Exp)
        S.barrier()

        heads = []
        for h in range(16):
            heads.append((h // 2, 64 * (h % 2), 24 + h // 8, h // 8, h * 64, True, h))
        for h in range(16):
            heads.append((8 + h // 2, 64 * (h % 2), 16 + h // 2, 2 + h, 1024 + h * 64, False, None))
        cur = {'q': [None, 0, 0], 'k': [None, 0, 0]}

        def ensure(kind, chunk, bufs):
            c = cur[kind]
            if c[0] == chunk:
                return c[1]
            sl = c[2] % 2
            c[0], c[1], c[2] = chunk, sl, c[2] + 1
            if kind == 'q':
                S.dma('sp', out=qz[sl][0][0:64, :], in_=dr["qkT"][chunk][0:64, :], writes=[('qT', sl)], sname=f'qT{sl}')
                S.dma('sp', out=qz[sl][1][64:128, :], in_=dr["qkT"][chunk][64:128, :], writes=[('qT', sl)], sname=f'qT{sl}')
            else:
                S.dma('sp', out=bufs[sl], in_=dr["qkT"][chunk], writes=[(kind + 'T', sl)], sname=f'{kind}T{sl}')
            return sl

        SB = (0, 1, 6, 7)
        NPT = len(PT)
        units = []
        ruse = [0]

        def new_rl(n):
            i = ruse[0] % 4
            ruse[0] += 1
            return rl[:, 2 * i:2 * i + n], ('rl', i)

        for kv in range(2):
            for ip in range(4):
                qc = 4 * kv + ip
                h0 = 8 * kv + 2 * ip
                for qt in range(NT):
                    kts = [qt - 1, qt] if qt > 0 else [qt]
                    nb = len(kts)
                    sgr, pvl = [], []
                    sgr.append(dict(mms=[(64 * hd, kt, qt * 128, 128, (hd * nb + bi) * 128) for hd in range(2) for bi, kt in enumerate(kts)],
                                    ncols=2 * nb * 128, pt_off=0))
                    n = 2 * nb * 128
                    ab = 2 + len(units) % 4

                    def evacA(ab=ab, qt=qt, h0=h0):
                        for hd in range(2):
                            r, rk = new_rl(1)
                            S.op('dve', 'tensor_scalar', reads=[('ps', ab), 'esink'], writes=[rk], out=r,
                                 in0=ps[ab][:, hd * 128 + 64:hd * 128 + 65], scalar1=esink[:, h0 + hd:h0 + hd + 1], scalar2=None, op0=ALU.add)
                            S.op('dve', 'reciprocal', reads=[rk], writes=[rk], out=r, in_=r)
                            S.op('act', 'activation', reads=[('ps', ab), rk], writes=[('y', qt, h0 + hd)],
                                 out=y_all[:, qt, (h0 + hd) * 64:(h0 + hd + 1) * 64], in_=ps[ab][:, hd * 128:hd * 128 + 64],
                                 func=AF.Identity, scale=r)
                    for hd in range(2):
                        for bi, kt in enumerate(kts):
                            last = (hd == 1 and bi == nb - 1)
                            pvl.append(dict(off=(hd * nb + bi) * 128, kt=kt, vidx=kv, bank=ab, col0=hd * 128, start=(bi == 0),
                                            stop=(bi == nb - 1), evac=evacA if last else None))
                    units.append(dict(qc=qc, kc=24 + kv, sgr=sgr, n=n, mask=(maskA4 if nb == 2 else maskA2), mk='maskA', pv=pvl))
        for h in range(16):
            qc, base, kcb, vidx, ycol = 8 + h // 2, 64 * (h % 2), 16 + h // 2, 2 + h, 1024 + h * 64
            for qs in range(4):
                for kt in range(4 * qs + 4):
                    q0 = max(kt * 128, qs * 512)
                    n = (qs + 1) * 512 - q0
                    d0 = q0 - kt * 128
                    pvl = []
                    for qt in range(q0 // 128, (qs + 1) * 4):
                        ab = 2 + qt % 4

                        def evacB(ab=ab, qt=qt, ycol=ycol):
                            r, rk = new_rl(1)
                            S.op('dve', 'reciprocal', reads=[('ps', ab)], writes=[rk], out=r, in_=ps[ab][:, 64:65])
                            S.op('dve', 'tensor_scalar', reads=[('ps', ab), rk], writes=[('y', qt, ycol // 64)], out=y_all[:, qt, ycol:ycol + 64],
                                 in0=ps[ab][:, 0:64], scalar1=r, scalar2=None, op0=ALU.mult)
                        pvl.append(dict(off=qt * 128 - q0, kt=kt, vidx=vidx, bank=ab, col0=0, start=(kt == 0), stop=(kt == qt),
                                        evac=evacB if kt == qt else None))
                    units.append(dict(qc=qc, kc=kcb, sgr=[dict(mms=[(base, kt, q0, n, 0)], ncols=n, pt_off=0)], n=n,
                                      mask=maskB[:, d0:d0 + n], mk='maskB', pv=pvl))

        sbc = [0]

        def emit_S(i):
            U = units[i]
            qs_ = ensure('q', U['qc'], None)
            ks_ = ensure('k', U['kc'], kT)
            K = kT[ks_]
            n = U['n']
            pt = i % NPT
            for G in U['sgr']:
                sb_ = SB[sbc[0] % 4]
                sbc[0] += 1
                for j, (base, kt, q0, w, oc) in enumerate(G['mms']):
                    S.op('pe', 'matmul', reads=[('qT', qs_), ('kT', ks_)], writes=[('ps', sb_)], inc=(j == len(G['mms']) - 1),
                         out=ps[sb_][:, oc:oc + w], lhsT=K[:, kt * 128:(kt + 1) * 128], rhs=qz[qs_][base // 64][:, q0:q0 + w],
                         start=True, stop=True)
                S.op('act', 'activation', reads=[('ps', sb_)], writes=[('PT', pt)], out=PT[pt][:, G['pt_off']:G['pt_off'] + G['ncols']],
                     in_=ps[sb_][:, 0:G['ncols']], func=AF.Exp, scale=0.125)
            S.op('dve', 'tensor_tensor', reads=[('PT', pt), U['mk']], writes=[('PT', pt)], out=PT[pt][:, 0:n],
                 in0=PT[pt][:, 0:n], in1=U['mask'], op=ALU.mult)

        def emit_PV(i):
            U = units[i]
            pt = i % NPT
            for e in U['pv']:
                ab = e['bank']
                S.op('pe', 'matmul', reads=[('PT', pt), 'V_all'], writes=[('ps', ab)], out=ps[ab][:, e['col0']:e['col0'] + 65],
                     lhsT=PT[pt][:, e['off']:e['off'] + 128], rhs=V_all[:, e['kt'], e['vidx'], :], start=e['start'], stop=e['stop'])
                if e['evac'] is not None:
                    e['evac']()

        LOOK = 3
        GRP = 3
        assert NPT >= LOOK + GRP
        for i0 in range(0, len(units) + LOOK + GRP, GRP):
            for i in range(i0, i0 + GRP):
                if i < len(units):
                    emit_S(i)
            for i in range(i0, i0 + GRP):
                if 0 <= i - LOOK < len(units):
                    emit_PV(i - LOOK)
        if STOP_AFTER == 'attn':
            for t in range(NT):
                S.dma('sp', out=dr["x2"][t * 128:(t + 1) * 128, :], in_=y_all[:, t, :], reads=[('y', t, h_) for h_ in range(32)], writes=[('x2o', t)], sname='out')
            finish()
            return nc
        S.barrier()

        goT = sk16
        A.off = Y_END
        junk2 = A.alloc((1024,), BF16)
        goT = A.alloc((16,), F32)
        A.seek_bytes(TOPB - 16 * 2048 * 2)
        assert A.off >= Y_END + 1024
        yT = A.alloc((16, 2048), BF16)
        S.dma('sp', out=st16[0:8, :], in_=dr["g_out_a"].rearrange("o (a p) -> (o a) p", p=128), writes=['st16'], sname='st16')
        S.dma('sp', out=st16[8:16, :], in_=dr["g_out_b"].rearrange("o (a p) -> (o a) p", p=128), writes=['st16'], sname='st16')
        S.op('pe', 'transpose', reads=['st16', 'ident'], writes=[('ps', 7)], out=ps[7][:, 0:16], in_=st16[0:16, :],
             identity=ident[0:16, 0:16])
        S.op('dve', 'tensor_copy', reads=[('ps', 7)], writes=['goT'], out=goT, in_=ps[7][:, 0:16])
        for t in range(NT):
            for g in range(2):
                col = t * 2 + g
                yv = y_all[:, t, g * 1024:(g + 1) * 1024]
                S.op('act', 'activation', reads=[('y', t, h_) for h_ in range(16 * g, 16 * g + 16)], writes=['junk2', 'ssq2'], out=junk2, in_=yv, func=AF.Square,
                     accum_out=ssq2[:, col:col + 1])
        rstd_ops(ssq2, rstd2, 'ssq2', 'rstd2', 1.0 / 1024)
        for t in range(NT):
            for g in range(2):
                col = t * 2 + g
                yv = y_all[:, t, g * 1024:(g + 1) * 1024]
                S.op('dve', 'tensor_scalar', reads=[('y', t, h_) for h_ in range(16 * g, 16 * g + 16)] + ['rstd2'],
                     writes=[('y', t, h_) for h_ in range(16 * g, 16 * g + 16)], out=yv, in0=yv,
                     scalar1=rstd2[:, col:col + 1], scalar2=None, op0=ALU.mult)
            for g4 in range(4):
                pb = 6 + g4 % 2
                for i in range(4):
                    kc = g4 * 4 + i
                    S.op('pe', 'transpose', reads=[('y', t, 2 * kc), ('y', t, 2 * kc + 1), 'ident'], writes=[('ps', pb)], inc=(i == 3),
                         out=ps[pb][:, i * 128:(i + 1) * 128], in_=y_all[:, t, kc * 128:(kc + 1) * 128], identity=ident)
                for i in range(4):
                    kc = g4 * 4 + i
                    o = yT[:, kc, t * 128:(t + 1) * 128]
                    src = ps[pb][:, i * 128:(i + 1) * 128]
                    if g4 % 2 == 0:
                        S.op('act', 'activation', reads=[('ps', pb), 'goT'], writes=[('yT', t, kc)], out=o, in_=src, func=AF.Identity,
                             scale=goT[:, kc:kc + 1])
                    else:
                        S.op('dve', 'tensor_scalar', reads=[('ps', pb), 'goT'], writes=[('yT', t, kc)], out=o, in0=src,
                             scalar1=goT[:, kc:kc + 1], scalar2=None, op0=ALU.mult)
        S.barrier()

        A.off = m0
        wos = [A.alloc((16, 512), BF16) for _ in range(2)]
        g1b = A.alloc((2048,), F32)
        bg = A.alloc((2048,), F32)
        xq = [A.alloc((512,), F32) for _ in range(6)]
        tq = [A.alloc((512,), F32) for _ in range(2)]
        assert A.off * 4 <= TOPB - 16 * 2048 * 2
        wov = dr["w_out"].rearrange("(kc p) n -> p kc n", p=128)
        S.dma('sp', out=bg, in_=dr["b_out"].broadcast_to([128, D]), writes=['bg'], sname='bc')
        S.dma('sp', out=g1b, in_=dr["gbc"][1], writes=['g1b'], sname='bc')
        S.op('dve', 'tensor_tensor', reads=['bg', 'g1b'], writes=['bg'], out=bg, in0=bg, in1=g1b, op=ALU.mult)
        S.dma('pool', out=wos[0], in_=wov[:, :, 0:512], writes=[('wos', 0)], sname='wos0')
        NXQ = len(xq)
        PRE = 3
        ounits = [(s_, t_) for s_ in range(4) for t_ in range(NT)]

        def qload(k):
            s_, t_ = ounits[k]
            S.dma('sp', out=xq[k % NXQ], in_=dr["x1"][t_ * 128:(t_ + 1) * 128, s_ * 512:(s_ + 1) * 512], writes=[('xq', k % NXQ)],
                  sname=f'xld{k % NXQ}')
        for k in range(PRE):
            qload(k)
        for k, (s, t) in enumerate(ounits):
            if k + PRE < len(ounits):
                qload(k + PRE)
            if t == 0 and s + 1 < 4:
                S.dma('pool', out=wos[(s + 1) % 2], in_=wov[:, :, (s + 1) * 512:(s + 2) * 512], writes=[('wos', (s + 1) % 2)],
                      sname=f'wos{(s + 1) % 2}')
            ob = 4 + k % 2
            xs = k % NXQ
            ts_ = k % 2
            for kc in range(16):
                S.op('pe', 'matmul', reads=[('yT', t, kc), ('wos', s % 2)], writes=[('ps', ob)], inc=(kc == 15), out=PS(ob),
                     lhsT=yT[:, kc, t * 128:(t + 1) * 128], rhs=wos[s % 2][:, kc, :], start=(kc == 0), stop=(kc == 15))
            S.op('dve', 'tensor_tensor', reads=[('ps', ob), 'g1b'], writes=[('tq', ts_)], out=tq[ts_], in0=PS(ob),
                 in1=g1b[:, s * 512:(s + 1) * 512], op=ALU.mult)
            S.op('dve', 'tensor_tensor', reads=[('tq', ts_), 'bg'], writes=[('tq', ts_)], out=tq[ts_], in0=tq[ts_],
                 in1=bg[:, s * 512:(s + 1) * 512], op=ALU.add)
            S.op('dve', 'tensor_tensor', reads=[('tq', ts_), ('xq', xs)], writes=[('xq', xs)], out=xq[xs], in0=tq[ts_], in1=xq[xs],
                 op=ALU.add)
            S.dma('sp', out=dr["x2"][t * 128:(t + 1) * 128, s * 512:(s + 1) * 512], in_=xq[xs], reads=[('xq', xs)],
                  writes=[('x2', t, s)], sname=f'xst{xs}')
        if STOP_AFTER == 'mix':
            finish()
            return nc
        S.barrier()

        ffn(dr["x2"], dr["x3"], 'x3', dr["w_ffn2_in"], dr["w_ffn2_out"], 2)

        A.off = m0
        gfb = A.alloc((2048,), F32)
        xf = [A.alloc((2048,), F32) for _ in range(6)]
        junk3 = A.alloc((2048,), BF16)
        S.dma('sp', out=gfb, in_=dr["g_final"].broadcast_to([128, D]), writes=['gfb'], sname='bc')
        def fA1(t):
            b = t % 6
            xk = ('xf', b)
            S.dma('sp', out=xf[b], in_=dr["x3"][t * 128:(t + 1) * 128, :], writes=[xk], sname=f'xld{b}')
            S.op('act', 'activation', reads=[xk], writes=['junk3', ('ssq', t)], out=junk3, in_=xf[b], func=AF.Square,
                 accum_out=ssq[:, t:t + 1])
            S.op('dve', 'tensor_scalar', reads=[('ssq', t)], writes=[('rstd', t)], out=rstd[:, t:t + 1], in0=ssq[:, t:t + 1],
                 scalar1=1.0 / D, scalar2=EPS, op0=ALU.mult, op1=ALU.add)

        def fA2(t):
            S.op('act', 'activation', reads=[('rstd', t)], writes=[('rstd', t)], out=rstd[:, t:t + 1], in_=rstd[:, t:t + 1], func=AF.Sqrt)
            S.op('dve', 'reciprocal', reads=[('rstd', t)], writes=[('rstd', t)], out=rstd[:, t:t + 1], in_=rstd[:, t:t + 1])

        def fB(t):
            b = t % 6
            xk = ('xf', b)
            S.op('dve', 'scalar_tensor_tensor', reads=[xk, ('rstd', t), 'gfb'], writes=[xk], out=xf[b], in0=xf[b],
                 scalar=rstd[:, t:t + 1], in1=gfb, op0=ALU.mult, op1=ALU.mult)
            S.dma('pool', out=dr["y"][t * 128:(t + 1) * 128, :], in_=xf[b], reads=[xk], writes=[('yout', t)], sname=f'pst{b}')
        for i in range(NT + 2):
            if i < NT:
                fA1(i)
            if 0 <= i - 1 < NT:
                fA2(i - 1)
            if 0 <= i - 2 < NT:
                fB(i - 2)
        finish()
    return nc


_W_NAMES = ["w_ada", "b_ada", "g_ffn1", "w_ffn1_in", "w_ffn1_out", "g_mix", "w_in", "b_in", "sinks", "g_out_a", "g_out_b",
            "w_out", "b_out", "g_ffn2", "w_ffn2_in", "w_ffn2_out"]


def make_in_maps(inputs):
    shared = {}
    for n in _W_NAMES:
        a = np.ascontiguousarray(np.asarray(inputs[n], dtype=np.float32))
        shared[n] = a.reshape(a.shape[-2], a.shape[-1]) if a.ndim == 3 else a.reshape(1, -1)
    shared["g_final"] = np.ascontiguousarray(np.asarray(inputs["g_final"], dtype=np.float32)).reshape(1, -1)
    x = np.asarray(inputs["x"], dtype=np.float32)
    c = np.asarray(inputs["c"], dtype=np.float32)
    pos = np.asarray(inputs["positions"], dtype=np.int32)
    maps = []
    for b in range(8):
        m = dict(shared)
        m["x"] = np.ascontiguousarray(x[b])
        m["c"] = np.ascontiguousarray(c[b:b + 1])
        m["positions"] = np.ascontiguousarray(pos[b:b + 1])
        maps.append(m)
    return maps


def kernel(**inputs):
    nc = build_program()
    in_maps = make_in_maps(inputs)
    res = run_bass_kernel_spmd(nc, in_maps, core_ids=list(range(8)))
    return np.stack([np.asarray(r["y"], dtype=np.float32) for r in res.results], axis=0)
```

```python
import contextlib
import math
import numpy as np
import concourse.bass as bass
import concourse.mybir as mybir
from concourse.bass_utils import run_bass_kernel_spmd

F32 = mybir.dt.float32
BF16 = mybir.dt.bfloat16
I32 = mybir.dt.int32
ALU = mybir.AluOpType
AF = mybir.ActivationFunctionType

D = 2048
T = 2048
DFF = 5632
INW = 4352
EPS = 1e-5
NT = 16
PI = math.pi
DT_SIZE = {F32: 4, BF16: 2, I32: 4}

STOP_AFTER = None


class Sched:
    ENG = ('pe', 'act', 'dve', 'pool', 'sp')
    ATTR = {'pe': 'tensor', 'act': 'scalar', 'dve': 'vector', 'pool': 'gpsimd', 'sp': 'sync'}

    def __init__(self, nc, sems):
        self.nc = nc
        self.free = list(sems)
        self.prog = {e: [] for e in self.ENG}
        self.esem = {e: self.free.pop() for e in self.ENG}
        self.ecnt = {e: 0 for e in self.ENG}
        self.dsem = {}
        self.waited = {e: {} for e in self.ENG}
        self.lastw = {}
        self.readers = {}

    def _sem(self, sk):
        return self.esem[sk[1]] if sk[0] == 'e' else self.dsem[sk[1]][0]

    def _need(self, eng, reads, writes):
        need = {}

        def add(ev):
            sk, v = ev
            if sk[0] == 'd':
                v = self.dsem[sk[1]][1]
            if v > need.get(sk, 0):
                need[sk] = v
        for k in reads:
            if k in self.lastw:
                add(self.lastw[k])
        for k in writes:
            if k in self.lastw:
                add(self.lastw[k])
            for sk, v in self.readers.get(k, {}).items():
                add((sk, v))
        out = []
        for sk, v in need.items():
            if sk == ('e', 'pe') and eng == 'pe':
                continue
            if self.waited[eng].get(sk, 0) >= v:
                continue
            self.waited[eng][sk] = v
            out.append((self._sem(sk), v))
        return out

    def _commit(self, ev, reads, writes):
        for k in writes:
            self.lastw[k] = ev
            self.readers[k] = {}
        for k in reads:
            r = self.readers.setdefault(k, {})
            if ev[1] > r.get(ev[0], 0):
                r[ev[0]] = ev[1]

    def op(self, eng, meth, reads=(), writes=(), inc=True, **kw):
        waits = self._need(eng, reads, writes)
        if inc:
            self.ecnt[eng] += 1
            ev = (('e', eng), self.ecnt[eng])
        else:
            ev = (('e', eng), self.ecnt[eng] + 1)
        self._commit(ev, reads, writes)
        sem = self.esem[eng]

        def run(e):
            for s, v in waits:
                e.wait_ge(s, v)
            ins = getattr(e, meth)(**kw)
            if inc:
                ins.then_inc(sem, 1)
        self.prog[eng].append(run)

    def dma(self, q, out, in_, reads=(), writes=(), sname='misc'):
        if sname not in self.dsem:
            self.dsem[sname] = [self.free.pop(), 0]
        waits = self._need(q, reads, writes)
        ds = self.dsem[sname]
        ds[1] += 16
        ev = (('d', sname), ds[1])
        self._commit(ev, reads, writes)
        sem = ds[0]

        def run(e):
            for s, v in waits:
                e.wait_ge(s, v)
            e.dma_start(out=out, in_=in_).then_inc(sem, 16)
        self.prog[q].append(run)

    def barrier(self, engines=None):
        evs = [(('e', e), self.ecnt[e]) for e in self.ENG if self.ecnt[e] > 0]
        evs += [(('d', n), c[1]) for n, c in self.dsem.items()]
        for eng in (engines or self.ENG):
            waits = []
            for sk, v in evs:
                if sk == ('e', eng):
                    continue
                if self.waited[eng].get(sk, 0) >= v:
                    continue
                self.waited[eng][sk] = v
                waits.append((self._sem(sk), v))

            def run(e, waits=waits):
                for s, v in waits:
                    e.wait_ge(s, v)
            self.prog[eng].append(run)
        if engines is None:
            self.lastw = {}
            self.readers = {}

    def emit(self):
        with self.nc.Block() as block:
            for eng in self.ENG:
                prog = self.prog[eng]

                def body(e, prog=prog):
                    for run in prog:
                        run(e)
                getattr(block, self.ATTR[eng])(body)


class Arena:
    def __init__(self, ap_f32, nwords):
        self.t = ap_f32
        self.n = nwords
        self.off = 0

    def mark(self):
        return self.off

    def release(self, m):
        self.off = m

    def seek_bytes(self, b):
        assert b % 32 == 0
        self.off = b // 4

    def alloc(self, free_shape, dt):
        nel = 1
        for s in free_shape:
            nel *= s
        nbytes = nel * DT_SIZE[dt]
        nw = (nbytes + 31) // 32 * 8
        assert self.off + nw <= self.n, f"arena overflow: {self.off + nw} > {self.n}"
        ap = self.t[:, self.off:self.off + nw]
        self.off += nw
        if dt != F32:
            ap = ap.bitcast(dt)
        ap = ap[:, 0:nel]
        if len(free_shape) == 2:
            ap = ap.rearrange("p (a b) -> p a b", b=free_shape[1])
        elif len(free_shape) == 3:
            ap = ap.rearrange("p (a b c) -> p a b c", b=free_shape[1], c=free_shape[2])
        return ap


def build_program():
    nc = bass.Bass("TRN2", target_bir_lowering=False)
    dr = {}

    def din(name, shape, dt=F32):
        dr[name] = nc.dram_tensor(name, shape, dt, kind="ExternalInput").ap()

    din("x", [T, D]); din("c", [1, D]); din("positions", [1, T], I32)
    din("w_ada", [D, 9 * D]); din("b_ada", [1, 9 * D])
    din("g_ffn1", [1, D]); din("w_ffn1_in", [D, 2 * DFF]); din("w_ffn1_out", [DFF, D])
    din("g_mix", [1, D]); din("w_in", [D, INW]); din("b_in", [1, INW]); din("sinks", [1, 16])
    din("g_out_a", [1, 1024]); din("g_out_b", [1, 1024]); din("w_out", [D, D]); din("b_out", [1, D])
    din("g_ffn2", [1, D]); din("w_ffn2_in", [D, 2 * DFF]); din("w_ffn2_out", [DFF, D]); din("g_final", [1, D])
    dr["y"] = nc.dram_tensor("y", [T, D], F32, kind="ExternalOutput").ap()
    dbg = STOP_AFTER is not None
    skind = "ExternalOutput" if dbg else "Internal"
    dr["x1"] = nc.dram_tensor("x1", [T, D], F32, kind=skind).ap()
    dr["x2"] = nc.dram_tensor("x2", [T, D], F32, kind=skind).ap()
    dr["x3"] = nc.dram_tensor("x3", [T, D], F32, kind=skind).ap()
    dr["gbc"] = nc.dram_tensor("gbc", [3, 128, D], F32, kind=skind).ap()
    dr["qkT"] = nc.dram_tensor("qkT", [26, 128, T], BF16, kind=skind).ap()
    if dbg:
        dr["dbg"] = nc.dram_tensor("dbg", [128, 2048], F32, kind="ExternalOutput").ap()

    ARENA_WORDS = 51 * 1024
    with contextlib.ExitStack() as st:
        arena_t = st.enter_context(nc.sbuf_tensor("arena", [128, ARENA_WORDS], F32))
        iar = st.enter_context(nc.sbuf_tensor("iarena", [128, 768], I32))
        ps = [st.enter_context(nc.psum_tensor(f"ps{i}", [128, 512], F32)) for i in range(8)]
        sems = [st.enter_context(nc.semaphore(f"s{i}")) for i in range(96)]
        S = Sched(nc, sems)
        A = Arena(arena_t[:, :], ARENA_WORDS)

        def PS(i):
            return ps[i][:, :]

        ident = A.alloc((128,), F32)
        ones = A.alloc((128,), F32)
        condT = A.alloc((16,), F32)
        modT = A.alloc((9, 16), F32)
        gscT = A.alloc((3, 16), F32)
        ssq = A.alloc((16,), F32)
        rstd = A.alloc((16,), F32)
        ssq2 = A.alloc((32,), F32)
        rstd2 = A.alloc((32,), F32)
        rl = A.alloc((8,), F32)
        esink = A.alloc((16,), F32)
        cosT = A.alloc((16, 8), F32)
        sinT = A.alloc((16, 8), F32)
        st16 = A.alloc((128,), F32)

        S.op('pool', 'memset', writes=['ones'], ap=ones, constant=1.0)
        S.op('pool', 'memset', writes=['ident'], ap=ident, constant=1.0)
        S.op('pool', 'affine_select', reads=['ident'], writes=['ident'], out=ident, in_=ident,
             pattern=[[-1, 128]], compare_op=ALU.is_equal, fill=0.0, base=0, channel_multiplier=1)

        def vec_to_T(src_row, dst, extra=None):
            S.dma('sp', out=st16[0:16, :], in_=src_row.rearrange("o (a p) -> (o a) p", p=128), writes=['st16'], sname='st16')
            S.op('pe', 'transpose', reads=['st16', 'ident'], writes=[('ps', 7)], out=ps[7][:, 0:16], in_=st16[0:16, :],
                 identity=ident[0:16, 0:16])
            extra(ps[7][:, 0:16])

        m0 = A.mark()
        vec_to_T(dr["c"], condT, lambda p: S.op('act', 'activation', reads=[('ps', 7)], writes=['condT'],
                                                out=condT, in_=p, func=AF.Silu))
        wv = dr["w_ada"].rearrange("(kc p) n -> p kc n", p=128)
        wcnt = [0]

        def m_block_gen(blk, wt, nk, a, ak, gbp, pbanks):
            S.op('pool', 'memset', writes=[ak], ap=a, constant=0.0)
            S.dma('sp', out=a[0:1, :], in_=dr["b_ada"][0:1, blk * 2048:(blk + 1) * 2048], writes=[ak], sname='bada')
            nst = 16 // nk

            def load(st):
                sl = wcnt[0] % len(wt)
                wcnt[0] += 1
                S.dma('sp', out=wt[sl][:, 0:nk, :], in_=wv[:, st * nk:(st + 1) * nk, blk * 2048:(blk + 1) * 2048], writes=[('wt', sl)],
                      sname=f'wt{sl}')
                return sl
            slots = {0: load(0)}
            for st in range(nst):
                if st + 1 < nst:
                    slots[st + 1] = load(st + 1)
                sl = slots[st]
                for j in range(nk):
                    kc = st * nk + j
                    S.op('dve', 'scalar_tensor_tensor', reads=[('wt', sl), 'condT', ak], writes=[ak], out=a, in0=wt[sl][:, j, :],
                         scalar=condT[:, kc:kc + 1], in1=a, op0=ALU.mult, op1=ALU.add)
                yield
            if blk % 3 != 2:
                pb = pbanks[0]
                for cc in range(16):
                    S.op('pe', 'matmul', reads=[ak, 'ones'], writes=[('ps', pb)], inc=(cc == 15), out=ps[pb][:, cc:cc + 1],
                         lhsT=a[:, cc * 128:(cc + 1) * 128], rhs=ones[:, 0:1], start=True, stop=True)
                S.op('dve', 'tensor_copy', reads=[('ps', pb)], writes=[('modT', blk)], out=modT[:, blk, :], in_=ps[pb][:, 0:16])
            else:
                sub = blk // 3
                for s4 in range(4):
                    pb = pbanks[s4 % 2]
                    S.op('pe', 'matmul', reads=[ak, 'ones'], writes=[('ps', pb)], out=PS(pb), lhsT=ones,
                         rhs=a[:, s4 * 512:(s4 + 1) * 512], start=True, stop=True)
                    S.op('act', 'activation', reads=[('ps', pb)], writes=[('gbp', s4 % 2)], out=gbp[s4 % 2], in_=PS(pb),
                         func=AF.Copy, scale=(1.0 if sub == 1 else 0.5))
                    S.dma('sp', out=dr["gbc"][sub][:, s4 * 512:(s4 + 1) * 512], in_=gbp[s4 % 2], reads=[('gbp', s4 % 2)],
                          writes=[('gbc', sub)], sname=f'gbst{s4 % 2}')
            yield

        def gsc_for(sub):
            gname = ("g_ffn1", "g_mix", "g_ffn2")[sub]
            vec_to_T(dr[gname], None, lambda p: S.op(
                'dve', 'scalar_tensor_tensor', reads=[('ps', 7), ('modT', 3 * sub + 1)], writes=[('gscT', sub)], out=gscT[:, sub, :],
                in0=modT[:, 3 * sub + 1, :], scalar=1.0, in1=p, op0=ALU.add, op1=ALU.mult))

        wtB = [A.alloc((4, 2048), F32) for _ in range(2)]
        accB = [A.alloc((2048,), F32) for _ in range(2)]
        gbpB = [A.alloc((512,), F32) for _ in range(2)]
        for blk in range(2):
            for _ in m_block_gen(blk, wtB, 4, accB[blk % 2], ('acc', blk % 2), gbpB, (blk % 2, 2 + blk % 2)):
                pass
        gsc_for(0)

        def m_chain(wt, a, gbp):
            for blk in range(2, 9):
                yield from m_block_gen(blk, wt, 1, a, ('accS', 0), gbp, (6, 7))
                if blk == 4:
                    gsc_for(1)
                if blk == 7:
                    gsc_for(2)
        S.barrier()
        A.release(m0)

        hT = A.alloc((16, 2048), BF16)
        H_END = A.mark()
        TOPB = ARENA_WORDS * 4
        V_BYTES = 16 * 18 * 65 * 2

        def rstd_ops(src, dst, kin, kout, inv_n):
            S.op('dve', 'tensor_scalar', reads=[kin], writes=[kout], out=dst, in0=src, scalar1=inv_n, scalar2=EPS,
                 op0=ALU.mult, op1=ALU.add)
            S.op('act', 'activation', reads=[kout], writes=[kout], out=dst, in_=dst, func=AF.Sqrt)
            S.op('dve', 'reciprocal', reads=[kout], writes=[kout], out=dst, in_=dst)

        NXT = 4
        NORM_ALIAS = [('xt', i) for i in range(NXT)] + ['junk']

        def norm_transpose(xin, sub, dst, dstkey):
            A.off = H_END
            xt = [A.alloc((2048,), F32) for _ in range(NXT)]
            junk = A.alloc((2048,), BF16)
            def stA1(t):
                b = t % NXT
                xk = ('xt', b)
                S.dma('sp', out=xt[b], in_=xin[t * 128:(t + 1) * 128, :], writes=[xk], sname=f'xt{b}')
                S.op('act', 'activation', reads=[xk], writes=['junk', ('ssq', t)], out=junk, in_=xt[b], func=AF.Square,
                     accum_out=ssq[:, t:t + 1])
                S.op('dve', 'tensor_scalar', reads=[('ssq', t)], writes=[('rstd', t)], out=rstd[:, t:t + 1], in0=ssq[:, t:t + 1],
                     scalar1=1.0 / D, scalar2=EPS, op0=ALU.mult, op1=ALU.add)

            def stA2(t):
                b = t % NXT
                xk = ('xt', b)
                S.op('act', 'activation', reads=[('rstd', t)], writes=[('rstd', t)], out=rstd[:, t:t + 1], in_=rstd[:, t:t + 1],
                     func=AF.Sqrt)
                S.op('dve', 'reciprocal', reads=[('rstd', t)], writes=[('rstd', t)], out=rstd[:, t:t + 1], in_=rstd[:, t:t + 1])
                S.op('dve', 'tensor_scalar', reads=[xk, ('rstd', t)], writes=[xk], out=xt[b], in0=xt[b],
                     scalar1=rstd[:, t:t + 1], scalar2=None, op0=ALU.mult)

            def stB(t):
                b = t % NXT
                xk = ('xt', b)
                for g4 in range(4):
                    pb = 4 + g4
                    for i in range(4):
                        kc = g4 * 4 + i
                        S.op('pe', 'transpose', reads=[xk, 'ident'], writes=[('ps', pb)], inc=(i == 3),
                             out=ps[pb][:, i * 128:(i + 1) * 128], in_=xt[b][:, kc * 128:(kc + 1) * 128], identity=ident)
                    for i in range(4):
                        kc = g4 * 4 + i
                        o = dst[:, kc, t * 128:(t + 1) * 128]
                        src = ps[pb][:, i * 128:(i + 1) * 128]
                        if g4 % 2 == 0:
                            S.op('act', 'activation', reads=[('ps', pb), ('gscT', sub), ('modT', 3 * sub)], writes=[(dstkey, t, kc)], out=o, in_=src,
                                 func=AF.Identity, scale=gscT[:, sub, kc:kc + 1], bias=modT[:, 3 * sub, kc:kc + 1])
                        else:
                            S.op('dve', 'tensor_scalar', reads=[('ps', pb), ('gscT', sub), ('modT', 3 * sub)], writes=[(dstkey, t, kc)], out=o, in0=src,
                                 scalar1=gscT[:, sub, kc:kc + 1], scalar2=modT[:, 3 * sub, kc:kc + 1], op0=ALU.mult, op1=ALU.add)
            for i in range(NT + 2):
                if i < NT:
                    stA1(i)
                if 0 <= i - 1 < NT:
                    stA2(i - 1)
                if 0 <= i - 2 < NT:
                    stB(i - 2)
            A.off = H_END

        def ffn(xin, xout, xname, w_in_d, w_out_d, sub, passes=((0, 12), (12, 24), (24, 34), (34, 44)), with_m=False):
            norm_transpose(xin, sub, hT, 'hT')
            passes = list(passes)
            maxn = max(b_ - a_ for a_, b_ in passes)
            wd = [A.alloc((maxn, 512), BF16) for _ in range(2)]
            wg = [A.alloc((16, 256), BF16) for _ in range(2)]
            wu = [A.alloc((16, 256), BF16) for _ in range(2)]
            actT = A.alloc((maxn, 2048), BF16)
            sg = [A.alloc((512,), F32) for _ in range(2)]
            gbt = A.alloc((2048,), F32)
            xp = [A.alloc((512,), F32) for _ in range(6)]
            tmp = [A.alloc((512,), F32) for _ in range(2)]
            mch = None
            if with_m:
                wtS = [A.alloc((1, 2048), F32) for _ in range(2)]
                accS = A.alloc((2048,), F32)
                gbpS = [A.alloc((512,), F32) for _ in range(2)]
                mch = m_chain(wtS, accS, gbpS)
            wiv = w_in_d.rearrange("(kc p) n -> p kc n", p=128)
            wov = w_out_d.rearrange("(j p) d -> p j d", p=128)
            pairs = []
            for pi, (a, b) in enumerate(passes):
                for pr in range(a // 2, b // 2):
                    pairs.append((pi, pr))

            def load_pair(idx):
                _, pr = pairs[idx]
                sl = idx % 2
                S.dma('pool', out=wg[sl], in_=wiv[:, :, pr * 256:(pr + 1) * 256], writes=[('wg', sl)] + NORM_ALIAS, sname=f'wg{sl}')
                S.dma('pool', out=wu[sl], in_=wiv[:, :, DFF + pr * 256:DFF + (pr + 1) * 256], writes=[('wu', sl)] + NORM_ALIAS,
                      sname=f'wu{sl}')

            wd_count = [0]

            def load_wd(pi, s):
                a, b = passes[pi]
                sl = (pi * 4 + s) % 2
                S.dma('pool', out=wd[sl][:, 0:b - a, :], in_=wov[:, a:b, s * 512:(s + 1) * 512], writes=[('wd', sl)] + NORM_ALIAS,
                      sname=f'wd{sl}')

            unit = 0
            ep = 0
            load_pair(0)
            for idx, (pi, pr) in enumerate(pairs):
                a, b = passes[pi]
                if idx + 1 < len(pairs):
                    load_pair(idx + 1)
                sl = idx % 2
                for ci in range(2):
                    jj = 2 * pr + ci - a
                    for ts in range(4):
                        slot = unit % 2
                        unit += 1
                        gbk, ubk = 2 * slot, 2 * slot + 1
                        for kc in range(16):
                            S.op('pe', 'matmul', reads=[('wg', sl)] + [('hT', 4 * ts + q, kc) for q in range(4)], writes=[('ps', gbk)], inc=(kc == 15), out=PS(gbk),
                                 lhsT=wg[sl][:, kc, ci * 128:(ci + 1) * 128], rhs=hT[:, kc, ts * 512:(ts + 1) * 512],
                                 start=(kc == 0), stop=(kc == 15))
                        for kc in range(16):
                            S.op('pe', 'matmul', reads=[('wu', sl)] + [('hT', 4 * ts + q, kc) for q in range(4)], writes=[('ps', ubk)], inc=(kc == 15), out=PS(ubk),
                                 lhsT=wu[sl][:, kc, ci * 128:(ci + 1) * 128], rhs=hT[:, kc, ts * 512:(ts + 1) * 512],
                                 start=(kc == 0), stop=(kc == 15))
                        S.op('act', 'activation', reads=[('ps', gbk)], writes=[('sg', slot)], out=sg[slot], in_=PS(gbk), func=AF.Silu)
                        S.op('dve', 'tensor_tensor', reads=[('sg', slot), ('ps', ubk)], writes=[('actT', ts, jj)],
                             out=actT[:, jj, ts * 512:(ts + 1) * 512], in0=sg[slot], in1=PS(ubk), op=ALU.mult)
                        if mch is not None:
                            next(mch, None)
                first_in_pass = (pr == a // 2)
                second_in_pass = (pr == a // 2 + 1)
                if first_in_pass:
                    load_wd(pi, 0)
                if second_in_pass:
                    load_wd(pi, 1)
                if pr == b // 2 - 1:
                    n = b - a
                    if pi == 0:
                        S.dma('sp', out=gbt, in_=dr["gbc"][sub], reads=[('gbc', sub)], writes=['gbt'], sname='gbt')
                    xsrc = xin if pi == 0 else xout
                    NXP = len(xp)
                    PRE = 3
                    dunits = [(s_, t_) for s_ in range(4) for t_ in range(NT)]

                    def xload(k):
                        s_, t_ = dunits[k]
                        xs_ = k % NXP
                        S.dma('sp', out=xp[xs_], in_=xsrc[t_ * 128:(t_ + 1) * 128, s_ * 512:(s_ + 1) * 512],
                              reads=[(xname, t_, s_)] if pi > 0 else [], writes=[('xp', xs_)], sname=f'xld{xs_}')
                    for k in range(PRE):
                        xload(k)
                    for k, (s, t) in enumerate(dunits):
                        if k + PRE < len(dunits):
                            xload(k + PRE)
                        wsl = (pi * 4 + s) % 2
                        ob = 4 + ep % 2
                        xs = k % NXP
                        ts_ = ep % 2
                        ep += 1
                        xkey = (xname, t, s)
                        for jj in range(n):
                            S.op('pe', 'matmul', reads=[('actT', t // 4, jj), ('wd', wsl)], writes=[('ps', ob)], inc=(jj == n - 1),
                                 out=PS(ob), lhsT=actT[:, jj, t * 128:(t + 1) * 128], rhs=wd[wsl][:, jj, :],
                                 start=(jj == 0), stop=(jj == n - 1))
                        S.op('dve', 'tensor_tensor', reads=[('ps', ob), 'gbt'], writes=[('tmp', ts_)], out=tmp[ts_], in0=PS(ob),
                             in1=gbt[:, s * 512:(s + 1) * 512], op=ALU.mult)
                        S.op('dve', 'tensor_tensor', reads=[('tmp', ts_), ('xp', xs)], writes=[('xp', xs)], out=xp[xs],
                             in0=tmp[ts_], in1=xp[xs], op=ALU.add)
                        S.dma('sp', out=xout[t * 128:(t + 1) * 128, s * 512:(s + 1) * 512], in_=xp[xs], reads=[('xp', xs)],
                              writes=[xkey], sname=f'xst{xs}')
                        if t == NT - 1 and s + 2 < 4:
                            load_wd(pi, s + 2)
            if mch is not None:
                for _ in mch:
                    pass
            S.barrier()
            A.off = H_END

        ffn(dr["x"], dr["x1"], 'x1', dr["w_ffn1_in"], dr["w_ffn1_out"], 0,
            passes=((0, 8), (8, 16), (16, 24), (24, 32), (32, 40), (40, 44)), with_m=True)

        def finish():
            S.barrier(engines=['sp'])
            S.emit()

        if STOP_AFTER in ('mod', 'ffn1'):
            finish()
            return nc

        A.off = H_END
        wis = [A.alloc((16, 512), BF16) for _ in range(2)]
        norm_transpose(dr["x1"], 1, hT, 'hT')
        A.off = H_END
        wis = [A.alloc((16, 512), BF16) for _ in range(2)]
        A.seek_bytes(TOPB - V_BYTES)
        V_all = A.alloc((16, 18, 65), BF16)
        A.off = H_END + 2 * 16 * 512 * 2 // 4
        S.op('dve', 'memset', writes=['V_all'], ap=V_all, constant=1.0)
        b_in_bc = A.alloc((INW,), F32)
        S.dma('sp', out=b_in_bc, in_=dr["b_in"].broadcast_to([128, INW]), writes=['b_in_bc'] + NORM_ALIAS, sname='bc')
        posi = iar[:, 0:128]
        posf = A.alloc((128,), F32)
        posT = A.alloc((16,), F32)
        invf = A.alloc((8,), F32)
        ang = A.alloc((16, 8), F32)
        S.dma('sp', out=posi[0:16, :], in_=dr["positions"].rearrange("o (a p) -> (o a) p", p=128), writes=['posi'], sname='bc')
        S.op('dve', 'tensor_copy', reads=['posi'], writes=['posf'], out=posf[0:16, :], in_=posi[0:16, :])
        S.op('pe', 'transpose', reads=['posf', 'ident'], writes=[('ps', 7)], out=ps[7][:, 0:16], in_=posf[0:16, :],
             identity=ident[0:16, 0:16])
        S.op('dve', 'tensor_copy', reads=[('ps', 7)], writes=['posT'], out=posT, in_=ps[7][:, 0:16])
        inv_freq = (np.float32(500000.0) ** (-(np.arange(0, 16, 2, dtype=np.float32) / np.float32(16)))).astype(np.float32)
        for f in range(8):
            S.op('dve', 'memset', writes=['invf'], ap=invf[:, f:f + 1], constant=float(inv_freq[f]))
        S.op('dve', 'tensor_tensor', reads=['posT', 'invf'], writes=['ang'], out=ang,
             in0=posT.unsqueeze(2).broadcast_to([128, 16, 8]), in1=invf.unsqueeze(1).broadcast_to([128, 16, 8]), op=ALU.mult)
        kq = A.alloc((16, 8), F32)
        kqi = iar[:, 128:256].rearrange("p (a b) -> p a b", b=8)
        C1 = 6.28125
        C2 = 2.0 * PI - C1
        for dst, shift in ((sinT, 0.0), (cosT, 0.5 * PI)):
            S.op('dve', 'tensor_scalar', reads=['ang'], writes=['rtab'], out=dst, in0=ang, scalar1=shift, scalar2=None, op0=ALU.add)
            S.op('dve', 'tensor_scalar', reads=['rtab'], writes=['kq'], out=kq, in0=dst, scalar1=1.0 / (2.0 * PI), scalar2=None,
                 op0=ALU.mult)
            S.op('dve', 'tensor_copy', reads=['kq'], writes=['kqi'], out=kqi, in_=kq)
            S.op('dve', 'tensor_copy', reads=['kqi'], writes=['kq'], out=kq, in_=kqi)
            S.op('dve', 'scalar_tensor_tensor', reads=['kq', 'rtab'], writes=['rtab'], out=dst, in0=kq, scalar=-C1, in1=dst,
                 op0=ALU.mult, op1=ALU.add)
            S.op('dve', 'scalar_tensor_tensor', reads=['kq', 'rtab'], writes=['rtab'], out=dst, in0=kq, scalar=-C2, in1=dst,
                 op0=ALU.mult, op1=ALU.add)
            S.op('dve', 'tensor_scalar', reads=['rtab'], writes=['rtab'], out=dst, in0=dst, scalar1=-PI, scalar2=PI,
                 op0=ALU.max, op1=ALU.min)
            S.op('act', 'activation', reads=['rtab'], writes=['rtab'], out=dst, in_=dst, func=AF.Sin)

        slabs = [(0, 512, 'qk', 0), (512, 512, 'qk', 4), (1280, 512, 'qk', 8), (1792, 512, 'qk', 12),
                 (2304, 512, 'qk', 16), (2816, 512, 'qk', 20), (1024, 256, 'kava', 24),
                 (3328, 512, 'v', 2), (3840, 512, 'v', 10)]
        qst = [A.alloc((4, 2048), BF16) for _ in range(2)]
        pj = [A.alloc((512,), F32) for _ in range(4)]
        rt = [A.alloc((8, 8), F32) for _ in range(4)]
        kd = [A.alloc((2, 128), F32) for _ in range(2)]
        assert A.off * 4 <= TOPB - V_BYTES
        wiv = dr["w_in"].rearrange("(kc p) n -> p kc n", p=128)

        def load_slab(si):
            c0, w, _, _ = slabs[si]
            S.dma('pool', out=wis[si % 2][:, :, 0:w], in_=wiv[:, :, c0:c0 + w], writes=[('wis', si % 2)] + NORM_ALIAS,
                  sname=f'wis{si % 2}')

        def rope(pjt, nh, t, pk):
            v = pjt[:, 0:nh * 64].rearrange("p (h d) -> p h d", d=64)
            x1, x2 = v[:, :, 0:8], v[:, :, 8:16]
            cb = cosT[:, t, :].unsqueeze(1).broadcast_to([128, nh, 8])
            sb = sinT[:, t, :].unsqueeze(1).broadcast_to([128, nh, 8])
            r = [q[:, 0:nh, :] for q in rt]
            S.op('dve', 'tensor_tensor', reads=[pk, 'rtab'], writes=['rt0'], out=r[0], in0=x1, in1=cb, op=ALU.mult)
            S.op('dve', 'tensor_tensor', reads=[pk, 'rtab'], writes=['rt1'], out=r[1], in0=x2, in1=sb, op=ALU.mult)
            S.op('dve', 'tensor_tensor', reads=[pk, 'rtab'], writes=['rt2'], out=r[2], in0=x2, in1=cb, op=ALU.mult)
            S.op('dve', 'tensor_tensor', reads=[pk, 'rtab'], writes=['rt3'], out=r[3], in0=x1, in1=sb, op=ALU.mult)
            S.op('dve', 'tensor_tensor', reads=['rt0', 'rt1'], writes=[pk], out=x1, in0=r[0], in1=r[1], op=ALU.subtract)
            S.op('dve', 'tensor_tensor', reads=['rt2', 'rt3'], writes=[pk], out=x2, in0=r[2], in1=r[3], op=ALU.add)

        load_slab(0)
        u = 0
        trc = [0]
        pend = [None]
        for si, (c0, w, kind, aux) in enumerate(slabs):
            if si + 1 < len(slabs):
                load_slab(si + 1)
            wsl = si % 2
            qsl = si % 2
            for t in range(NT):
                pb = (0, 1, 4, 5)[u % 4]
                pjt = pj[u % 4]
                pk = ('pj', u % 4)
                u += 1
                for kc in range(16):
                    S.op('pe', 'matmul', reads=[('wis', wsl), ('hT', t, kc)], writes=[('ps', pb)], inc=(kc == 15), out=ps[pb][:, 0:w],
                         lhsT=hT[:, kc, t * 128:(t + 1) * 128], rhs=wis[wsl][:, kc, 0:w], start=(kc == 0), stop=(kc == 15))
                S.op('dve', 'tensor_tensor', reads=[('ps', pb), 'b_in_bc'], writes=[pk], out=pjt[:, 0:w], in0=ps[pb][:, 0:w],
                     in1=b_in_bc[:, c0:c0 + w], op=ALU.add)
                if pend[0] is not None:
                    pend[0]()
                    pend[0] = None
                if kind == 'qk':
                    rope(pjt, 8, t, pk)

                    def post(pjt=pjt, pk=pk, t=t, qsl=qsl):
                        tb = (2, 3, 6, 7)[trc[0] % 4]
                        trc[0] += 1
                        for i in range(4):
                            S.op('pe', 'transpose', reads=[pk, 'ident'], writes=[('ps', tb)], inc=(i == 3),
                                 out=ps[tb][:, i * 128:(i + 1) * 128], in_=pjt[:, i * 128:(i + 1) * 128], identity=ident)
                        S.op('act', 'activation', reads=[('ps', tb)], writes=[('qst', qsl)], out=qst[qsl][:, :, t * 128:(t + 1) * 128],
                             in_=ps[tb][:, :].rearrange("p (a b) -> p a b", b=128), func=AF.Copy)
                    pend[0] = post
                elif kind == 'kava':
                    rope(pjt, 2, t, pk)
                    for hh in range(2):
                        for dd in range(2):
                            S.op('dve', 'tensor_copy', reads=[pk], writes=[('kd', t % 2)], out=kd[t % 2][:, hh, dd * 64:(dd + 1) * 64],
                                 in_=pjt[:, hh * 64:(hh + 1) * 64])
                    S.op('act', 'activation', reads=[pk], writes=['V_all'], out=V_all[:, t, 0:2, 0:64],
                         in_=pjt[:, 128:256].rearrange("p (h d) -> p h d", d=64), func=AF.Copy)

                    def post(t=t, qsl=qsl):
                        tb = (2, 3, 6, 7)[trc[0] % 4]
                        trc[0] += 1
                        for i in range(2):
                            S.op('pe', 'transpose', reads=[('kd', t % 2), 'ident'], writes=[('ps', tb)], inc=(i == 1),
                                 out=ps[tb][:, i * 128:(i + 1) * 128], in_=kd[t % 2][:, i, :], identity=ident)
                        S.op('act', 'activation', reads=[('ps', tb)], writes=[('qst', qsl)], out=qst[qsl][:, 0:2, t * 128:(t + 1) * 128],
                             in_=ps[tb][:, 0:256].rearrange("p (a b) -> p a b", b=128), func=AF.Copy)
                    pend[0] = post
                else:
                    S.op('act', 'activation', reads=[pk], writes=['V_all'], out=V_all[:, t, aux:aux + 8, 0:64],
                         in_=pjt[:, 0:512].rearrange("p (h d) -> p h d", d=64), func=AF.Copy)
            if pend[0] is not None and kind in ('qk', 'kava'):
                pend[0]()
                pend[0] = None
            if kind == 'qk':
                S.dma('sp', out=dr["qkT"][aux:aux + 4].rearrange("c p n -> p c n"), in_=qst[qsl], reads=[('qst', qsl)],
                      writes=[('qkT', aux // 4)], sname=f'qst{qsl}')
            elif kind == 'kava':
                S.dma('sp', out=dr["qkT"][24:26].rearrange("c p n -> p c n"), in_=qst[qsl][:, 0:2, :], reads=[('qst', qsl)],
                      writes=[('qkT', 6)], sname=f'qst{qsl}')
        if STOP_AFTER == 'proj':
            for t in range(NT):
                S.op('dve', 'tensor_copy', reads=['V_all'], writes=[('pj', 0)], out=pj[0],
                     in_=V_all[:, t, 0:8, 0:64].rearrange("p h d -> p (h d)"))
                S.dma('sp', out=dr["x2"][t * 128:(t + 1) * 128, 0:512], in_=pj[0], reads=[('pj', 0)], writes=[('x2o', t)], sname='out')
            finish()
            return nc
        S.barrier()

        Y_END = m0 + 16 * 2048
        A.off = Y_END
        maskB = A.alloc((2048,), BF16)
        maskA4 = A.alloc((512,), BF16)
        maskA2 = A.alloc((256,), BF16)
        qz = [[A.alloc((2048,), BF16) for _ in range(2)] for _ in range(2)]
        kT = [A.alloc((2048,), BF16) for _ in range(2)]
        PT = [A.alloc((512,), BF16) for _ in range(6)]
        sk16 = A.alloc((16,), F32)
        assert A.off * 4 <= TOPB - V_BYTES
        A.off = m0
        vf = A.alloc((2048,), F32)
        c1 = A.alloc((2048,), F32)
        c2 = A.alloc((2048,), F32)
        ji = iar[:, 256:768]
        t4 = A.alloc((2048,), F32)
        t16 = A.alloc((2048,), F32)
        A.off = m0
        y_all = A.alloc((16, 2048), F32)
        S.op('pool', 'iota', writes=['vf'], out=vf, pattern=[[1, 2048]], base=2048, channel_multiplier=-1,
             allow_small_or_imprecise_dtypes=True)
        S.op('dve', 'tensor_scalar', reads=['vf'], writes=['c1'], out=c1, in0=vf, scalar1=2048.0, scalar2=None, op0=ALU.is_ge)
        for blk, diag in ((0, False), (1, True), (2, False), (3, True)):
            if diag:
                S.op('dve', 'tensor_copy', reads=['c1'], writes=['maskA'], out=maskA4[:, blk * 128:(blk + 1) * 128], in_=c1[:, 0:128])
            else:
                S.op('dve', 'tensor_scalar', reads=['c1'], writes=['maskA'], out=maskA4[:, blk * 128:(blk + 1) * 128], in0=c1[:, 0:128],
                     scalar1=-1.0, scalar2=1.0, op0=ALU.mult, op1=ALU.add)
        for blk in range(2):
            S.op('dve', 'tensor_copy', reads=['c1'], writes=['maskA'], out=maskA2[:, blk * 128:(blk + 1) * 128], in_=c1[:, 0:128])
        S.op('dve', 'scalar_tensor_tensor', reads=['vf', 'c1'], writes=['c2'], out=c2, in0=vf, scalar=2048.0 + 128.0, in1=c1,
             op0=ALU.is_le, op1=ALU.mult)
        for ch in range(4):
            cs = slice(ch * 512, (ch + 1) * 512)
            for msk, dstt, dk in ((3, t4, 't4'), (15, t16, 't16')):
                S.op('dve', 'tensor_copy', reads=['vf'], writes=['ji'], out=ji, in_=vf[:, cs])
                S.op('dve', 'tensor_scalar', reads=['ji'], writes=['ji'], out=ji, in0=ji, scalar1=msk, scalar2=None, op0=ALU.bitwise_and)
                S.op('dve', 'tensor_scalar', reads=['ji'], writes=[dk], out=dstt[:, cs], in0=ji, scalar1=0.0, scalar2=None, op0=ALU.is_equal)
        S.op('dve', 'scalar_tensor_tensor', reads=['vf', 't4'], writes=['t4'], out=t4, in0=vf, scalar=2048.0 + 512.0, in1=t4,
             op0=ALU.is_le, op1=ALU.mult)
        S.op('dve', 'tensor_tensor', reads=['t4', 'c1'], writes=['t4'], out=t4, in0=t4, in1=c1, op=ALU.mult)
        S.op('dve', 'tensor_tensor', reads=['t4', 'c2'], writes=['c2'], out=c2, in0=t4, in1=c2, op=ALU.add)
        S.op('dve', 'tensor_tensor', reads=['t16', 'c1'], writes=['t16'], out=t16, in0=t16, in1=c1, op=ALU.mult)
        S.op('dve', 'tensor_tensor', reads=['t16', 'c2'], writes=['maskB'], out=maskB, in0=t16, in1=c2, op=ALU.add)
        S.dma('sp', out=sk16, in_=dr["sinks"].broadcast_to([128, 16]), writes=['sk16'], sname='bc')
        for sl_ in range(2):
            S.op('dve', 'memset', writes=[('qT', sl_)], ap=qz[sl_][0][64:128, :], constant=0.0)
            S.op('dve', 'memset', writes=[('qT', sl_)], ap=qz[sl_][1][0:64, :], constant=0.0)
        S.op('act', 'activation', reads=['sk16'], writes=['esink'], out=esink, in_=sk16, func=AF.Exp)
        S.barrier()

        heads = []
        for h in range(16):
            heads.append((h // 2, 64 * (h % 2), 24 + h // 8, h // 8, h * 64, True, h))
        for h in range(16):
            heads.append((8 + h // 2, 64 * (h % 2), 16 + h // 2, 2 + h, 1024 + h * 64, False, None))
        cur = {'q': [None, 0, 0], 'k': [None, 0, 0]}

        def ensure(kind, chunk, bufs):
            c = cur[kind]
            if c[0] == chunk:
                return c[1]
            sl = c[2] % 2
            c[0], c[1], c[2] = chunk, sl, c[2] + 1
            if kind == 'q':
                S.dma('sp', out=qz[sl][0][0:64, :], in_=dr["qkT"][chunk][0:64, :], writes=[('qT', sl)], sname=f'qT{sl}')
                S.dma('sp', out=qz[sl][1][64:128, :], in_=dr["qkT"][chunk][64:128, :], writes=[('qT', sl)], sname=f'qT{sl}')
            else:
                S.dma('sp', out=bufs[sl], in_=dr["qkT"][chunk], writes=[(kind + 'T', sl)], sname=f'{kind}T{sl}')
            return sl

        SB = (0, 1, 6, 7)
        NPT = len(PT)
        units = []
        ruse = [0]

        def new_rl(n):
            i = ruse[0] % 4
            ruse[0] += 1
            return rl[:, 2 * i:2 * i + n], ('rl', i)

        for kv in range(2):
            for ip in range(4):
                qc = 4 * kv + ip
                h0 = 8 * kv + 2 * ip
                for qt in range(NT):
                    kts = [qt - 1, qt] if qt > 0 else [qt]
                    nb = len(kts)
                    sgr, pvl = [], []
                    sgr.append(dict(mms=[(64 * hd, kt, qt * 128, 128, (hd * nb + bi) * 128) for hd in range(2) for bi, kt in enumerate(kts)],
                                    ncols=2 * nb * 128, pt_off=0))
                    n = 2 * nb * 128
                    ab = 2 + len(units) % 4

                    def evacA(ab=ab, qt=qt, h0=h0):
                        for hd in range(2):
                            r, rk = new_rl(1)
                            S.op('dve', 'tensor_scalar', reads=[('ps', ab), 'esink'], writes=[rk], out=r,
                                 in0=ps[ab][:, hd * 128 + 64:hd * 128 + 65], scalar1=esink[:, h0 + hd:h0 + hd + 1], scalar2=None, op0=ALU.add)
                            S.op('dve', 'reciprocal', reads=[rk], writes=[rk], out=r, in_=r)
                            S.op('act', 'activation', reads=[('ps', ab), rk], writes=[('y', qt, h0 + hd)],
                                 out=y_all[:, qt, (h0 + hd) * 64:(h0 + hd + 1) * 64], in_=ps[ab][:, hd * 128:hd * 128 + 64],
                                 func=AF.Identity, scale=r)
                    for hd in range(2):
                        for bi, kt in enumerate(kts):
                            last = (hd == 1 and bi == nb - 1)
                            pvl.append(dict(off=(hd * nb + bi) * 128, kt=kt, vidx=kv, bank=ab, col0=hd * 128, start=(bi == 0),
                                            stop=(bi == nb - 1), evac=evacA if last else None))
                    units.append(dict(qc=qc, kc=24 + kv, sgr=sgr, n=n, mask=(maskA4 if nb == 2 else maskA2), mk='maskA', pv=pvl))
        for h in range(16):
            qc, base, kcb, vidx, ycol = 8 + h // 2, 64 * (h % 2), 16 + h // 2, 2 + h, 1024 + h * 64
            for qs in range(4):
                for kt in range(4 * qs + 4):
                    q0 = max(kt * 128, qs * 512)
                    n = (qs + 1) * 512 - q0
                    d0 = q0 - kt * 128
                    pvl = []
                    for qt in range(q0 // 128, (qs + 1) * 4):
                        ab = 2 + qt % 4

                        def evacB(ab=ab, qt=qt, ycol=ycol):
                            r, rk = new_rl(1)
                            S.op('dve', 'reciprocal', reads=[('ps', ab)], writes=[rk], out=r, in_=ps[ab][:, 64:65])
                            S.op('dve', 'tensor_scalar', reads=[('ps', ab), rk], writes=[('y', qt, ycol // 64)], out=y_all[:, qt, ycol:ycol + 64],
                                 in0=ps[ab][:, 0:64], scalar1=r, scalar2=None, op0=ALU.mult)
                        pvl.append(dict(off=qt * 128 - q0, kt=kt, vidx=vidx, bank=ab, col0=0, start=(kt == 0), stop=(kt == qt),
                                        evac=evacB if kt == qt else None))
                    units.append(dict(qc=qc, kc=kcb, sgr=[dict(mms=[(base, kt, q0, n, 0)], ncols=n, pt_off=0)], n=n,
                                      mask=maskB[:, d0:d0 + n], mk='maskB', pv=pvl))

        sbc = [0]

        def emit_S(i):
            U = units[i]
            qs_ = ensure('q', U['qc'], None)
            ks_ = ensure('k', U['kc'], kT)
            K = kT[ks_]
            n = U['n']
            pt = i % NPT
            for G in U['sgr']:
                sb_ = SB[sbc[0] % 4]
                sbc[0] += 1
                for j, (base, kt, q0, w, oc) in enumerate(G['mms']):
                    S.op('pe', 'matmul', reads=[('qT', qs_), ('kT', ks_)], writes=[('ps', sb_)], inc=(j == len(G['mms']) - 1),
                         out=ps[sb_][:, oc:oc + w], lhsT=K[:, kt * 128:(kt + 1) * 128], rhs=qz[qs_][base // 64][:, q0:q0 + w],
                         start=True, stop=True)
                S.op('act', 'activation', reads=[('ps', sb_)], writes=[('PT', pt)], out=PT[pt][:, G['pt_off']:G['pt_off'] + G['ncols']],
                     in_=ps[sb_][:, 0:G['ncols']], func=AF.Exp, scale=0.125)
            S.op('dve', 'tensor_tensor', reads=[('PT', pt), U['mk']], writes=[('PT', pt)], out=PT[pt][:, 0:n],
                 in0=PT[pt][:, 0:n], in1=U['mask'], op=ALU.mult)

        def emit_PV(i):
            U = units[i]
            pt = i % NPT
            for e in U['pv']:
                ab = e['bank']
                S.op('pe', 'matmul', reads=[('PT', pt), 'V_all'], writes=[('ps', ab)], out=ps[ab][:, e['col0']:e['col0'] + 65],
                     lhsT=PT[pt][:, e['off']:e['off'] + 128], rhs=V_all[:, e['kt'], e['vidx'], :], start=e['start'], stop=e['stop'])
                if e['evac'] is not None:
                    e['evac']()

        LOOK = 4
        GRP = 2
        assert NPT >= LOOK + GRP
        for i0 in range(0, len(units) + LOOK + GRP, GRP):
            for i in range(i0, i0 + GRP):
                if i < len(units):
                    emit_S(i)
            for i in range(i0, i0 + GRP):
                if 0 <= i - LOOK < len(units):
                    emit_PV(i - LOOK)
        if STOP_AFTER == 'attn':
            for t in range(NT):
                S.dma('sp', out=dr["x2"][t * 128:(t + 1) * 128, :], in_=y_all[:, t, :], reads=[('y', t, h_) for h_ in range(32)], writes=[('x2o', t)], sname='out')
            finish()
            return nc
        S.barrier()

        goT = sk16
        A.off = Y_END
        junk2 = A.alloc((1024,), BF16)
        goT = A.alloc((16,), F32)
        A.seek_bytes(TOPB - 16 * 2048 * 2)
        assert A.off >= Y_END + 1024
        yT = A.alloc((16, 2048), BF16)
        S.dma('sp', out=st16[0:8, :], in_=dr["g_out_a"].rearrange("o (a p) -> (o a) p", p=128), writes=['st16'], sname='st16')
        S.dma('sp', out=st16[8:16, :], in_=dr["g_out_b"].rearrange("o (a p) -> (o a) p", p=128), writes=['st16'], sname='st16')
        S.op('pe', 'transpose', reads=['st16', 'ident'], writes=[('ps', 7)], out=ps[7][:, 0:16], in_=st16[0:16, :],
             identity=ident[0:16, 0:16])
        S.op('dve', 'tensor_copy', reads=[('ps', 7)], writes=['goT'], out=goT, in_=ps[7][:, 0:16])
        for t in range(NT):
            for g in range(2):
                col = t * 2 + g
                yv = y_all[:, t, g * 1024:(g + 1) * 1024]
                S.op('act', 'activation', reads=[('y', t, h_) for h_ in range(16 * g, 16 * g + 16)], writes=['junk2', 'ssq2'], out=junk2, in_=yv, func=AF.Square,
                     accum_out=ssq2[:, col:col + 1])
        rstd_ops(ssq2, rstd2, 'ssq2', 'rstd2', 1.0 / 1024)
        for t in range(NT):
            for g in range(2):
                col = t * 2 + g
                yv = y_all[:, t, g * 1024:(g + 1) * 1024]
                S.op('dve', 'tensor_scalar', reads=[('y', t, h_) for h_ in range(16 * g, 16 * g + 16)] + ['rstd2'],
                     writes=[('y', t, h_) for h_ in range(16 * g, 16 * g + 16)], out=yv, in0=yv,
                     scalar1=rstd2[:, col:col + 1], scalar2=None, op0=ALU.mult)
            for g4 in range(4):
                pb = 6 + g4 % 2
                for i in range(4):
                    kc = g4 * 4 + i
                    S.op('pe', 'transpose', reads=[('y', t, 2 * kc), ('y', t, 2 * kc + 1), 'ident'], writes=[('ps', pb)], inc=(i == 3),
                         out=ps[pb][:, i * 128:(i + 1) * 128], in_=y_all[:, t, kc * 128:(kc + 1) * 128], identity=ident)
                for i in range(4):
                    kc = g4 * 4 + i
                    o = yT[:, kc, t * 128:(t + 1) * 128]
                    src = ps[pb][:, i * 128:(i + 1) * 128]
                    if g4 % 2 == 0:
                        S.op('act', 'activation', reads=[('ps', pb), 'goT'], writes=[('yT', t, kc)], out=o, in_=src, func=AF.Identity,
                             scale=goT[:, kc:kc + 1])
                    else:
                        S.op('dve', 'tensor_scalar', reads=[('ps', pb), 'goT'], writes=[('yT', t, kc)], out=o, in0=src,
                             scalar1=goT[:, kc:kc + 1], scalar2=None, op0=ALU.mult)
        S.barrier()

        A.off = m0
        wos = [A.alloc((16, 512), BF16) for _ in range(2)]
        g1b = A.alloc((2048,), F32)
        bg = A.alloc((2048,), F32)
        xq = [A.alloc((512,), F32) for _ in range(6)]
        tq = [A.alloc((512,), F32) for _ in range(2)]
        assert A.off * 4 <= TOPB - 16 * 2048 * 2
        wov = dr["w_out"].rearrange("(kc p) n -> p kc n", p=128)
        S.dma('sp', out=bg, in_=dr["b_out"].broadcast_to([128, D]), writes=['bg'], sname='bc')
        S.dma('sp', out=g1b, in_=dr["gbc"][1], writes=['g1b'], sname='bc')
        S.op('dve', 'tensor_tensor', reads=['bg', 'g1b'], writes=['bg'], out=bg, in0=bg, in1=g1b, op=ALU.mult)
        S.dma('pool', out=wos[0], in_=wov[:, :, 0:512], writes=[('wos', 0)], sname='wos0')
        NXQ = len(xq)
        PRE = 3
        ounits = [(s_, t_) for s_ in range(4) for t_ in range(NT)]

        def qload(k):
            s_, t_ = ounits[k]
            S.dma('sp', out=xq[k % NXQ], in_=dr["x1"][t_ * 128:(t_ + 1) * 128, s_ * 512:(s_ + 1) * 512], writes=[('xq', k % NXQ)],
                  sname=f'xld{k % NXQ}')
        for k in range(PRE):
            qload(k)
        for k, (s, t) in enumerate(ounits):
            if k + PRE < len(ounits):
                qload(k + PRE)
            if t == 0 and s + 1 < 4:
                S.dma('pool', out=wos[(s + 1) % 2], in_=wov[:, :, (s + 1) * 512:(s + 2) * 512], writes=[('wos', (s + 1) % 2)],
                      sname=f'wos{(s + 1) % 2}')
            ob = 4 + k % 2
            xs = k % NXQ
            ts_ = k % 2
            for kc in range(16):
                S.op('pe', 'matmul', reads=[('yT', t, kc), ('wos', s % 2)], writes=[('ps', ob)], inc=(kc == 15), out=PS(ob),
                     lhsT=yT[:, kc, t * 128:(t + 1) * 128], rhs=wos[s % 2][:, kc, :], start=(kc == 0), stop=(kc == 15))
            S.op('dve', 'tensor_tensor', reads=[('ps', ob), 'g1b'], writes=[('tq', ts_)], out=tq[ts_], in0=PS(ob),
                 in1=g1b[:, s * 512:(s + 1) * 512], op=ALU.mult)
            S.op('dve', 'tensor_tensor', reads=[('tq', ts_), 'bg'], writes=[('tq', ts_)], out=tq[ts_], in0=tq[ts_],
                 in1=bg[:, s * 512:(s + 1) * 512], op=ALU.add)
            S.op('dve', 'tensor_tensor', reads=[('tq', ts_), ('xq', xs)], writes=[('xq', xs)], out=xq[xs], in0=tq[ts_], in1=xq[xs],
                 op=ALU.add)
            S.dma('sp', out=dr["x2"][t * 128:(t + 1) * 128, s * 512:(s + 1) * 512], in_=xq[xs], reads=[('xq', xs)],
                  writes=[('x2', t, s)], sname=f'xst{xs}')
        if STOP_AFTER == 'mix':
            finish()
            return nc
        S.barrier()

        ffn(dr["x2"], dr["x3"], 'x3', dr["w_ffn2_in"], dr["w_ffn2_out"], 2)

        A.off = m0
        gfb = A.alloc((2048,), F32)
        xf = [A.alloc((2048,), F32) for _ in range(6)]
        junk3 = A.alloc((2048,), BF16)
        S.dma('sp', out=gfb, in_=dr["g_final"].broadcast_to([128, D]), writes=['gfb'], sname='bc')
        def fA1(t):
            b = t % 6
            xk = ('xf', b)
            S.dma('sp', out=xf[b], in_=dr["x3"][t * 128:(t + 1) * 128, :], writes=[xk], sname=f'xld{b}')
            S.op('act', 'activation', reads=[xk], writes=['junk3', ('ssq', t)], out=junk3, in_=xf[b], func=AF.Square,
                 accum_out=ssq[:, t:t + 1])
            S.op('dve', 'tensor_scalar', reads=[('ssq', t)], writes=[('rstd', t)], out=rstd[:, t:t + 1], in0=ssq[:, t:t + 1],
                 scalar1=1.0 / D, scalar2=EPS, op0=ALU.mult, op1=ALU.add)

        def fA2(t):
            S.op('act', 'activation', reads=[('rstd', t)], writes=[('rstd', t)], out=rstd[:, t:t + 1], in_=rstd[:, t:t + 1], func=AF.Sqrt)
            S.op('dve', 'reciprocal', reads=[('rstd', t)], writes=[('rstd', t)], out=rstd[:, t:t + 1], in_=rstd[:, t:t + 1])

        def fB(t):
            b = t % 6
            xk = ('xf', b)
            S.op('dve', 'scalar_tensor_tensor', reads=[xk, ('rstd', t), 'gfb'], writes=[xk], out=xf[b], in0=xf[b],
                 scalar=rstd[:, t:t + 1], in1=gfb, op0=ALU.mult, op1=ALU.mult)
            S.dma('pool', out=dr["y"][t * 128:(t + 1) * 128, :], in_=xf[b], reads=[xk], writes=[('yout', t)], sname=f'pst{b}')
        for i in range(NT + 2):
            if i < NT:
                fA1(i)
            if 0 <= i - 1 < NT:
                fA2(i - 1)
            if 0 <= i - 2 < NT:
                fB(i - 2)
        finish()
    return nc


_W_NAMES = ["w_ada", "b_ada", "g_ffn1", "w_ffn1_in", "w_ffn1_out", "g_mix", "w_in", "b_in", "sinks", "g_out_a", "g_out_b",
            "w_out", "b_out", "g_ffn2", "w_ffn2_in", "w_ffn2_out"]


def make_in_maps(inputs):
    shared = {}
    for n in _W_NAMES:
        a = np.ascontiguousarray(np.asarray(inputs[n], dtype=np.float32))
        shared[n] = a.reshape(a.shape[-2], a.shape[-1]) if a.ndim == 3 else a.reshape(1, -1)
    shared["g_final"] = np.ascontiguousarray(np.asarray(inputs["g_final"], dtype=np.float32)).reshape(1, -1)
    x = np.asarray(inputs["x"], dtype=np.float32)
    c = np.asarray(inputs["c"], dtype=np.float32)
    pos = np.asarray(inputs["positions"], dtype=np.int32)
    maps = []
    for b in range(8):
        m = dict(shared)
        m["x"] = np.ascontiguousarray(x[b])
        m["c"] = np.ascontiguousarray(c[b:b + 1])
        m["positions"] = np.ascontiguousarray(pos[b:b + 1])
        maps.append(m)
    return maps


def kernel(**inputs):
    nc = build_program()
    in_maps = make_in_maps(inputs)
    res = run_bass_kernel_spmd(nc, in_maps, core_ids=list(range(8)))
    return np.stack([np.asarray(r["y"], dtype=np.float32) for r in res.results], axis=0)
```
